# Optimizing a Trainium2 kernel written in Bass

```python
import math
import jax
import jax.numpy as jnp
from jax import lax
import numpy as np

D_MODEL = 1024
BATCH = 8
SEQ = 2048
DEPTH = 4
DEC_BATCH = 128
DEC_SEQ = 4
PAST_LEN = 16384
PAGE_SIZE = 128

BR_WIDTH = D_MODEL
N_BRANCH = 4
SSD_HEAD_DIM = 64
SSD_HEADS = BR_WIDTH // SSD_HEAD_DIM
SSD_GROUPS = 2
SSD_STATE = 128
SSD_CONV = 4
SSD_CONV_DIM = BR_WIDTH + 2 * SSD_GROUPS * SSD_STATE
S5_GROUP = 16
S5_GROUPS = BR_WIDTH // S5_GROUP
S5_STATE = 64
ML_HEADS = 4
ML_HEAD_DIM = BR_WIDTH // ML_HEADS
MEM_TOKENS = 256
XA_HEADS = 4
XA_HEAD_DIM = BR_WIDTH // XA_HEADS
CHUNK = 64
EPS = 1e-6
IN_SIZES = (BR_WIDTH, SSD_CONV_DIM, SSD_HEADS, BR_WIDTH, BR_WIDTH, BR_WIDTH, BR_WIDTH, BR_WIDTH,
            ML_HEADS, ML_HEADS, BR_WIDTH, BR_WIDTH, BR_WIDTH, BR_WIDTH, N_BRANCH * D_MODEL)
D_IN = sum(IN_SIZES)

kernel_name = 'hybrid_ssd_s5_mlstm_memxattn_step'


def _rmsnorm(x, g):
    xf = x.astype(jnp.float32)
    y = xf * lax.rsqrt(jnp.mean(xf * xf, axis=-1, keepdims=True) + EPS)
    return (y * g.astype(jnp.float32)).astype(x.dtype)


def _chunk_len(t):
    return CHUNK if t % CHUNK == 0 else t


def _to_chunks(a, L):
    b, t = a.shape[:2]
    return jnp.moveaxis(a.reshape((b, t // L, L) + a.shape[2:]), 1, 0)


def _from_chunks(a):
    a = jnp.moveaxis(a, 0, 1)
    return a.reshape((a.shape[0], a.shape[1] * a.shape[2]) + a.shape[3:])


def _causal_dwconv(x, buf, w, b):
    xp = jnp.concatenate([buf.astype(jnp.float32), x], axis=1)
    y = lax.conv_general_dilated(xp, w.astype(jnp.float32)[:, None, :], window_strides=(1,), padding='VALID',
                                 dimension_numbers=('NWC', 'WIO', 'NWC'), feature_group_count=x.shape[-1])
    return y + b, xp[:, xp.shape[1] - (SSD_CONV - 1):]


def _ssd(xbc_raw, dt_raw, conv_buf, state0, conv_w, conv_b, dt_bias, a_log, d_skip):
    bsz, t = xbc_raw.shape[:2]
    hg = SSD_HEADS // SSD_GROUPS
    xbc, new_buf = _causal_dwconv(xbc_raw, conv_buf, conv_w, conv_b)
    xbc = jax.nn.silu(xbc)
    xs, bm, cm = jnp.split(xbc, [BR_WIDTH, BR_WIDTH + SSD_GROUPS * SSD_STATE], axis=-1)
    xs = xs.reshape(bsz, t, SSD_GROUPS, hg, SSD_HEAD_DIM)
    bm = bm.reshape(bsz, t, SSD_GROUPS, SSD_STATE)
    cm = cm.reshape(bsz, t, SSD_GROUPS, SSD_STATE)
    dt = jax.nn.softplus(dt_raw + dt_bias).reshape(bsz, t, SSD_GROUPS, hg)
    la = dt * (-jnp.exp(a_log.astype(jnp.float32))).reshape(SSD_GROUPS, hg)
    L = _chunk_len(t)
    causal = jnp.tril(jnp.ones((L, L), dtype=jnp.bool_))

    def body(s, inp):
        xc, bc, cc, dtc, lac = inp
        acs = jnp.cumsum(lac, axis=1)
        seg = acs[:, :, None] - acs[:, None, :]
        decay = jnp.exp(jnp.where(causal[None, :, :, None, None], seg, -jnp.inf))
        cb = jnp.einsum('blgn,bsgn->blsg', cc, bc)
        xdt = xc * dtc[..., None]
        y_intra = jnp.einsum('blsg,blsgh,bsghp->blghp', cb, decay, xdt)
        y_inter = jnp.einsum('blgn,bghpn,blgh->blghp', cc, s, jnp.exp(acs))
        tail = jnp.exp(acs[:, -1:] - acs)
        s_new = jnp.exp(acs[:, -1])[..., None, None] * s + jnp.einsum('blgn,blgh,blghp->bghpn', bc, tail, xdt)
        return s_new, y_intra + y_inter

    s0 = state0.astype(jnp.float32).reshape(bsz, SSD_GROUPS, hg, SSD_HEAD_DIM, SSD_STATE)
    s_fin, y = lax.scan(body, s0, tuple(_to_chunks(a, L) for a in (xs, bm, cm, dt, la)))
    y = _from_chunks(y) + d_skip.reshape(SSD_GROUPS, hg)[..., None] * xs
    return (y.reshape(bsz, t, BR_WIDTH), new_buf,
            s_fin.reshape(bsz, SSD_HEADS, SSD_HEAD_DIM, SSD_STATE))


def _s5(u, state0_re, state0_im, a_re, a_im, log_dt, b_re, b_im, c_re, c_im, d_skip):
    f32 = jnp.float32
    bsz, t = u.shape[:2]
    lam = lax.complex(a_re.astype(f32), a_im.astype(f32))
    dt = jnp.exp(log_dt.astype(f32))[:, None]
    lam_bar = jnp.exp(lam * dt)
    b_bar = ((lam_bar - 1.0) / lam)[..., None] * lax.complex(b_re.astype(f32), b_im.astype(f32))
    ug = u.reshape(bsz, t, S5_GROUPS, S5_GROUP)
    bu = jnp.einsum('gnc,btgc->btgn', b_bar, ug.astype(jnp.complex64))
    h0 = lax.complex(state0_re.astype(f32), state0_im.astype(f32))
    bu = bu.at[:, 0].add(lam_bar * h0)
    a_el = jnp.broadcast_to(lam_bar, (t,) + lam_bar.shape)

    def combine(e1, e2):
        a1, b1 = e1
        a2, b2 = e2
        return a1 * a2, a2 * b1 + b2

    h = jax.vmap(lambda bseq: lax.associative_scan(combine, (a_el, bseq))[1])(bu)
    c = lax.complex(c_re.astype(f32), c_im.astype(f32))
    y = jnp.einsum('gcn,btgn->btgc', c, h).real.reshape(bsz, t, BR_WIDTH) + d_skip * u
    return y, h[:, -1].real, h[:, -1].imag


def _mlstm(q, k, v, i_raw, f_raw, c0, n0, m0):
    t = q.shape[1]
    L = _chunk_len(t)
    k = k * (ML_HEAD_DIM ** -0.5)
    logf = jax.nn.log_sigmoid(f_raw)
    causal = jnp.tril(jnp.ones((L, L), dtype=jnp.bool_))

    def body(carry, inp):
        c, n, m = carry
        qc, kc, vc, ic, lfc = inp
        bcum = jnp.cumsum(lfc, axis=1)
        dmat = bcum[:, :, None] - bcum[:, None, :] + ic[:, None, :]
        dmat = jnp.where(causal[None, :, :, None], dmat, -jnp.inf)
        g = bcum + m[:, None]
        m_l = jnp.maximum(g, jnp.max(dmat, axis=2))
        w = jnp.exp(dmat - m_l[:, :, None])
        w_inter = jnp.exp(g - m_l)
        qk = jnp.einsum('blhd,bshd->blsh', qc, kc) * w
        num = jnp.einsum('blsh,bshd->blhd', qk, vc) + w_inter[..., None] * jnp.einsum('blhd,bhde->blhe', qc, c)
        den = jnp.sum(qk, axis=2) + w_inter * jnp.einsum('blhd,bhd->blh', qc, n)
        h = num / jnp.maximum(jnp.abs(den), jnp.exp(-m_l))[..., None]
        g_end = bcum[:, -1] + m
        d_end = bcum[:, -1:] - bcum + ic
        m_new = jnp.maximum(g_end, jnp.max(d_end, axis=1))
        w_end = jnp.exp(d_end - m_new[:, None])
        decay = jnp.exp(g_end - m_new)
        c_new = decay[..., None, None] * c + jnp.einsum('blh,blhd,blhe->bhde', w_end, kc, vc)
        n_new = decay[..., None] * n + jnp.einsum('blh,blhd->bhd', w_end, kc)
        return (c_new, n_new, m_new), h

    init = (c0.astype(jnp.float32), n0.astype(jnp.float32), m0.astype(jnp.float32))
    (c_f, n_f, m_f), h = lax.scan(body, init, tuple(_to_chunks(a, L) for a in (q, k, v, i_raw, logf)))
    return _from_chunks(h), c_f, n_f, m_f


def _mem_kv(mem, g, w_kv):
    b, m = mem.shape[:2]
    kv = jnp.matmul(_rmsnorm(mem, g), w_kv)
    mk, mv = jnp.split(kv, 2, axis=-1)
    return mk.reshape(b, m, XA_HEADS, XA_HEAD_DIM), mv.reshape(b, m, XA_HEADS, XA_HEAD_DIM)


def _mem_attn(q, mk, mv):
    s = jnp.einsum('bthd,bmhd->bhtm', q, mk).astype(jnp.float32) * (XA_HEAD_DIM ** -0.5)
    p = jax.nn.softmax(s, axis=-1)
    return jnp.einsum('bhtm,bmhd->bthd', p, mv.astype(jnp.float32))


def _layer(x, mem_k, mem_v, conv_buf, ssd_s, s5_re, s5_im, ml_c, ml_n, ml_m, lw):
    (norm_in, w_in, b_gate, b_igate, b_fgate, ssd_conv_w, ssd_conv_b, ssd_dt_bias, ssd_a_log, ssd_d,
     ssd_norm, s5_a_re, s5_a_im, s5_log_dt, s5_b_re, s5_b_im, s5_c_re, s5_c_im, s5_d, s5_glu_w, s5_glu_b,
     ml_norm, w_down, w_out) = lw
    f32 = jnp.float32
    bsz, t = x.shape[:2]
    h = _rmsnorm(x, norm_in)
    proj = jnp.matmul(h, w_in).astype(f32)
    offs = []
    acc = 0
    for size in IN_SIZES[:-1]:
        acc += size
        offs.append(acc)
    (z_ssd, xbc, dt_raw, u_s5, z_s5, q_ml, k_ml, v_ml, i_raw, f_raw, o_ml, z_ml, q_xa, z_xa,
     gate_raw) = jnp.split(proj, offs, axis=-1)

    y_a, conv_new, ssd_new = _ssd(xbc, dt_raw, conv_buf, ssd_s, ssd_conv_w, ssd_conv_b, ssd_dt_bias, ssd_a_log, ssd_d)
    y_a = _rmsnorm(y_a * jax.nn.silu(z_ssd), ssd_norm)
    y_b, s5_re_new, s5_im_new = _s5(u_s5, s5_re, s5_im, s5_a_re, s5_a_im, s5_log_dt, s5_b_re, s5_b_im,
                                    s5_c_re, s5_c_im, s5_d)
    y_b = jax.nn.gelu(y_b)
    y_b = y_b * jax.nn.sigmoid(jnp.matmul(y_b, s5_glu_w.astype(f32)) + s5_glu_b) * jax.nn.silu(z_s5)
    heads = lambda a: a.reshape(bsz, t, ML_HEADS, ML_HEAD_DIM)
    y_c, c_new, n_new, m_new = _mlstm(heads(q_ml), heads(k_ml), heads(v_ml), i_raw + b_igate, f_raw + b_fgate,
                                      ml_c, ml_n, ml_m)
    y_c = _rmsnorm(jax.nn.sigmoid(heads(o_ml)) * y_c, ml_norm).reshape(bsz, t, BR_WIDTH) * jax.nn.silu(z_ml)
    y_d = _mem_attn(q_xa.reshape(bsz, t, XA_HEADS, XA_HEAD_DIM), mem_k, mem_v).reshape(bsz, t, BR_WIDTH)
    y_d = y_d * jax.nn.silu(z_xa)

    branches = jnp.stack([y_a, y_b, y_c, y_d], axis=2).astype(x.dtype)
    down = jnp.einsum('btkw,kwd->btkd', branches, w_down)
    gates = jax.nn.sigmoid(gate_raw + b_gate).reshape(bsz, t, N_BRANCH, D_MODEL)
    merged = jnp.sum(gates * down, axis=2).astype(x.dtype)
    x = x + jnp.matmul(merged, w_out).astype(x.dtype)
    dt = x.dtype
    return (x, conv_new.astype(dt), ssd_new.astype(dt), s5_re_new.astype(dt), s5_im_new.astype(dt),
            c_new.astype(dt), n_new.astype(dt), m_new.astype(dt))


def setup_inputs(seed: int = 0) -> dict:
    key = jax.random.key(seed)
    ks = iter(jax.random.split(key, 64))
    f32 = jnp.float32

    def nrm(shape, scale):
        return scale * jax.random.normal(next(ks), shape, f32)

    def unif(shape, lo, hi):
        return jax.random.uniform(next(ks), shape, f32, lo, hi)

    dt_ssd = jnp.exp(unif((DEPTH, SSD_HEADS), math.log(1e-3), math.log(1e-1)))
    n_idx = jnp.arange(S5_STATE, dtype=f32)
    return {
        'x_prompt': nrm((BATCH, SEQ, D_MODEL), 1.0),
        'x_sample': nrm((DEC_BATCH, DEC_SEQ, D_MODEL), 1.0),
        'mem_prompt': nrm((BATCH, MEM_TOKENS, D_MODEL), 1.0),
        'cache_mem_k': nrm((DEPTH, DEC_BATCH, MEM_TOKENS, XA_HEADS, XA_HEAD_DIM), 1.0),
        'cache_mem_v': nrm((DEPTH, DEC_BATCH, MEM_TOKENS, XA_HEADS, XA_HEAD_DIM), 1.0),
        'state_ssd_conv': nrm((DEPTH, DEC_BATCH, SSD_CONV - 1, SSD_CONV_DIM), 1.0),
        'state_ssd': nrm((DEPTH, DEC_BATCH, SSD_HEADS, SSD_HEAD_DIM, SSD_STATE), 0.1),
        'state_s5_re': nrm((DEPTH, DEC_BATCH, S5_GROUPS, S5_STATE), 0.5),
        'state_s5_im': nrm((DEPTH, DEC_BATCH, S5_GROUPS, S5_STATE), 0.5),
        'state_mlstm_c': nrm((DEPTH, DEC_BATCH, ML_HEADS, ML_HEAD_DIM, ML_HEAD_DIM), 0.05),
        'state_mlstm_n': nrm((DEPTH, DEC_BATCH, ML_HEADS, ML_HEAD_DIM), 0.5),
        'state_mlstm_m': nrm((DEPTH, DEC_BATCH, ML_HEADS), 0.5),
        'norm_in': 1.0 + nrm((DEPTH, D_MODEL), 0.02),
        'w_in': nrm((DEPTH, D_MODEL, D_IN), D_MODEL ** -0.5),
        'b_gate': nrm((DEPTH, N_BRANCH * D_MODEL), 0.02),
        'b_igate': nrm((DEPTH, ML_HEADS), 0.1),
        'b_fgate': 3.0 + unif((DEPTH, ML_HEADS), 0.0, 3.0),
        'ssd_conv_w': nrm((DEPTH, SSD_CONV, SSD_CONV_DIM), SSD_CONV ** -0.5),
        'ssd_conv_b': nrm((DEPTH, SSD_CONV_DIM), 0.02),
        'ssd_dt_bias': dt_ssd + jnp.log(-jnp.expm1(-dt_ssd)),
        'ssd_a_log': jnp.log(unif((DEPTH, SSD_HEADS), 1.0, 16.0)),
        'ssd_d': 1.0 + nrm((DEPTH, SSD_HEADS), 0.1),
        'ssd_norm': 1.0 + nrm((DEPTH, BR_WIDTH), 0.02),
        's5_a_re': -0.5 + nrm((DEPTH, S5_GROUPS, S5_STATE), 0.01),
        's5_a_im': math.pi * n_idx + nrm((DEPTH, S5_GROUPS, S5_STATE), 0.01),
        's5_log_dt': unif((DEPTH, S5_GROUPS), math.log(1e-3), math.log(1e-1)),
        's5_b_re': nrm((DEPTH, S5_GROUPS, S5_STATE, S5_GROUP), (2 * S5_GROUP) ** -0.5),
        's5_b_im': nrm((DEPTH, S5_GROUPS, S5_STATE, S5_GROUP), (2 * S5_GROUP) ** -0.5),
        's5_c_re': nrm((DEPTH, S5_GROUPS, S5_GROUP, S5_STATE), (2 * S5_STATE) ** -0.5),
        's5_c_im': nrm((DEPTH, S5_GROUPS, S5_GROUP, S5_STATE), (2 * S5_STATE) ** -0.5),
        's5_d': nrm((DEPTH, BR_WIDTH), 0.5),
        's5_glu_w': nrm((DEPTH, BR_WIDTH, BR_WIDTH), BR_WIDTH ** -0.5),
        's5_glu_b': nrm((DEPTH, BR_WIDTH), 0.02),
        'ml_norm': 1.0 + nrm((DEPTH, ML_HEADS, ML_HEAD_DIM), 0.02),
        'mem_norm': 1.0 + nrm((DEPTH, D_MODEL), 0.02),
        'w_mem_kv': nrm((DEPTH, D_MODEL, 2 * BR_WIDTH), D_MODEL ** -0.5),
        'w_down': nrm((DEPTH, N_BRANCH, BR_WIDTH, D_MODEL), BR_WIDTH ** -0.5),
        'w_out': nrm((DEPTH, D_MODEL, D_MODEL), 0.5 * D_MODEL ** -0.5),
        'final_norm': 1.0 + nrm((D_MODEL,), 0.02),
    }


def reference(x_prompt, x_sample, mem_prompt, cache_mem_k, cache_mem_v, state_ssd_conv, state_ssd,
              state_s5_re, state_s5_im, state_mlstm_c, state_mlstm_n, state_mlstm_m,
              norm_in, w_in, b_gate, b_igate, b_fgate, ssd_conv_w, ssd_conv_b, ssd_dt_bias, ssd_a_log, ssd_d,
              ssd_norm, s5_a_re, s5_a_im, s5_log_dt, s5_b_re, s5_b_im, s5_c_re, s5_c_im, s5_d, s5_glu_w,
              s5_glu_b, ml_norm, mem_norm, w_mem_kv, w_down, w_out, final_norm):
    f32 = jnp.float32

    def zeros(*shape):
        return jnp.zeros((BATCH,) + shape, f32)

    yp, ys = x_prompt, x_sample
    outs_p = [[] for _ in range(9)]
    outs_s = [[] for _ in range(7)]
    for l in range(DEPTH):
        lw = (norm_in[l], w_in[l], b_gate[l], b_igate[l], b_fgate[l], ssd_conv_w[l], ssd_conv_b[l],
              ssd_dt_bias[l], ssd_a_log[l], ssd_d[l], ssd_norm[l], s5_a_re[l], s5_a_im[l], s5_log_dt[l],
              s5_b_re[l], s5_b_im[l], s5_c_re[l], s5_c_im[l], s5_d[l], s5_glu_w[l], s5_glu_b[l],
              ml_norm[l], w_down[l], w_out[l])
        mk, mv = _mem_kv(mem_prompt, mem_norm[l], w_mem_kv[l])
        yp, *st_p = _layer(yp, mk, mv, zeros(SSD_CONV - 1, SSD_CONV_DIM),
                           zeros(SSD_HEADS, SSD_HEAD_DIM, SSD_STATE), zeros(S5_GROUPS, S5_STATE),
                           zeros(S5_GROUPS, S5_STATE), zeros(ML_HEADS, ML_HEAD_DIM, ML_HEAD_DIM),
                           zeros(ML_HEADS, ML_HEAD_DIM), zeros(ML_HEADS), lw)
        ys, *st_s = _layer(ys, cache_mem_k[l], cache_mem_v[l], state_ssd_conv[l], state_ssd[l], state_s5_re[l],
                           state_s5_im[l], state_mlstm_c[l], state_mlstm_n[l], state_mlstm_m[l], lw)
        for lst, a in zip(outs_p, [mk, mv] + st_p):
            lst.append(a)
        for lst, a in zip(outs_s, st_s):
            lst.append(a)
    (mk_p, mv_p, conv_p, ssd_p, s5re_p, s5im_p, mc_p, mn_p, mm_p) = [jnp.stack(a) for a in outs_p]
    (conv_s, ssd_s, s5re_s, s5im_s, mc_s, mn_s, mm_s) = [jnp.stack(a) for a in outs_s]
    y_prompt = _rmsnorm(yp, final_norm)
    y_sample = _rmsnorm(ys, final_norm)
    return (y_prompt, y_sample, mk_p, mv_p, conv_p, ssd_p, s5re_p, s5im_p, mc_p, mn_p, mm_p,
            conv_s, ssd_s, s5re_s, s5im_s, mc_s, mn_s, mm_s)
```

```python
import contextlib
import numpy as np
import concourse.bass as bass
import concourse.mybir as mybir
from concourse.bass_utils import run_bass_kernel_spmd

F32 = mybir.dt.float32
BF16 = mybir.dt.bfloat16
AF = mybir.ActivationFunctionType
ALU = mybir.AluOpType
AX = mybir.AxisListType

import os
SKIP = os.environ.get('K_SKIP', '').split(',')
NCORES = 8
D = 1024
DEPTH = 4
TP = 2048
TS = 64
NT = TP + TS
NB = 16
EPS = 1e-6
D_IN = 15896
O_ZSSD, O_XBC, O_DT, O_U, O_ZS5, O_Q, O_K, O_V, O_I, O_F, O_O, O_ZML, O_QXA, O_ZXA, O_GATE = (
    0, 1024, 2560, 2576, 3600, 4624, 5648, 6672, 7696, 7700, 7704, 8728, 9752, 10776, 11800)
TILES = [(128 * i, 128) for i in range(16)] + [(2048, 64)]
STS = [(512 * i, 512) for i in range(4)] + [(2048, 64)]
R_XBC, R_U, R_Q, R_K, R_QXA, R_ZS5, R_GATE = 0, 1536, 2560, 3584, 4608, 5632, 6656
NFM = 6656 + 4096
C_ZSSD, C_V, C_KT, C_O, C_ZML, C_ZXA = 0, 1024, 2048, 3072, 4096, 5120
NTM = 6144


class S:
    def __init__(self, ap, sub):
        self.ap = ap
        self.sub = sub


def _unw(a):
    if isinstance(a, S):
        return a.ap, a.sub
    return a, None


class Ctx:
    def __init__(self, nc):
        self.nc = nc
        self.es = contextlib.ExitStack()
        self.eng = {'pe': nc.tensor, 'act': nc.scalar, 'dve': nc.vector, 'pool': nc.gpsimd, 'sp': nc.sync}
        self.sem = {}
        self.cnt = {}
        self.semobj = {}
        for e in ('pe', 'act', 'dve', 'pool'):
            self.sem[e] = self.es.enter_context(nc.semaphore("sem_" + e))
            self.cnt[e] = 0
            self.semobj["sem_" + e] = self.sem[e]
        self.know = {e: {} for e in self.eng}
        self.reg = {}
        self.vcs = {}
        self.dma_sems = {}
        self.n_wait = 0
        self.n_inst = 0

    def sb(self, name, shape, dt=F32):
        return self.es.enter_context(self.nc.sbuf_tensor(name, list(shape), dt))

    def ps(self, name, shape, dt=F32):
        return self.es.enter_context(self.nc.psum_tensor(name, list(shape), dt))

    def dsem(self, name):
        if name not in self.dma_sems:
            if getattr(self, "free_sems", None):
                s, base = self.free_sems.pop()
            else:
                s, base = self.es.enter_context(self.nc.semaphore("d_%d" % len(self.semobj))), 0
            self.dma_sems[name] = [s, base]
            self.semobj["d_" + name] = s
        return self.dma_sems[name]

    def release(self, names):
        if not hasattr(self, "free_sems"):
            self.free_sems = []
        for n in names:
            for nm in (n, n + "_sw"):
                if nm in self.dma_sems:
                    s, v = self.dma_sems.pop(nm)
                    self.free_sems.append((s, v))

    def _deps(self, name, sub, is_write, eng=None):
        st = self.reg.get(name)
        if not st:
            return []
        keys = list(st.keys()) if sub is None else [k for k in (sub, None) if k in st]
        toks = []
        psum = name.startswith("pb")
        for k in keys:
            w, r = st[k]
            if w is not None:
                toks.append(w)
            if is_write:
                toks.extend(r.items())
            elif psum:
                toks.extend((s, v) for (s, v) in r.items() if s != "sem_" + str(eng))
        return toks

    def _record(self, name, sub, is_write, tok):
        st = self.reg.setdefault(name, {})
        if is_write:
            if sub is None:
                st.clear()
            st[sub] = [tok, {}]
        else:
            ent = st.setdefault(sub, [None, {}])
            s, v = tok
            if ent[1].get(s, -1) < v:
                ent[1][s] = v

    def _sync(self, e, reads, writes):
        toks = []
        for (n, s) in reads:
            toks += self._deps(n, s, False, e)
        for (n, s) in writes:
            toks += self._deps(n, s, True)
        need = {}
        kn = self.know[e]
        for (sname, v) in toks:
            if e == 'pe' and sname == 'sem_pe':
                continue
            if kn.get(sname, 0) >= v:
                continue
            if need.get(sname, 0) < v:
                need[sname] = v
        for sname, v in need.items():
            if sname.startswith("d_"):
                v = max(v, self.dma_sems[sname[2:]][1])
            if kn.get(sname, 0) >= v:
                continue
            self.eng[e].wait_ge(self.semobj[sname], v)
            self.n_wait += 1
            kn[sname] = v
            vc = self.vcs.get((sname, v))
            if vc:
                for k2, v2 in vc.items():
                    if kn.get(k2, 0) < v2:
                        kn[k2] = v2

    def _keys(self, aps):
        out = []
        for a in aps:
            if a is None:
                continue
            ap, sub = _unw(a)
            if not hasattr(ap, 'tensor'):
                continue
            out.append((ap.tensor.name, sub))
        return out

    def op(self, e, fn, reads, writes):
        rk = self._keys(reads)
        wk = self._keys(writes)
        self._sync(e, rk, wk)
        inst = fn(self.eng[e])
        self.cnt[e] += 1
        inst.then_inc(self.sem[e], 1)
        self.n_inst += 1
        sname = "sem_" + e
        tok = (sname, self.cnt[e])
        vc = dict(self.know[e])
        vc[sname] = self.cnt[e]
        self.vcs[tok] = vc
        for (n, s) in rk:
            self._record(n, s, False, tok)
        for (n, s) in wk:
            self._record(n, s, True, tok)
        return tok

    def dma(self, q, out, in_, slot=None, **kw):
        o, _ = _unw(out)
        i, _ = _unw(in_)
        rk = self._keys([in_])
        wk = self._keys([out])
        self._sync(q, rk, wk)
        if slot is None:
            slot = o.tensor.name if 'dram' not in str(type(o.tensor)).lower() else i.tensor.name
        if q == 'pool':
            slot = slot + "_sw"
        sem = self.dsem(slot)
        inst = self.eng[q].dma_start(out=o, in_=i, **kw)
        sem[1] += 16
        inst.then_inc(sem[0], 16)
        self.n_inst += 1
        tok = ("d_" + slot, sem[1])
        self.vcs[tok] = dict(self.know[q])
        for (n, s) in rk:
            self._record(n, s, False, tok)
        for (n, s) in wk:
            self._record(n, s, True, tok)
        return tok

    def barrier(self):
        for e in ('pe', 'act', 'dve', 'pool', 'sp'):
            kn = self.know[e]
            for f in ('pe', 'act', 'dve', 'pool'):
                if f != e and self.cnt[f] > kn.get('sem_' + f, 0):
                    self.eng[e].wait_ge(self.sem[f], self.cnt[f])
                    kn['sem_' + f] = self.cnt[f]
                    self.n_wait += 1
            for name, (s, v) in self.dma_sems.items():
                if v > kn.get('d_' + name, 0):
                    self.eng[e].wait_ge(s, v)
                    kn['d_' + name] = v
                    self.n_wait += 1
        for e in ('act', 'dve', 'pool'):
            if self.cnt[e] > self.know[e].get('sem_' + e, 0):
                self.eng[e].wait_ge(self.sem[e], self.cnt[e])
                self.know[e]['sem_' + e] = self.cnt[e]
                self.n_wait += 1

    def finish(self):
        for e in ('pe', 'act', 'dve', 'pool'):
            if self.cnt[e]:
                self.eng['sp'].wait_ge(self.sem[e], self.cnt[e])
        for name, (s, v) in self.dma_sems.items():
            if v:
                self.eng['sp'].wait_ge(s, v)

    def mm(self, out, lhsT, rhs, start=True, stop=True, **kw):
        o, l, r = _unw(out)[0], _unw(lhsT)[0], _unw(rhs)[0]
        return self.op('pe', lambda e: e.matmul(o, lhsT=l, rhs=r, start=start, stop=stop, **kw), [lhsT, rhs], [out])

    def tr(self, out, in_, ident):
        o, i, d = _unw(out)[0], _unw(in_)[0], _unw(ident)[0]
        return self.op('pe', lambda e: e.transpose(o, i, d), [in_, ident], [out])

    def act(self, out, in_, func, bias=None, scale=None, accum_out=None):
        o, i = _unw(out)[0], _unw(in_)[0]
        kw = {}
        rd = [in_]
        if bias is not None:
            kw['bias'] = _unw(bias)[0]
            rd.append(bias)
        if scale is not None:
            kw['scale'] = _unw(scale)[0]
            rd.append(scale)
        wr = [out]
        if accum_out is not None:
            kw['accum_out'] = _unw(accum_out)[0]
            wr.append(accum_out)
        return self.op('act', lambda e: e.activation(out=o, in_=i, func=func, **kw), rd, wr)

    def tt(self, out, in0, in1, op, eng='dve'):
        o, a, b = _unw(out)[0], _unw(in0)[0], _unw(in1)[0]
        return self.op(eng, lambda e: e.tensor_tensor(out=o, in0=a, in1=b, op=op), [in0, in1], [out])

    def ts(self, out, in0, s1, s2=None, op0=ALU.mult, op1=None, eng='dve'):
        o, a = _unw(out)[0], _unw(in0)[0]
        rd = [in0]
        s1v = _unw(s1)[0]
        if hasattr(s1v, 'tensor'):
            rd.append(s1)
        s2v = _unw(s2)[0] if s2 is not None else None
        if s2v is not None and hasattr(s2v, 'tensor'):
            rd.append(s2)
        kw = {}
        if op1 is not None:
            kw['op1'] = op1
        return self.op(eng, lambda e: e.tensor_scalar(out=o, in0=a, scalar1=s1v, scalar2=s2v, op0=op0, **kw), rd, [out])

    def stt(self, out, in0, scalar, in1, op0, op1):
        o, a, b = _unw(out)[0], _unw(in0)[0], _unw(in1)[0]
        sv = _unw(scalar)[0]
        rd = [in0, in1]
        if hasattr(sv, 'tensor'):
            rd.append(scalar)
        return self.op('dve', lambda e: e.scalar_tensor_tensor(out=o, in0=a, scalar=sv, in1=b, op0=op0, op1=op1), rd, [out])

    def copy(self, out, in_, eng='dve'):
        o, i = _unw(out)[0], _unw(in_)[0]
        if eng == 'act':
            return self.op('act', lambda e: e.copy(out=o, in_=i), [in_], [out])
        return self.op(eng, lambda e: e.tensor_copy(out=o, in_=i), [in_], [out])

    def memset(self, out, val, eng='dve'):
        o = _unw(out)[0]
        return self.op(eng, lambda e: e.memset(o, val), [], [out])

    def scan(self, out, d0, d1, initial, op0, op1):
        o, a, b = _unw(out)[0], _unw(d0)[0], _unw(d1)[0]
        iv = _unw(initial)[0]
        rd = [d0, d1]
        if hasattr(iv, 'tensor'):
            rd.append(initial)
        return self.op('dve', lambda e: e.tensor_tensor_scan(out=o, data0=a, data1=b, initial=iv, op0=op0, op1=op1), rd, [out])

    def reduce(self, out, in_, op, axis=AX.X):
        o, i = _unw(out)[0], _unw(in_)[0]
        return self.op('dve', lambda e: e.tensor_reduce(out=o, in_=i, axis=axis, op=op), [in_], [out])

    def recip(self, out, in_):
        o, i = _unw(out)[0], _unw(in_)[0]
        return self.op('dve', lambda e: e.reciprocal(out=o, in_=i), [in_], [out])


class Rot:
    def __init__(self, bufs):
        self.bufs = bufs
        self.i = 0

    def next(self):
        b = self.bufs[self.i % len(self.bufs)]
        self.i += 1
        return b


def colmaj(v):
    v = np.asarray(v, np.float32)
    j = v.shape[-1] // 128
    return np.ascontiguousarray(np.swapaxes(v.reshape(v.shape[:-1] + (j, 128)), -1, -2))


def make_consts():
    c = {}
    c['ident'] = np.eye(128, dtype=np.float32)
    s = np.arange(128)[:, None]
    l = np.arange(128)[None, :]
    causal = (s <= l)
    blk = (s // 4 == l // 4)
    c['maskneg_p'] = np.where(causal, 0.0, -30000.0).astype(np.float32)
    c['maskneg_s'] = np.where(causal & blk, 0.0, -30000.0).astype(np.float32)
    c['maskbig_p'] = np.where(causal, 0.0, 30000.0).astype(np.float32)
    c['maskbig_s'] = np.where(causal & blk, 0.0, 30000.0).astype(np.float32)
    sel = np.zeros((16, 16, 128), np.float32)
    for h in range(16):
        sel[h, h, :] = 1.0
    c['sel16'] = sel.reshape(16, 16 * 128)
    c['ones'] = np.ones((128, 128), np.float32)
    cm = (np.arange(64)[None, :] // 4 == np.arange(16)[:, None]).astype(np.float32)
    c['colmask'] = np.broadcast_to(cm.reshape(1, 16 * 64), (128, 16 * 64)).copy()
    rm = np.zeros((128, 128), np.float32)
    rm[:64, :16] = cm.T
    c['rowmask'] = rm
    ps = np.zeros((128, 128), np.float32)
    for k in range(16):
        ps[k, :] = ((np.arange(128) // 64) == (k % 2))
    c['parsel'] = ps
    hm = np.zeros((128, 128), np.float32)
    for k in range(16):
        hm[k, k // 2] = 1.0
    c['hmask'] = hm
    bm = np.zeros((128, 128), np.float32)
    for p in range(128):
        bm[p, p // 16] = 1.0
    c['bmask'] = bm
    return c


CONST_ORDER = ['ident', 'maskneg_p', 'maskneg_s', 'maskbig_p', 'maskbig_s', 'ones', 'rowmask', 'parsel', 'hmask', 'bmask']


def build(depth=DEPTH, stage=99, dbg=False):
    nc = bass.Bass("TRN2", target_bir_lowering=False)
    c = Ctx(nc)

    def din(name, shape, dt=F32):
        return nc.dram_tensor(name, list(shape), dt, kind="ExternalInput").ap()

    def dout(name, shape, dt=F32):
        return nc.dram_tensor(name, list(shape), dt, kind="ExternalOutput").ap()

    def dscr(name, shape, dt):
        return nc.dram_tensor(name, list(shape), dt, kind="Internal").ap()

    IN_SHAPES = {
        "x_p": [TP, D], "x_s": [TS, D], "mem": [256, D],
        "ck": [DEPTH, NB, 256, D], "cv": [DEPTH, NB, 256, D],
        "st_conv": [DEPTH, NB * 3, 1536], "st_ssd": [DEPTH, NB, 1024, 128],
        "w_kv": [DEPTH, D, 2048], "w_down": [DEPTH, 4, D, D], "w_out": [DEPTH, D, D],
        "norm_in_c": [DEPTH, 128, 8], "mem_norm_c": [DEPTH, 128, 8], "final_norm": [1, D],
        "b_gate_c": [DEPTH, 128, 32], "c_sel16": [16, 16 * 128], "c_colmask": [128, 16 * 64],
        "conv_w_c": [DEPTH, 128, 48], "conv_b_c": [DEPTH, 128, 12], "ssd_hp": [DEPTH, 16, 2],
        "ssd_d": [DEPTH, 16], "ssd_norm": [DEPTH, D],
        "ml_hp": [DEPTH, 4, 2], "ml_norm": [DEPTH, D], "st_mlc": [DEPTH, NB, 1024, 256], "st_mln": [DEPTH, NB, 1024],
        "st_mlm": [DEPTH, 4, NB],
        "s5_a_re_t": [DEPTH, 64, 64], "s5_a_im_t": [DEPTH, 64, 64], "s5_log_dt": [DEPTH, 64],
        "s5_b_re_t": [DEPTH, 64, 1024], "s5_b_im_t": [DEPTH, 64, 1024], "s5_c_re_t": [DEPTH, 64, 1024], "s5_c_im_t": [DEPTH, 64, 1024],
        "s5_d_c": [DEPTH, 128, 8], "st_s5_t": [DEPTH, 64, 2, 64, NB], "s5_glu_w": [DEPTH, D, D], "s5_glu_b_c": [DEPTH, 128, 8],
    }
    for i in range(DEPTH):
        IN_SHAPES["w_in%d" % i] = [D, D_IN]
    for k in CONST_ORDER:
        IN_SHAPES["c_" + k] = [128, 128]
    _ins = {}

    def IN(name):
        if name not in _ins:
            _ins[name] = din(name, IN_SHAPES[name])
        return _ins[name]

    nc._used_inputs = _ins

    y_p = dout("y_p", [TP, D])
    y_s = dout("y_s", [TS, D])
    mk_o = dout("mk_o", [DEPTH, 256, D])
    mv_o = dout("mv_o", [DEPTH, 256, D])
    conv_po = dout("conv_po", [DEPTH, 3, 1536])
    conv_so = dout("conv_so", [DEPTH, NB, 3, 1536])
    ssd_po = dout("ssd_po", [DEPTH, 1024, 128]) if stage >= 2 else None
    ssd_so = dout("ssd_so", [DEPTH, NB, 1024, 128]) if stage >= 2 else None
    if stage >= 3:
        s5re_po = dout("s5re_po", [DEPTH, 64, 64])
        s5im_po = dout("s5im_po", [DEPTH, 64, 64])
        s5re_so = dout("s5re_so", [DEPTH, NB, 64, 64])
        s5im_so = dout("s5im_so", [DEPTH, NB, 64, 64])
    if stage >= 4:
        mc_po = dout("mc_po", [DEPTH, 1024, 256])
        mn_po = dout("mn_po", [DEPTH, 1024])
        mm_po = dout("mm_po", [DEPTH, 4])
        mc_so = dout("mc_so", [DEPTH, NB, 1024, 256])
        mn_so = dout("mn_so", [DEPTH, NB, 1024])
        mm_so = dout("mm_so", [DEPTH, 4, NB])
    dbg_o = {}

    def DBG(name):
        if name not in dbg_o:
            dbg_o[name] = dout("dbg_" + name, [1024, NT])
        return dbg_o[name]

    sfm = dscr("sfm", [NFM, NT], BF16)
    stm = dscr("stm", [NT, NTM], BF16)
    x_scr = dscr("x_scr", [NT, D], F32)
    g_scr = dscr("g_scr", [24, NT], F32)

    with c.es:
        h_fm = c.sb("h_fm", [128, 8, NT], BF16)
        yk_fm = c.sb("yk_fm", [128, 8, NT], BF16)
        ident = c.sb("ident", [128, 128], F32)
        ident_b = c.sb("ident_b", [128, 128], BF16)
        ones_f = c.sb("ones_f", [128, 128], F32)
        cmask = {k: c.sb(k, [128, 128], F32) for k in ('maskneg_p', 'maskneg_s', 'maskbig_p', 'maskbig_s', 'rowmask', 'parsel', 'hmask')}
        colmask = c.sb("colmask", [128, 16 * 64], BF16)
        gin = c.sb("gin", [128, 8], F32)
        gmem = c.sb("gmem", [128, 8], F32)
        bgate = c.sb("bgate", [128, 32], F32)
        hm_fm = yk_fm[:, :, 0:256]
        wbufs = Rot([c.sb("wb%d" % i, [128, 8, 512], BF16) for i in range(2)])
        stg_b = Rot([c.sb("stgb%d" % i, [128, 512], BF16) for i in range(4)])
        stg_f = Rot([c.sb("stgf%d" % i, [128, 512], F32) for i in range(2)])
        f4 = Rot([c.sb("f4_%d" % i, [128, D], F32) for i in range(4)])
        b2 = Rot([c.sb("b2_%d" % i, [128, D], BF16) for i in range(5)])
        fq = Rot([c.sb("fq_%d" % i, [128, 128], F32) for i in range(6)])
        bq = Rot([c.sb("bq_%d" % i, [128, 128], BF16) for i in range(4)])
        sq_junk = c.sb("sq_junk", [128, D], BF16)
        st_small = Rot([c.sb("sts%d" % i, [128, 4], F32) for i in range(4)])
        banks = [c.ps("pb%d" % i, [128, 512], F32) for i in range(8)]
        pbank = Rot(banks[0:4])
        ptr2 = [banks[4], banks[5]]

        def bfv(bank):
            return bank[:].bitcast(BF16)

        c.dma('sp', ident[:], IN("c_ident")[:, :])
        c.dma('sp', ones_f[:], IN("c_ones")[:, :])
        c.copy(ident_b[:], ident[:])
        if stage >= 2:
            for k in cmask:
                c.dma('sp', cmask[k][:], IN("c_" + k)[:, :])
            c.dma('pool', colmask[:], IN("c_colmask")[:, :])

        def rms_rstd(src, P, n=D):
            st = st_small.next()
            c.act(sq_junk[0:P, 0:n], src, AF.Square, accum_out=st[0:P, 0:1])
            c.ts(st[0:P, 1:2], st[0:P, 0:1], 1.0 / n, EPS, op0=ALU.mult, op1=ALU.add)
            c.act(st[0:P, 2:3], st[0:P, 1:2], AF.Sqrt)
            c.recip(st[0:P, 3:4], st[0:P, 2:3])
            return st[0:P, 3:4]

        def norm_to_fm(src, P, gcol, dst):
            rstd = rms_rstd(src, P)
            xn = f4.next()
            c.ts(xn[0:P, :], src, rstd, None, op0=ALU.mult)
            for j in range(8):
                pt = ptr2[j // 4]
                c.tr(pt[:, (j % 4) * 128:(j % 4) * 128 + P], xn[0:P, j * 128:(j + 1) * 128], ident[0:P, 0:P])
            for hlf in range(2):
                pv = ptr2[hlf][:].rearrange("p (j t) -> p j t", t=128)[:, :, 0:P]
                c.tt(dst[:, 4 * hlf:4 * hlf + 4, :], pv,
                     gcol[:, 4 * hlf:4 * hlf + 4].unsqueeze(2).to_broadcast([128, 4, P]), ALU.mult)

        def tm_to_fm(src_b, P, dst):
            pv = bfv(ptr2[0])
            for j in range(8):
                c.tr(pv[:, j * 128:j * 128 + P], src_b[0:P, j * 128:(j + 1) * 128], ident_b[0:P, 0:P])
            c.copy(dst, pv.rearrange("p (j t) -> p j t", t=128)[:, :, 0:P], eng='act')

        def load_w(src):
            wb = wbufs.next()
            cw = src.shape[1]
            c.dma('pool', wb[:, :, 0:cw], src.rearrange("(kt p) c -> p kt c", p=128))
            return wb

        evac_flip = [0]

        def evac(out, in_, func=None, bias=None, scale=None):
            if func is not None:
                c.act(out, in_, func, bias=bias, scale=scale)
            elif scale is not None:
                if evac_flip[0] % 2 == 0:
                    c.act(out, in_, AF.Copy, scale=scale)
                else:
                    c.ts(out, in_, scale, None, op0=ALU.mult)
                evac_flip[0] += 1
            else:
                c.copy(out, in_, eng='act' if evac_flip[0] % 2 == 0 else 'dve')
                evac_flip[0] += 1

        def proj_fm(l, col0, ncols, row0, func=None, scale=None, biascol=None):
            for c0 in range(0, ncols, 512):
                cw = min(512, ncols - c0)
                wb = load_w(IN("w_in%d" % l)[:, col0 + c0:col0 + c0 + cw])
                for cb in range(cw // 128):
                    for (t0, n) in STS:
                        pb = pbank.next()
                        for kt in range(8):
                            c.mm(pb[:, 0:n], wb[:, kt, cb * 128:(cb + 1) * 128], h_fm[:, kt, t0:t0 + n],
                                 start=(kt == 0), stop=(kt == 7))
                        sg = stg_b.next()
                        b = biascol((c0 + cb * 128) // 128) if biascol is not None else None
                        evac(sg[:, 0:n], pb[:, 0:n], func=func, bias=b, scale=scale)
                        r0 = row0 + c0 + cb * 128
                        c.dma('sp', sfm[r0:r0 + 128, t0:t0 + n], sg[:, 0:n])

        def proj_tm(l, col0, ncols, scol0, func=None, scale=None):
            for c0 in range(0, ncols, 512):
                wb = load_w(IN("w_in%d" % l)[:, col0 + c0:col0 + c0 + 512])
                for (t0, P) in TILES:
                    pb = pbank.next()
                    for kt in range(8):
                        c.mm(pb[0:P, :], h_fm[:, kt, t0:t0 + P], wb[:, kt, :], start=(kt == 0), stop=(kt == 7))
                    sg = stg_b.next()
                    evac(sg[0:P, :], pb[0:P, :], func=func, scale=scale)
                    c.dma('sp', stm[t0:t0 + P, scol0 + c0:scol0 + c0 + 512], sg[0:P, :])

        def proj_small(l, col0, ncols, dst):
            wb = load_w(IN("w_in%d" % l)[:, col0:col0 + ncols])
            for (t0, n) in STS:
                pb = pbank.next()
                for kt in range(8):
                    c.mm(pb[0:ncols, 0:n], wb[:, kt, 0:ncols], h_fm[:, kt, t0:t0 + n], start=(kt == 0), stop=(kt == 7))
                sg = stg_f.next()
                c.copy(sg[0:ncols, 0:n], pb[0:ncols, 0:n], eng='act')
                c.dma('sp', dst[:, t0:t0 + n], sg[0:ncols, 0:n])

        def dump_fm(name, src):
            if dbg:
                for j in range(8):
                    c.dma('pool', DBG(name)[j * 128:(j + 1) * 128, :], src[:, j, :])

        def x_src(l, tt):
            t0, P = TILES[tt]
            if l == 0:
                return IN("x_p")[t0:t0 + P, :] if tt < 16 else IN("x_s")[:, :]
            return x_scr[t0:t0 + P, :]

        def phase0(l):
            c.dma('sp', gin[:], IN("norm_in_c")[l])
            c.dma('sp', gmem[:], IN("mem_norm_c")[l])
            c.dma('sp', bgate[:], IN("b_gate_c")[l])
            for tt, (t0, P) in enumerate(TILES):
                xt = f4.next()
                c.dma('sp', xt[0:P, :], x_src(l, tt))
                norm_to_fm(xt[0:P, :], P, gin, h_fm[:, :, t0:t0 + P])

        def memkv(l, mk_fm, mv_tm):
            for mt in range(2):
                mtile = f4.next()
                c.dma('sp', mtile[:, :], IN("mem")[mt * 128:(mt + 1) * 128, :])
                norm_to_fm(mtile[:, :], 128, gmem, hm_fm[:, :, mt * 128:(mt + 1) * 128])
            for half in ([h_ for h_ in range(2) if ('kv%d' % h_) not in SKIP] if 'kvw' not in SKIP else ()):
                for c0 in range(0, 1024, 512):
                    wb = load_w(IN("w_kv")[l, :, half * 1024 + c0:half * 1024 + c0 + 512])
                    for mt in range(2):
                        pb = pbank.next()
                        for kt in range(8):
                            c.mm(pb[:, :], hm_fm[:, kt, mt * 128:(mt + 1) * 128], wb[:, kt, :], start=(kt == 0), stop=(kt == 7))
                        sg = stg_f.next()
                        c.copy(sg[:, :], pb[:, :], eng='act')
                        dst = mk_o if half == 0 else mv_o
                        c.dma('sp', dst[l, mt * 128:(mt + 1) * 128, c0:c0 + 512], sg[:, :])
                        if half == 1:
                            c.copy(mv_tm[:, mt, c0:c0 + 512], sg[:, :], eng='dve')
                    if half == 0 and 'mkfm' not in SKIP:
                        for cb in range(4):
                            pb = pbank.next()
                            for kt in range(8):
                                c.mm(pb[:, 0:256], wb[:, kt, cb * 128:(cb + 1) * 128], hm_fm[:, kt, :], start=(kt == 0), stop=(kt == 7))
                            c.ts(mk_fm[:, c0 // 128 + cb, :], pb[:, 0:256], 1.0 / 16, None, op0=ALU.mult)

        def projections(l):
            if 'projfm' not in SKIP:
                proj_fm(l, O_XBC, 1536, R_XBC)
            for tt in ((15, 16) if 'convst' not in SKIP else ()):
                t0, P = TILES[tt]
                for c0 in range(0, 1536, 512):
                    wb = load_w(IN("w_in%d" % l)[:, O_XBC + c0:O_XBC + c0 + 512])
                    pb = pbank.next()
                    for kt in range(8):
                        c.mm(pb[0:P, :], h_fm[:, kt, t0:t0 + P], wb[:, kt, :], start=(kt == 0), stop=(kt == 7))
                    sg = stg_f.next()
                    c.copy(sg[0:P, :], pb[0:P, :], eng='act')
                    if tt == 15:
                        c.dma('sp', conv_po[l, :, c0:c0 + 512], sg[125:128, :])
                    else:
                        for b in range(NB):
                            c.dma('sp', conv_so[l, b, :, c0:c0 + 512], sg[4 * b + 1:4 * b + 4, :])
            if 'small' not in SKIP:
                if 's16' not in SKIP:
                    proj_small(l, O_DT, 16, g_scr[0:16, :])
                if 's8' not in SKIP:
                    proj_small(l, O_I, 8, g_scr[16:24, :])
            if stage >= 2:
                proj_tm(l, O_ZSSD, 1024, C_ZSSD, func=AF.Silu)
            if stage >= 3:
                proj_fm(l, O_U, 1024, R_U)
                proj_fm(l, O_ZS5, 1024, R_ZS5, func=AF.Silu)
                proj_fm(l, O_Q, 1024, R_Q)
                proj_fm(l, O_K, 1024, R_K, scale=1.0 / 16)
                proj_tm(l, O_K, 1024, C_KT, scale=1.0 / 16)
                proj_tm(l, O_V, 1024, C_V)
                proj_tm(l, O_O, 1024, C_O, func=AF.Sigmoid)
                proj_tm(l, O_ZML, 1024, C_ZML, func=AF.Silu)
                proj_fm(l, O_QXA, 1024, R_QXA)
                proj_tm(l, O_ZXA, 1024, C_ZXA, func=AF.Silu)
                proj_fm(l, O_GATE, 4096, R_GATE, func=AF.Sigmoid, biascol=lambda j: bgate[:, j:j + 1])

        def ssd_branch(l):
            bes = contextlib.ExitStack()
            uid = "_a%d" % l
            anames = []

            def A(name, shape, dt=F32):
                anames.append(name + uid)
                return bes.enter_context(nc.sbuf_tensor(name + uid, list(shape), dt))

            cw_sb = A("cw_sb", [128, 12, 4])
            cb_sb = A("cb_sb", [128, 12])
            hp16 = A("hp16", [16, 8])
            dbc = A("dbc", [128, 16])
            gbc = A("gbc", [128, D])
            DI = A("DI", [128, 16, 128], BF16)
            xr = A("xr", [128, 12, 515], BF16)
            xr_s = A("xr_s", [128, 12, 16, 7], BF16)
            xc = A("xc", [128, 12, 512], BF16)
            acc = A("acc", [128, 512])
            acs_r = Rot([A("acs%d" % i, [16, 128]) for i in range(2)])
            dt_r = Rot([A("dtt%d" % i, [16, 128]) for i in range(2)])
            tw_r = Rot([A("tww%d" % i, [16, 128]) for i in range(2)])
            rawg_r = Rot([A("rawg%d" % i, [16, 128]) for i in range(2)])
            selm_r = Rot([A("selm%d" % i, [16, 128]) for i in range(3)])
            STs_r = Rot([A("STs%d" % i, [128, D], BF16) for i in range(2)])
            btm_r = Rot([A("btm%d" % i, [128, 256], BF16) for i in range(2)])
            g16 = Rot([A("g16_%d" % i, [16, 128]) for i in range(4)])
            gtm_r = Rot([A("gtm%d" % i, [128, 96]) for i in range(2)])
            eatm_r = Rot([A("eatm%d" % i, [128, 16]) for i in range(2)])
            cbT_r = Rot([A("cbT%d" % i, [128, 2, 128]) for i in range(2)])
            ST = A("ST", [128, D])
            ST_bf = A("ST_bf", [128, D], BF16)
            dvec = A("dvec", [128, 16, 8])
            ers = A("ers", [16, 16, 8])
            els = A("els", [16, 16])
            dcs = A("dcs", [128, 16])
            S_r = Rot([A("S_b%d" % i, [128, 8, 128]) for i in range(2)])
            Cm_r = Rot([A("Cm%d" % i, [128, 2, 64], BF16) for i in range(2)])
            Bm_r = Rot([A("Bm%d" % i, [64, 256], BF16) for i in range(2)])
            c.dma('sp', cw_sb[:].rearrange("p j k -> p (j k)"), IN("conv_w_c")[l])
            c.dma('sp', cb_sb[:], IN("conv_b_c")[l])
            c.dma('sp', hp16[:, 0:2], IN("ssd_hp")[l])
            c.dma('sp', dbc[:], IN("ssd_d")[l:l + 1, :].partition_broadcast(128))
            c.dma('sp', gbc[:], IN("ssd_norm")[l:l + 1, :].partition_broadcast(128))
            c.act(hp16[:, 2:3], hp16[:, 1:2], AF.Exp)
            c.ts(hp16[:, 3:4], hp16[:, 2:3], -1.0, None, op0=ALU.mult)
            dtb, aneg = hp16[:, 0:1], hp16[:, 3:4]
            for h in range(16):
                c.ts(DI[:, h, :], ident[:], dbc[:, h:h + 1], None, op0=ALU.mult)
            c.memset(ST[:], 0.0)
            c.memset(ST_bf[:], 0.0)
            pX, pB, pCB, pYA, pYB, pEA, pEB, pBC = banks
            pbc_r = Rot([pBC, pCB])
            for si, (s0, n) in enumerate(STS):
                sample = (si == 4)
                src_rows = sfm[R_XBC:R_XBC + 1536, :].rearrange("(j p) t -> p j t", p=128)
                if not sample:
                    if s0 == 0:
                        c.memset(xr[:, :, 0:3], 0.0)
                        c.dma('sp', xr[:, :, 3:515], src_rows[:, :, 0:512])
                    else:
                        c.dma('sp', xr[:, :, 0:515], src_rows[:, :, s0 - 3:s0 + 512])
                    for j in range(12):
                        c.ts(acc[:, :], xr[:, j, 3:515], cw_sb[:, j, 3:4], cb_sb[:, j:j + 1], op0=ALU.mult, op1=ALU.add)
                        for k in (2, 1, 0):
                            c.stt(acc[:, :], xr[:, j, k:k + 512], cw_sb[:, j, k:k + 1], acc[:, :], ALU.mult, ALU.add)
                        c.act(xc[:, j, :], acc[:, :], AF.Silu)
                else:
                    raw = b2.next()
                    rawv = raw[:, 0:768].rearrange("p (j t) -> p j t", t=64)
                    c.dma('sp', rawv, src_rows[:, :, 2048:2112])
                    csts = (f4.next(), f4.next())
                    for hlf in range(2):
                        c.dma('sp', csts[hlf][0:48, 0:768], IN("st_conv")[l][:, hlf * 768:(hlf + 1) * 768])
                    for j in range(12):
                        pt = ptr2[j // 6]
                        c.tr(pt[:, (j % 6) * 48:(j % 6) * 48 + 48], csts[j // 6][0:48, (j % 6) * 128:(j % 6 + 1) * 128], ident[0:48, 0:48])
                    for j in range(12):
                        c.copy(xr_s[:, j, :, 0:3], ptr2[j // 6][:, (j % 6) * 48:(j % 6) * 48 + 48].rearrange("p (b k) -> p b k", k=3),
                               eng='act' if j % 2 else 'dve')
                        c.copy(xr_s[:, j, :, 3:7], rawv[:, j, :].rearrange("p (b t) -> p b t", t=4), eng='dve' if j % 2 else 'act')
                    for j in range(12):
                        av = acc[:, 0:64].rearrange("p (b t) -> p b t", t=4)
                        c.ts(av, xr_s[:, j, :, 3:7], cw_sb[:, j, 3:4], cb_sb[:, j:j + 1], op0=ALU.mult, op1=ALU.add)
                        for k in (2, 1, 0):
                            c.stt(av, xr_s[:, j, :, k:k + 4], cw_sb[:, j, k:k + 1], av, ALU.mult, ALU.add)
                        c.act(xc[:, j, 0:64], acc[:, 0:64], AF.Silu)
                for (t0, P) in [t for t in TILES if s0 <= t[0] < s0 + n]:
                    lo = t0 - s0
                    mneg = cmask['maskneg_s'] if sample else cmask['maskneg_p']
                    acs, dtt, tww = acs_r.next(), dt_r.next(), tw_r.next()
                    ge, gla = g16.next(), g16.next()
                    rawg = rawg_r.next()
                    c.dma('sp', rawg[:, 0:P], g_scr[0:16, t0:t0 + P])
                    c.act(ge[:, 0:P], rawg[:, 0:P], AF.Exp, bias=dtb)
                    c.act(dtt[:, 0:P], ge[:, 0:P], AF.Ln, bias=1.0)
                    c.ts(gla[:, 0:P], dtt[:, 0:P], aneg, None, op0=ALU.mult)
                    if not sample:
                        c.scan(acs[:, 0:P], ones_f[0:16, 0:P], gla[:, 0:P], 0.0, ALU.mult, ALU.add)
                        alast = acs[:, P - 1:P]
                        gd = g16.next()
                        c.act(gd[:, 0:P], acs[:, 0:P], AF.Exp, bias=alast, scale=-1.0)
                    else:
                        av = acs[:, 0:64].rearrange("p (b t) -> p b t", t=4)
                        lv = gla[:, 0:64].rearrange("p (b t) -> p b t", t=4)
                        c.copy(av[:, :, 0:1], lv[:, :, 0:1])
                        for t in (1, 2, 3):
                            c.tt(av[:, :, t:t + 1], av[:, :, t - 1:t], lv[:, :, t:t + 1], ALU.add)
                        gd = g16.next()
                        dv = gd[:, 0:64].rearrange("p (b t) -> p b t", t=4)
                        c.tt(dv, av[:, :, 3:4].to_broadcast([16, 16, 4]), av, ALU.subtract)
                        c.act(gd[:, 0:64], gd[:, 0:64], AF.Exp)
                    c.tt(tww[:, 0:P], gd[:, 0:P], dtt[:, 0:P], ALU.mult)
                    for qi, qsrc in enumerate((acs, dtt, tww)):
                        c.tr(pB[0:P, 32 * qi:32 * qi + 16], qsrc[:, 0:P], ident[0:16, 0:16])
                    gtm = gtm_r.next()
                    c.copy(gtm[0:P, :].rearrange("p (q x) -> p q x", x=32)[:, :, 0:16], pB[0:P, 0:96].rearrange("p (q x) -> p q x", x=32)[:, :, 0:16])
                    eatm = eatm_r.next()
                    c.act(eatm[0:P, :], gtm[0:P, 0:16], AF.Exp)
                    pxv = bfv(pX)
                    for j in range(8):
                        c.tr(pxv[0:P, j * 128:(j + 1) * 128], xc[:, j, lo:lo + P], ident_b[:, :])
                    xtm = b2.next()
                    c.copy(xtm[0:P, :], pxv[0:P, :], eng='act')
                    pbv = bfv(pYA)
                    for g in range(2):
                        c.tr(pbv[0:P, g * 128:(g + 1) * 128], xc[:, 8 + g, lo:lo + P], ident_b[:, :])
                    btm = btm_r.next()
                    c.copy(btm[0:P, :], pbv[0:P, 0:256])
                    btms = (btm[:, 0:128], btm[:, 128:256])
                    cbT = cbT_r.next()
                    for g in range(2):
                        c.mm(pCB[0:P, g * 128:g * 128 + P], xc[:, 8 + g, lo:lo + P], xc[:, 10 + g, lo:lo + P])
                    c.copy(cbT[0:P, :, 0:P], pCB[0:P, 0:256].rearrange("p (g t) -> p g t", g=2)[:, :, 0:P], eng='act')
                    for h in range(16):
                        g = h // 8
                        pbc = pbc_r.next()
                        selm = selm_r.next()
                        c.ts(selm[:, 0:P], acs[:, 0:P], ident[0:16, h:h + 1], None, op0=ALU.mult)
                        c.mm(pbc[0:P, 0:P], ones_f[0:16, 0:P], selm[:, 0:P])
                        tsb = fq.next()
                        c.stt(tsb[0:P, 0:P], pbc[0:P, 0:P], gtm[0:P, h:h + 1], mneg[0:P, 0:P], ALU.subtract, ALU.min)
                        E = fq.next()
                        c.act(E[0:P, 0:P], tsb[0:P, 0:P], AF.Exp)
                        MT = bq.next()
                        c.stt(MT[0:P, 0:P], E[0:P, 0:P], gtm[0:P, 32 + h:33 + h], cbT[0:P, g, 0:P], ALU.mult, ALU.mult)
                        py = pYA if h < 8 else pYB
                        oc = (h % 8) * 64
                        c.mm(py[0:P, oc:oc + 64], MT[0:P, 0:P], xtm[0:P, h * 64:(h + 1) * 64], start=True, stop=False)
                        c.mm(py[0:P, oc:oc + 64], DI[0:P, h, 0:P], xtm[0:P, h * 64:(h + 1) * 64], start=False, stop=True)
                    if not sample:
                        for g, pe_ in enumerate((pEA, pEB)):
                            c.mm(pe_[0:P, :], xc[:, 10 + g, lo:lo + P], ST_bf[:, g * 512:(g + 1) * 512])
                    else:
                        for b in range(NB):
                            Sb = S_r.next()
                            c.dma('sp', Sb[:], IN("st_ssd")[l, b].rearrange("(j p) n -> p j n", p=128))
                            for j in range(8):
                                c.tr((pX, pB)[j // 4][:, (j % 4) * 128:(j % 4 + 1) * 128], Sb[:, j, :], ident[:, :])
                            STs = STs_r.next()
                            c.copy(STs[:, 0:512], pX[:, :], eng='act')
                            c.copy(STs[:, 512:1024], pB[:, :], eng='dve')
                            Cm = Cm_r.next()
                            c.tt(Cm[:, :, :], xc[:, 10:12, 0:64],
                                 colmask[:, b * 64:(b + 1) * 64].unsqueeze(1).to_broadcast([128, 2, 64]), ALU.mult)
                            for g, pe_ in enumerate((pEA, pEB)):
                                c.mm(pe_[0:P, :], Cm[:, g, :], STs[:, g * 512:(g + 1) * 512], start=(b == 0), stop=(b == NB - 1))
                            if b == 0:
                                avl = acs[:, 0:64].rearrange("p (b t) -> p b t", t=4)[:, :, 3:4]
                                c.act(els[:, :].unsqueeze(2), avl, AF.Exp)
                                c.tt(ers[:, :, :], els[:, :].unsqueeze(2).to_broadcast([16, 16, 8]),
                                     cmask['hmask'][0:16, 0:8].unsqueeze(1).to_broadcast([16, 16, 8]), ALU.mult)
                                c.mm(pCB[:, 0:128], cmask['parsel'][0:16, :], ers[:, :, :].rearrange("p b j -> p (b j)"))
                                c.copy(dvec[:, :, :], pCB[:, 0:128].rearrange("p (b j) -> p b j", j=8))
                                xw = b2.next()
                                c.tt(xw[0:P, :].rearrange("p (h q) -> p h q", q=64), xtm[0:P, :].rearrange("p (h q) -> p h q", q=64),
                                     gtm[0:P, 64:80].unsqueeze(2).to_broadcast([P, 16, 64]), ALU.mult)
                                xw_keep = xw
                            Bm = Bm_r.next()
                            for g in range(2):
                                c.ts(Bm[0:64, g * 128:(g + 1) * 128], btm[0:64, g * 128:(g + 1) * 128], cmask['rowmask'][0:64, b:b + 1], None, op0=ALU.mult)
                            for j in range(8):
                                pd = pCB if j < 4 else pBC
                                c.mm(pd[:, (j % 4) * 128:(j % 4 + 1) * 128], xw_keep[0:64, j * 128:(j + 1) * 128],
                                     Bm[0:64, (j // 4) * 128:(j // 4 + 1) * 128])
                            c.tt(Sb[:, :, :], Sb[:, :, :], dvec[:, b, :].unsqueeze(2).to_broadcast([128, 8, 128]), ALU.mult)
                            c.tt(Sb[:, 0:4, :], Sb[:, 0:4, :], pCB[:, :].rearrange("p (j n) -> p j n", n=128), ALU.add)
                            c.tt(Sb[:, 4:8, :], Sb[:, 4:8, :], pBC[:, :].rearrange("p (j n) -> p j n", n=128), ALU.add)
                            c.dma('sp', ssd_so[l, b].rearrange("(j p) n -> p j n", p=128), Sb[:, :, :])
                    ty = f4.next()
                    for g, (pe_, py) in enumerate(((pEA, pYA), (pEB, pYB))):
                        tv = ty[0:P, g * 512:(g + 1) * 512]
                        c.tt(tv.rearrange("p (h q) -> p h q", q=64), pe_[0:P, :].rearrange("p (h q) -> p h q", q=64),
                             eatm[0:P, g * 8:(g + 1) * 8].unsqueeze(2).to_broadcast([P, 8, 64]), ALU.mult)
                        c.tt(tv, tv, py[0:P, :], ALU.add)
                    ztm = b2.next()
                    c.dma('sp', ztm[0:P, :], stm[t0:t0 + P, C_ZSSD:C_ZSSD + 1024])
                    c.tt(ty[0:P, :], ty[0:P, :], ztm[0:P, :], ALU.mult)
                    rstd = rms_rstd(ty[0:P, :], P)
                    yn = b2.next()
                    c.stt(yn[0:P, :], ty[0:P, :], rstd, gbc[0:P, :], ALU.mult, ALU.mult)
                    tm_to_fm(yn, P, yk_fm[:, :, t0:t0 + P])
                    if not sample:
                        xw = b2.next()
                        c.tt(xw[0:P, :].rearrange("p (h q) -> p h q", q=64), xtm[0:P, :].rearrange("p (h q) -> p h q", q=64),
                             gtm[0:P, 64:80].unsqueeze(2).to_broadcast([P, 16, 64]), ALU.mult)
                        for g, pe_ in enumerate((pEA, pEB)):
                            c.mm(pe_[:, :], btm[0:P, g * 128:(g + 1) * 128], xw[0:P, g * 512:(g + 1) * 512])
                        e16 = g16.next()
                        c.act(e16[:, 0:1], alast, AF.Exp)
                        ed = g16.next()
                        c.ts(ed[:, 0:16], ident[0:16, 0:16], e16[:, 0:1], None, op0=ALU.mult)
                        c.mm(pX[:, 0:16], ones_f[0:16, :], ed[:, 0:16])
                        c.copy(dcs[:, :], pX[:, 0:16])
                        c.tt(ST[:, :].rearrange("p (h q) -> p h q", q=64), ST[:, :].rearrange("p (h q) -> p h q", q=64),
                             dcs[:, :].unsqueeze(2).to_broadcast([128, 16, 64]), ALU.mult)
                        for g, pe_ in enumerate((pEA, pEB)):
                            c.tt(ST[:, g * 512:(g + 1) * 512], ST[:, g * 512:(g + 1) * 512], pe_[:, :], ALU.add)
                        c.copy(ST_bf[:, :], ST[:, :], eng='act')
                        if t0 == 1920:
                            for j in range(8):
                                c.tr(ptr2[j // 4][:, (j % 4) * 128:(j % 4 + 1) * 128], ST[:, j * 128:(j + 1) * 128], ident[:, :])
                            so = f4.next()
                            c.copy(so[:, 0:512], ptr2[0][:, :], eng='act')
                            c.copy(so[:, 512:1024], ptr2[1][:, :], eng='dve')
                            c.dma('sp', ssd_po[l].rearrange("(j p) n -> p j n", p=128), so[:, :].rearrange("p (j n) -> p j n", n=128))
            dump_fm("y_a", yk_fm)
            c.barrier()
            c.release(anames)
            bes.close()
        def s5_branch(l):
            bes = contextlib.ExitStack()
            uid = "_b%d" % l
            anames = []

            def A(name, shape, dt=F32):
                anames.append(name + uid)
                return bes.enter_context(nc.sbuf_tensor(name + uid, list(shape), dt))

            TWO_PI = 6.283185307179586
            PI = 3.141592653589793
            par = A("par", [64, 3, 64])
            tb = [A("tb%d" % i, [64, 64]) for i in range(12)]
            tbi = A("tbi", [64, 64], mybir.dt.int32)
            A2 = A("A2", [64, 2, 64])
            Bc = A("Bc", [64, 2, 64])
            sre, sim = A("sre", [64, 64]), A("sim", [64, 64])
            big = A("big", [64, 2 * 64 * 16])
            Bb = big[:, :].rearrange("p (r x) -> p r x", r=2)
            CT = A("CT", [64, 2, 64 * 16])
            BT = A("BT", [128, 8, 2, 8, 64], BF16)
            bmask = A("bmask", [128, 8])
            Dd = A("Dd", [128, 8, 128], BF16)
            dcol = A("dcol", [128, 8])
            SUB = 16
            u_st = A("u_st", [128, 8, 256], BF16)
            bu1 = A("bu1", [64, 2 * 64 * SUB])
            b4 = lambda t: t[:, :].rearrange("p (r g t) -> p r g t", r=2, g=64)
            bu_r = Rot([b4(big), b4(bu1)])
            hist_r = Rot([b4(A("hist%d" % i, [64, 2 * 64 * SUB])) for i in range(2)])
            t1_r = Rot([A("t1_0", [64, 2, 64 * 4])])
            t2_r = Rot([A("t2_0", [64, 2, 64 * 4])])
            H0s = A("H0s", [64, 2, 64, 4])
            yg_r = Rot([A("yg%d" % i, [SUB, D], BF16) for i in range(2)])
            hs = A("hs", [64, 128])

            c.dma('sp', par[:, 0, :], IN("s5_a_re_t")[l])
            c.dma('sp', par[:, 1, :], IN("s5_a_im_t")[l])
            c.dma('sp', par[:, 2, :], IN("s5_log_dt")[l:l + 1, :].partition_broadcast(64))
            c.dma('sp', Bb[:, 0, :], IN("s5_b_re_t")[l])
            c.dma('sp', Bb[:, 1, :], IN("s5_b_im_t")[l])
            c.dma('sp', CT[:, 0, :], IN("s5_c_re_t")[l])
            c.dma('sp', CT[:, 1, :], IN("s5_c_im_t")[l])
            c.dma('sp', bmask[:], IN("c_bmask")[:, 0:8])
            c.dma('sp', dcol[:], IN("s5_d_c")[l])
            c.ts(CT[:, 1, :], CT[:, 1, :], -1.0, None, op0=ALU.mult)
            for jb in range(8):
                c.ts(Dd[:, jb, :], ident[:, :], dcol[:, jb:jb + 1], None, op0=ALU.mult)
            are, aim = par[:, 0, :], par[:, 1, :]
            dtt, lr, th, mag, red, sn, cs, t_a, t_b, t_c, den, rden = tb
            c.act(dtt[:], par[:, 2, :], AF.Exp)
            c.tt(lr[:], are, dtt[:], ALU.mult)
            c.tt(th[:], aim, dtt[:], ALU.mult)
            c.act(mag[:], lr[:], AF.Exp)

            def sin_of(dst, ang, shift):
                c.ts(red[:], ang, shift, None, op0=ALU.add)
                c.ts(t_a[:], red[:], 1.0 / TWO_PI, None, op0=ALU.mult)
                c.copy(tbi[:], t_a[:])
                c.copy(t_a[:], tbi[:])
                c.stt(red[:], t_a[:], -TWO_PI, red[:], ALU.mult, ALU.add)
                c.ts(t_b[:], red[:], PI, -TWO_PI, op0=ALU.is_gt, op1=ALU.mult)
                c.tt(red[:], red[:], t_b[:], ALU.add)
                c.ts(t_b[:], red[:], -PI, TWO_PI, op0=ALU.is_lt, op1=ALU.mult)
                c.tt(red[:], red[:], t_b[:], ALU.add)
                c.ts(red[:], red[:], PI, -PI, op0=ALU.min, op1=ALU.max)
                c.act(dst, red[:], AF.Sin)

            sin_of(sn[:], th[:], 0.0)
            sin_of(cs[:], th[:], PI / 2)
            c.tt(A2[:, 0, :], mag[:], cs[:], ALU.mult)
            c.copy(A2[:, 1, :], A2[:, 0, :])
            c.tt(Bc[:, 1, :], mag[:], sn[:], ALU.mult)
            c.ts(Bc[:, 0, :], Bc[:, 1, :], -1.0, None, op0=ALU.mult)
            lre, lim = A2[:, 0, :], Bc[:, 1, :]
            c.ts(t_a[:], lre, -1.0, None, op0=ALU.add)
            c.tt(den[:], are, are, ALU.mult)
            c.tt(t_b[:], aim, aim, ALU.mult)
            c.tt(den[:], den[:], t_b[:], ALU.add)
            c.recip(rden[:], den[:])
            c.tt(t_b[:], t_a[:], are, ALU.mult)
            c.tt(t_c[:], lim, aim, ALU.mult)
            c.tt(t_b[:], t_b[:], t_c[:], ALU.add)
            c.tt(sre[:], t_b[:], rden[:], ALU.mult)
            c.tt(t_b[:], lim, are, ALU.mult)
            c.tt(t_c[:], t_a[:], aim, ALU.mult)
            c.tt(t_b[:], t_b[:], t_c[:], ALU.subtract)
            c.tt(sim[:], t_b[:], rden[:], ALU.mult)
            v3 = lambda ap: ap.rearrange("p (g c) -> p g c", c=16)
            sb_ = lambda ap: ap.unsqueeze(2).to_broadcast([64, 64, 16])
            T1 = bu1[:, 0:1024]
            T2 = bu1[:, 1024:2048]
            c.tt(v3(T1), v3(Bb[:, 0, :]), sb_(sim[:]), ALU.mult)
            c.tt(v3(T2), v3(Bb[:, 1, :]), sb_(sim[:]), ALU.mult)
            c.tt(v3(Bb[:, 0, :]), v3(Bb[:, 0, :]), sb_(sre[:]), ALU.mult)
            c.tt(Bb[:, 0, :], Bb[:, 0, :], T2, ALU.subtract)
            c.tt(v3(Bb[:, 1, :]), v3(Bb[:, 1, :]), sb_(sre[:]), ALU.mult)
            c.tt(Bb[:, 1, :], Bb[:, 1, :], T1, ALU.add)
            pW = banks[0]
            for jb in range(8):
                for ri in range(2):
                    c.tr(pW[:, ri * 64:(ri + 1) * 64], Bb[:, ri, jb * 128:(jb + 1) * 128], ident[0:64, 0:64])
                for ri in range(2):
                    c.tt(BT[:, jb, ri, :, :], pW[:, ri * 64:(ri + 1) * 64].unsqueeze(1).to_broadcast([128, 8, 64]),
                         bmask[:, :].unsqueeze(2).to_broadcast([128, 8, 64]), ALU.mult)

            pBU = Rot([banks[1], banks[2]])
            pY = [(banks[3], banks[4]), (banks[5], banks[6])]
            pyi = [0]
            pTo = banks[7]
            nsub = 0
            prev_hist = None
            for si, (s0, n) in enumerate([(256 * i, 256) for i in range(8)] + [(2048, 64)]):
                sample = (si == 8)
                c.dma('sp', u_st[:, :, 0:n], sfm[R_U:R_U + 1024, s0:s0 + n].rearrange("(j p) t -> p j t", p=128))
                for q0 in range(0, n, SUB):
                    bu = bu_r.next()
                    for jb in range(8):
                        pb = pBU.next()
                        for ri in range(2):
                            for g in range(8):
                                o = (ri * 8 + g) * SUB
                                c.mm(pb[0:64, o:o + SUB], BT[:, jb, ri, g, :], u_st[:, jb, q0:q0 + SUB])
                        c.copy(bu[:, :, jb * 8:(jb + 1) * 8, :], pb[0:64, 0:16 * SUB].rearrange("p (r g t) -> p r g t", r=2, g=8), eng='act')
                    hist = hist_r.next()
                    if not sample:
                        for t in range(SUB):
                            if nsub == 0 and t == 0:
                                c.copy(hist[:, :, :, 0], bu[:, :, :, 0])
                                continue
                            hp = prev_hist[:, :, :, SUB - 1] if t == 0 else hist[:, :, :, t - 1]
                            t1, t2 = t1_r.next(), t2_r.next()
                            t1v = t1[:, :, 0:64]
                            t2v = t2[:, :, 0:64]
                            c.tt(t1v, A2[:, :, :], hp, ALU.mult)
                            c.tt(t2v[:, 0, :], Bc[:, 0, :], hp[:, 1, :], ALU.mult, eng='pool')
                            c.tt(t2v[:, 1, :], Bc[:, 1, :], hp[:, 0, :], ALU.mult, eng='pool')
                            c.tt(t1v, t1v, t2v, ALU.add)
                            c.tt(hist[:, :, :, t], t1v, bu[:, :, :, t], ALU.add)
                    else:
                        bh = q0 // SUB
                        c.dma('sp', H0s[:, :, :, :], IN("st_s5_t")[l][:, :, :, bh * 4:(bh + 1) * 4])
                        hv = hist[:, :, :, :].rearrange("p r g (b t) -> p r g b t", t=4)
                        bv = bu[:, :, :, :].rearrange("p r g (b t) -> p r g b t", t=4)
                        for t in range(4):
                            t1, t2 = t1_r.next(), t2_r.next()
                            t1v = t1[:, :, :].rearrange("p r (g b) -> p r g b", b=4)
                            t2v = t2[:, :, :].rearrange("p r (g b) -> p r g b", b=4)
                            for ri in range(2):
                                hp_r = H0s[:, ri, :, :] if t == 0 else hv[:, ri, :, :, t - 1]
                                hp_o = H0s[:, 1 - ri, :, :] if t == 0 else hv[:, 1 - ri, :, :, t - 1]
                                c.tt(t1v[:, ri], A2[:, ri, :].unsqueeze(2).to_broadcast([64, 64, 4]), hp_r, ALU.mult)
                                c.tt(t2v[:, ri], Bc[:, ri, :].unsqueeze(2).to_broadcast([64, 64, 4]), hp_o, ALU.mult, eng='pool')
                            for ri in range(2):
                                c.tt(t1v[:, ri], t1v[:, ri], t2v[:, ri], ALU.add)
                                c.tt(hv[:, ri, :, :, t], t1v[:, ri], bv[:, ri, :, :, t], ALU.add)
                        for ri, dst in enumerate((s5re_so, s5im_so)):
                            for b in range(4):
                                c.copy(hs[:, (b % 2) * 64:(b % 2 + 1) * 64], hv[:, ri, :, b, 3])
                                c.tr(pTo[0:64, (b % 2) * 64:(b % 2 + 1) * 64], hs[:, (b % 2) * 64:(b % 2 + 1) * 64], ident[0:64, 0:64])
                                so = stg_f.next()
                                c.copy(so[0:64, 0:64], pTo[0:64, (b % 2) * 64:(b % 2 + 1) * 64], eng='act')
                                c.dma('sp', dst[l, bh * 4 + b], so[0:64, 0:64])
                    pya, pyb = pY[pyi[0] % 2]
                    pyi[0] += 1
                    for jb in range(8):
                        py = pya if jb < 4 else pyb
                        oc = (jb % 4) * 128
                        for g in range(8):
                            gg = jb * 8 + g
                            oo = oc + g * 16
                            c.mm(py[0:SUB, oo:oo + 16], u_st[:, jb, q0:q0 + SUB], Dd[:, jb, g * 16:(g + 1) * 16], start=True, stop=False)
                            for ri in range(2):
                                c.mm(py[0:SUB, oo:oo + 16], hist[:, ri, gg, :], CT[:, ri, gg * 16:(gg + 1) * 16], start=False, stop=(ri == 1))
                    yg = yg_r.next()
                    c.act(yg[:, 0:512], pya[0:SUB, :], AF.Gelu)
                    c.act(yg[:, 512:1024], pyb[0:SUB, :], AF.Gelu)
                    tm_to_fm(yg, SUB, yk_fm[:, :, s0 + q0:s0 + q0 + SUB])
                    prev_hist = hist
                    nsub += 1
                if si == 7:
                    for ri, dst in enumerate((s5re_po, s5im_po)):
                        c.copy(hs[:, 0:64], prev_hist[:, ri, :, SUB - 1])
                        c.tr(pTo[0:64, 0:64], hs[:, 0:64], ident[0:64, 0:64])
                        so = stg_f.next()
                        c.copy(so[0:64, 0:64], pTo[0:64, 0:64], eng='act')
                        c.dma('sp', dst[l], so[0:64, 0:64])
            dump_fm("y_b0g", yk_fm)
            c.barrier()
            c.release(anames)
            bes.close()
            c.dma('sp', bgate[:, 0:8], IN("s5_glu_b_c")[l])
            for c0 in (0, 512):
                wb = load_w(IN("s5_glu_w")[l, :, c0:c0 + 512])
                for cb in range(4):
                    jb = c0 // 128 + cb
                    for (t0, n) in STS:
                        pb = pbank.next()
                        for kt in range(8):
                            c.mm(pb[:, 0:n], wb[:, kt, cb * 128:(cb + 1) * 128], yk_fm[:, kt, t0:t0 + n], start=(kt == 0), stop=(kt == 7))
                        sg = stg_b.next()
                        c.act(sg[:, 0:n], pb[:, 0:n], AF.Sigmoid, bias=bgate[:, jb:jb + 1])
                        zt = stg_b.next()
                        c.dma('sp', zt[:, 0:n], sfm[R_ZS5 + jb * 128:R_ZS5 + (jb + 1) * 128, t0:t0 + n])
                        c.tt(sg[:, 0:n], sg[:, 0:n], zt[:, 0:n], ALU.mult)
                        c.tt(sg[:, 0:n], sg[:, 0:n], yk_fm[:, jb, t0:t0 + n], ALU.mult)
                        c.dma('sp', sfm[R_U + jb * 128:R_U + (jb + 1) * 128, t0:t0 + n], sg[:, 0:n])
            c.dma('sp', bgate[:], IN("b_gate_c")[l])
            for jb in range(8):
                c.dma('sp', yk_fm[:, jb, :], sfm[R_U + jb * 128:R_U + (jb + 1) * 128, :])
            dump_fm("y_b", yk_fm)

        def mlstm_branch(l):
            bes = contextlib.ExitStack()
            uid = "_c%d" % l
            anames = []

            def A(name, shape, dt=F32):
                anames.append(name + uid)
                return bes.enter_context(nc.sbuf_tensor(name + uid, list(shape), dt))

            hp4 = A("hp4", [4, 8])
            gml = A("gml", [128, D])
            q_st = A("q_st", [128, 8, 512], BF16)
            k_st = A("k_st", [128, 8, 512], BF16)
            Cst = A("Cst", [128, 4, 2, 257])
            Cbf = A("Cbf", [128, 4, 2, 257], BF16)
            mprev = A("mprev", [4, 16])
            mprev_s = A("mprev_s", [4, 16])
            g4 = Rot([A("g4_%d" % i, [4, 128]) for i in range(14)])
            gtm_r = Rot([A("mgtm%d" % i, [128, 128]) for i in range(2)])
            vaug_r = Rot([A("vaug%d" % i, [128, 4, 257], BF16) for i in range(2)])
            qs_r = Rot([A("qs%d" % i, [128, 2, 128], BF16) for i in range(2)])
            kw_r = Rot([A("kw%d" % i, [128, 256], BF16) for i in range(2)])
            hc_r = Rot([A("hc%d" % i, [128, 256]) for i in range(2)])
            dec_sb = A("dec_sb", [128, 64])
            dg = A("dg", [4, 64])
            Cb_r = Rot([A("Cb%d" % i, [128, 2, 257]) for i in range(2)])
            Cbb_r = Rot([A("Cbb%d" % i, [128, 2, 257], BF16) for i in range(2)])
            qsb_r = Rot([A("qsb%d" % i, [128, 2, 64], BF16) for i in range(2)])
            for v in vaug_r.bufs:
                c.memset(v[:, :, 256:257], 1.0)
            c.dma('sp', hp4[:, 0:2], IN("ml_hp")[l])
            c.ts(hp4[:, 2:3], hp4[:, 1:2], -1.0, None, op0=ALU.mult)
            c.dma('sp', gml[:], IN("ml_norm")[l:l + 1, :].partition_broadcast(128))
            c.memset(Cst[:, :, :, :].rearrange("p a b c -> p (a b c)"), 0.0)
            c.memset(Cbf[:, :, :, :].rearrange("p a b c -> p (a b c)"), 0.0)
            c.memset(mprev[:, 0:1], 0.0)
            bi, nbf = hp4[:, 0:1], hp4[:, 2:3]
            pQK, pBC, pN0, pN1, pC0, pC1, pT, pBC2 = banks
            pn_r = Rot([pN0, pN1])
            for si, (s0, n) in enumerate(STS):
                sample = (si == 4)
                c.dma('sp', q_st[:, :, 0:n], sfm[R_Q:R_Q + 1024, s0:s0 + n].rearrange("(j p) t -> p j t", p=128))
                c.dma('sp', k_st[:, :, 0:n], sfm[R_K:R_K + 1024, s0:s0 + n].rearrange("(j p) t -> p j t", p=128))
                if sample:
                    c.dma('sp', mprev_s[:, :], IN("st_mlm")[l])
                for (t0, P) in [t for t in TILES if s0 <= t[0] < s0 + n]:
                    lo = t0 - s0
                    mbig = cmask['maskbig_s'] if sample else cmask['maskbig_p']
                    ir, fr, e1, lnp, bcn, a_, cm, mx, wi, ml_, enm, wen = [g4.next() for _ in range(12)]
                    c.dma('sp', ir[:, 0:P], g_scr[16:20, t0:t0 + P])
                    c.dma('sp', fr[:, 0:P], g_scr[20:24, t0:t0 + P])
                    c.act(e1[:, 0:P], fr[:, 0:P], AF.Exp, bias=nbf, scale=-1.0)
                    c.act(lnp[:, 0:P], e1[:, 0:P], AF.Ln, bias=1.0)
                    if not sample:
                        c.scan(bcn[:, 0:P], ones_f[0:4, 0:P], lnp[:, 0:P], 0.0, ALU.mult, ALU.add)
                        c.stt(a_[:, 0:P], ir[:, 0:P], bi, bcn[:, 0:P], ALU.add, ALU.add)
                        c.scan(cm[:, 0:P], a_[:, 0:P], a_[:, 0:P], -1e30, ALU.max, ALU.max)
                        c.ts(mx[:, 0:P], cm[:, 0:P], mprev[:, 0:1], None, op0=ALU.max)
                        c.act(wi[:, 0:P], mx[:, 0:P], AF.Exp, bias=mprev[:, 0:1], scale=-1.0)
                        c.tt(ml_[:, 0:P], mx[:, 0:P], bcn[:, 0:P], ALU.subtract)
                        c.act(enm[:, 0:P], ml_[:, 0:P], AF.Exp, scale=-1.0)
                        nml = g4.next()
                        c.ts(nml[:, 0:1], mx[:, P - 1:P], -1.0, None, op0=ALU.mult)
                        c.act(wen[:, 0:P], a_[:, 0:P], AF.Exp, bias=nml[:, 0:1])
                    else:
                        v3 = lambda tl: tl[:, 0:64].rearrange("p (b t) -> p b t", t=4)
                        c.copy(v3(bcn)[:, :, 0:1], v3(lnp)[:, :, 0:1])
                        for t in (1, 2, 3):
                            c.tt(v3(bcn)[:, :, t:t + 1], v3(bcn)[:, :, t - 1:t], v3(lnp)[:, :, t:t + 1], ALU.add)
                        c.stt(a_[:, 0:P], ir[:, 0:P], bi, bcn[:, 0:P], ALU.add, ALU.add)
                        c.copy(v3(cm)[:, :, 0:1], v3(a_)[:, :, 0:1])
                        for t in (1, 2, 3):
                            c.tt(v3(cm)[:, :, t:t + 1], v3(cm)[:, :, t - 1:t], v3(a_)[:, :, t:t + 1], ALU.max)
                        mpb = mprev_s[:, :].unsqueeze(2).to_broadcast([4, 16, 4])
                        c.tt(v3(mx), v3(cm), mpb, ALU.max)
                        c.tt(v3(wi), mpb, v3(mx), ALU.subtract)
                        c.act(wi[:, 0:P], wi[:, 0:P], AF.Exp)
                        c.tt(ml_[:, 0:P], mx[:, 0:P], bcn[:, 0:P], ALU.subtract)
                        c.act(enm[:, 0:P], ml_[:, 0:P], AF.Exp, scale=-1.0)
                        c.tt(v3(wen), v3(a_), v3(mx)[:, :, 3:4].to_broadcast([4, 16, 4]), ALU.subtract)
                        c.act(wen[:, 0:P], wen[:, 0:P], AF.Exp)
                    for qi, qsrc in enumerate((a_, wi, enm, wen)):
                        c.tr(pT[0:P, 32 * qi:32 * qi + 4], qsrc[:, 0:P], ident[0:4, 0:4])
                    gtm = gtm_r.next()
                    c.copy(gtm[0:P, :].rearrange("p (q x) -> p q x", x=32)[:, :, 0:4], pT[0:P, 0:128].rearrange("p (q x) -> p q x", x=32)[:, :, 0:4])
                    vaug = vaug_r.next()
                    c.dma('sp', vaug[0:P, :, 0:256], stm[t0:t0 + P, C_V:C_V + 1024].rearrange("t (h e) -> t h e", e=256))
                    ktm = b2.next()
                    c.dma('sp', ktm[0:P, :], stm[t0:t0 + P, C_KT:C_KT + 1024])
                    otm = b2.next()
                    c.dma('sp', otm[0:P, :], stm[t0:t0 + P, C_O:C_O + 1024])
                    ztm = b2.next()
                    c.dma('sp', ztm[0:P, :], stm[t0:t0 + P, C_ZML:C_ZML + 1024])
                    yc = b2.next()
                    if sample:
                        wv = v3(wi)[:, :, 3:4]
                        dgv = dg[:, :].rearrange("p (b h) -> p b h", h=4)
                        c.tt(dgv, wv.to_broadcast([4, 16, 4]), ident[0:4, 0:4].unsqueeze(1).to_broadcast([4, 16, 4]), ALU.mult)
                        c.mm(pBC2[:, 0:64], ones_f[0:4, :], dg[:, :])
                        c.copy(dec_sb[:, :], pBC2[:, 0:64])
                    for hh in range(4):
                        for j in range(2):
                            c.mm(pQK[0:P, 0:P], k_st[:, 2 * hh + j, lo:lo + P], q_st[:, 2 * hh + j, lo:lo + P], start=(j == 0), stop=(j == 1))
                        selm = g4.next()
                        c.ts(selm[:, 0:P], mx[:, 0:P], ident[0:4, hh:hh + 1], None, op0=ALU.mult)
                        c.mm(pBC[0:P, 0:P], ones_f[0:4, 0:P], selm[:, 0:P])
                        tsb = fq.next()
                        c.stt(tsb[0:P, 0:P], pBC[0:P, 0:P], gtm[0:P, hh:hh + 1], mbig[0:P, 0:P], ALU.subtract, ALU.max)
                        E = fq.next()
                        c.act(E[0:P, 0:P], tsb[0:P, 0:P], AF.Exp, scale=-1.0)
                        WT = bq.next()
                        c.tt(WT[0:P, 0:P], E[0:P, 0:P], pQK[0:P, 0:P], ALU.mult)
                        selw = g4.next()
                        c.ts(selw[:, 0:P], wi[:, 0:P], ident[0:4, hh:hh + 1], None, op0=ALU.mult)
                        c.mm(pBC2[:, 0:P], ones_f[0:4, :], selw[:, 0:P])
                        qs = qs_r.next()
                        for j in range(2):
                            c.tt(qs[:, j, 0:P], q_st[:, 2 * hh + j, lo:lo + P], pBC2[:, 0:P], ALU.mult)
                        pn = pn_r.next()
                        c.mm(pn[0:P, 0:257], WT[0:P, 0:P], vaug[0:P, hh, :], start=True, stop=False)
                        if not sample:
                            for j in range(2):
                                c.mm(pn[0:P, 0:257], qs[:, j, 0:P], Cbf[:, hh, j, :], start=False, stop=(j == 1))
                        else:
                            kw = kw_r.next()
                            for b in range(NB):
                                Cb = Cb_r.next()
                                Cbb = Cbb_r.next()
                                src_c = IN("st_mlc")[l, b, hh * 256:(hh + 1) * 256, :].rearrange("(j p) e -> p j e", p=128)
                                src_n = IN("st_mln")[l, b, hh * 256:(hh + 1) * 256].rearrange("(j p o) -> p j o", p=128, o=1)
                                c.dma('sp', Cb[:, :, 0:256], src_c)
                                c.dma('sp', Cb[:, :, 256:257], src_n, allow_slow_non_contiguous=True)
                                c.copy(Cbb[:, :, :], Cb[:, :, :], eng='act')
                                qsb = qsb_r.next()
                                c.tt(qsb[:, :, :], qs[:, :, 0:64], colmask[:, b * 64:(b + 1) * 64].unsqueeze(1).to_broadcast([128, 2, 64]), ALU.mult)
                                for j in range(2):
                                    c.mm(pn[0:P, 0:257], qsb[:, j, :], Cbb[:, j, :], start=False, stop=(b == NB - 1 and j == 1))
                                c.ts(kw[0:64, :], ktm[0:64, hh * 256:(hh + 1) * 256], gtm[0:64, 96 + hh:97 + hh], cmask['rowmask'][0:64, b:b + 1], op0=ALU.mult, op1=ALU.mult)
                                for j, pc in enumerate((pC0, pC1)):
                                    c.mm(pc[:, 0:257], kw[0:64, j * 128:(j + 1) * 128], vaug[0:64, hh, :])
                                for j, pc in enumerate((pC0, pC1)):
                                    c.stt(Cb[:, j, :], Cb[:, j, :], dec_sb[:, b * 4 + hh:b * 4 + hh + 1], pc[:, 0:257], ALU.mult, ALU.add)
                                c.dma('sp', mc_so[l, b, hh * 256:(hh + 1) * 256, :].rearrange("(j p) e -> p j e", p=128), Cb[:, :, 0:256])
                                c.dma('sp', mn_so[l, b, hh * 256:(hh + 1) * 256].rearrange("(j p o) -> p j o", p=128, o=1), Cb[:, :, 256:257], allow_slow_non_contiguous=True)
                        st = st_small.next()
                        c.act(st[0:P, 2:3], pn[0:P, 256:257], AF.Abs)
                        c.ts(st[0:P, 0:1], st[0:P, 2:3], gtm[0:P, 64 + hh:65 + hh], None, op0=ALU.max)
                        c.recip(st[0:P, 1:2], st[0:P, 0:1])
                        hc = hc_r.next()
                        c.stt(hc[0:P, :], pn[0:P, 0:256], st[0:P, 1:2], otm[0:P, hh * 256:(hh + 1) * 256], ALU.mult, ALU.mult)
                        rstd = rms_rstd(hc[0:P, :], P, n=256)
                        c.stt(hc[0:P, :], hc[0:P, :], rstd, gml[0:P, hh * 256:(hh + 1) * 256], ALU.mult, ALU.mult)
                        c.tt(yc[0:P, hh * 256:(hh + 1) * 256], hc[0:P, :], ztm[0:P, hh * 256:(hh + 1) * 256], ALU.mult)
                    tm_to_fm(yc, P, yk_fm[:, :, t0:t0 + P])
                    if not sample:
                        c.ts(dg[:, 0:4], ident[0:4, 0:4], wi[:, P - 1:P], None, op0=ALU.mult)
                        c.mm(pBC2[:, 0:4], ones_f[0:4, :], dg[:, 0:4])
                        c.copy(dec_sb[:, 0:4], pBC2[:, 0:4])
                        for hh in range(4):
                            kw = kw_r.next()
                            c.ts(kw[0:P, :], ktm[0:P, hh * 256:(hh + 1) * 256], gtm[0:P, 96 + hh:97 + hh], None, op0=ALU.mult)
                            for j, pc in enumerate((pC0, pC1)):
                                c.mm(pc[:, 0:257], kw[0:P, j * 128:(j + 1) * 128], vaug[0:P, hh, :])
                            for j, pc in enumerate((pC0, pC1)):
                                c.stt(Cst[:, hh, j, :], Cst[:, hh, j, :], dec_sb[:, hh:hh + 1], pc[:, 0:257], ALU.mult, ALU.add)
                        c.copy(Cbf[:, :, :, :].rearrange("p a b c -> p (a b c)"), Cst[:, :, :, :].rearrange("p a b c -> p (a b c)"), eng='act')
                        c.copy(mprev[:, 0:1], ml_[:, P - 1:P])
                    else:
                        mlast = g4.next()
                        c.copy(mlast[:, 0:16].unsqueeze(2), v3(ml_)[:, :, 3:4])
                        c.dma('sp', mm_so[l], mlast[:, 0:16])
            for hh in range(4):
                c.dma('sp', mc_po[l, hh * 256:(hh + 1) * 256, :].rearrange("(j p) e -> p j e", p=128), Cst[:, hh, :, 0:256])
                c.dma('sp', mn_po[l, hh * 256:(hh + 1) * 256].rearrange("(j p o) -> p j o", p=128, o=1), Cst[:, hh, :, 256:257], allow_slow_non_contiguous=True)
            c.dma('sp', mm_po[l].rearrange("(h o) -> h o", o=1), mprev[:, 0:1])
            dump_fm("y_c", yk_fm)
            c.barrier()
            c.release(anames)
            bes.close()

        def xattn_branch(l):
            bes = contextlib.ExitStack()
            uid = "_d%d" % l
            anames = []

            def A(name, shape, dt=F32):
                anames.append(name + uid)
                return bes.enter_context(nc.sbuf_tensor(name + uid, list(shape), dt))

            mk_fm = A("mk_fm", [128, 8, 256], BF16)
            mv_tm = A("mv_tm", [128, 2, D], BF16)
            memkv(l, mk_fm, mv_tm)
            qx_st = A("qx_st", [128, 8, 512], BF16)
            p_r = Rot([A("pp%d" % i, [128, 256], BF16) for i in range(2)])
            pT_r = Rot([A("ppT%d" % i, [128, 2, 128], BF16) for i in range(5)])
            kl_r = Rot([A("kl%d" % i, [128, 2, D]) for i in range(2)])
            mkT_r = Rot([A("mkT%d" % i, [128, 8, 256], BF16) for i in range(2)])
            mvb_r = Rot([A("mvb%d" % i, [128, 2, D], BF16) for i in range(2)])
            qm_r = Rot([A("qm%d" % i, [128, 8, 64], BF16) for i in range(2)])
            pTm_r = Rot([A("pTm%d" % i, [128, 2, 64], BF16) for i in range(3)])
            pS = banks[0:4]
            pTr = banks[4:6]
            pV = Rot(banks[6:8])

            def softmax_T(sc, P):
                st = st_small.next()
                c.reduce(st[0:P, 0:1], sc[0:P, 0:256], ALU.max)
                c.ts(st[0:P, 1:2], st[0:P, 0:1], -1.0, None, op0=ALU.mult)
                p = p_r.next()
                c.act(p[0:P, :], sc[0:P, 0:256], AF.Exp, bias=st[0:P, 1:2], accum_out=st[0:P, 2:3])
                c.recip(st[0:P, 3:4], st[0:P, 2:3])
                pv = bfv(pTr[0])
                for mt in range(2):
                    c.tr(pv[:, mt * 128:mt * 128 + P], p[0:P, mt * 128:(mt + 1) * 128], ident_b[0:P, 0:P])
                pT = pT_r.next()
                c.copy(pT[:, :, 0:P], pv[:, 0:256].rearrange("p (m t) -> p m t", t=128)[:, :, 0:P], eng='act')
                return pT, st[0:P, 3:4]

            for si, (s0, n) in enumerate(STS):
                sample = (si == 4)
                c.dma('sp', qx_st[:, :, 0:n], sfm[R_QXA:R_QXA + 1024, s0:s0 + n].rearrange("(j p) t -> p j t", p=128))
                for (t0, P) in [t for t in TILES if s0 <= t[0] < s0 + n]:
                    lo = t0 - s0
                    ztm = b2.next()
                    c.dma('sp', ztm[0:P, :], stm[t0:t0 + P, C_ZXA:C_ZXA + 1024])
                    yd = b2.next()
                    if not sample:
                        for hh in range(4):
                            sc = pS[hh % 2]
                            for j in range(2):
                                c.mm(sc[0:P, 0:256], qx_st[:, 2 * hh + j, lo:lo + P], mk_fm[:, 2 * hh + j, :], start=(j == 0), stop=(j == 1))
                            pT, rinv = softmax_T(sc, P)
                            pvb = pV.next()
                            for mt in range(2):
                                c.mm(pvb[0:P, 0:256], pT[:, mt, 0:P], mv_tm[:, mt, hh * 256:(hh + 1) * 256], start=(mt == 0), stop=(mt == 1))
                            c.stt(yd[0:P, hh * 256:(hh + 1) * 256], pvb[0:P, 0:256], rinv, ztm[0:P, hh * 256:(hh + 1) * 256], ALU.mult, ALU.mult)
                    else:
                        for b in range(NB):
                            kl = kl_r.next()
                            c.dma('sp', kl[:, :, :], IN("ck")[l, b].rearrange("(mt p) d -> p mt d", p=128))
                            mkT = mkT_r.next()
                            for rnd in range(2):
                                for q4 in range(8):
                                    blk = rnd * 4 + q4 // 2
                                    mt = q4 % 2
                                    c.tr(pTr[q4 // 4][:, (q4 % 4) * 128:(q4 % 4 + 1) * 128], kl[:, mt, blk * 128:(blk + 1) * 128], ident[:, :])
                                for hf in range(2):
                                    blks = mkT[:, rnd * 4 + 2 * hf:rnd * 4 + 2 * hf + 2, :]
                                    src = pTr[hf][:, :].rearrange("p (b m) -> p b m", m=256)
                                    if hf == 0:
                                        c.act(blks, src, AF.Copy, scale=1.0 / 16)
                                    else:
                                        c.ts(blks, src, 1.0 / 16, None, op0=ALU.mult)
                            qm = qm_r.next()
                            c.tt(qm[:, :, :], qx_st[:, :, 0:64], colmask[:, b * 64:(b + 1) * 64].unsqueeze(1).to_broadcast([128, 8, 64]), ALU.mult)
                            for hh in range(4):
                                for j in range(2):
                                    c.mm(pS[hh][0:64, 0:256], qm[:, 2 * hh + j, :], mkT[:, 2 * hh + j, :],
                                         start=(b == 0 and j == 0), stop=(b == NB - 1 and j == 1))
                        pTs, rinvs = [], []
                        for hh in range(4):
                            pT, rinv = softmax_T(pS[hh], 64)
                            pTs.append(pT)
                            rinvs.append(rinv)
                        for b in range(NB):
                            mvb = mvb_r.next()
                            c.dma('pool', mvb[:, :, :], IN("cv")[l, b].rearrange("(mt p) d -> p mt d", p=128))
                            for hh in range(4):
                                pTm = pTm_r.next()
                                c.tt(pTm[:, :, :], pTs[hh][:, :, 0:64], colmask[:, b * 64:(b + 1) * 64].unsqueeze(1).to_broadcast([128, 2, 64]), ALU.mult)
                                for mt in range(2):
                                    c.mm(pS[hh][0:64, 0:256], pTm[:, mt, :], mvb[:, mt, hh * 256:(hh + 1) * 256],
                                         start=(b == 0 and mt == 0), stop=(b == NB - 1 and mt == 1))
                        for hh in range(4):
                            c.stt(yd[0:64, hh * 256:(hh + 1) * 256], pS[hh][0:64, 0:256], rinvs[hh], ztm[0:64, hh * 256:(hh + 1) * 256], ALU.mult, ALU.mult)
                    tm_to_fm(yd, P, yk_fm[:, :, t0:t0 + P])
            dump_fm("y_d", yk_fm)
            c.barrier()
            c.release(anames)
            bes.close()

        merged = h_fm

        def merge_branch(l, k, first):
            for c0 in (0, 512):
                wb = load_w(IN("w_down")[l, k, :, c0:c0 + 512])
                for cb in range(4):
                    jb = c0 // 128 + cb
                    for (t0, n) in STS:
                        pb = pbank.next()
                        for kt in range(8):
                            c.mm(pb[:, 0:n], wb[:, kt, cb * 128:(cb + 1) * 128], yk_fm[:, kt, t0:t0 + n], start=(kt == 0), stop=(kt == 7))
                        gt = stg_b.next()
                        r0 = R_GATE + k * 1024 + jb * 128
                        c.dma('sp', gt[:, 0:n], sfm[r0:r0 + 128, t0:t0 + n])
                        if first:
                            c.tt(merged[:, jb, t0:t0 + n], gt[:, 0:n], pb[:, 0:n], ALU.mult)
                        else:
                            tmp = stg_f.next()
                            c.tt(tmp[:, 0:n], gt[:, 0:n], pb[:, 0:n], ALU.mult)
                            c.tt(merged[:, jb, t0:t0 + n], merged[:, jb, t0:t0 + n], tmp[:, 0:n], ALU.add, eng='pool')

        def outproj(l):
            wos = [load_w(IN("w_out")[l, :, c0:c0 + 512]) for c0 in (0, 512)]
            for tt, (t0, P) in enumerate(TILES):
                xt = f4.next()
                c.dma('sp', xt[0:P, :], x_src(l, tt))
                for half in range(2):
                    pb = pbank.next()
                    for kt in range(8):
                        c.mm(pb[0:P, :], merged[:, kt, t0:t0 + P], wos[half][:, kt, :], start=(kt == 0), stop=(kt == 7))
                    c.tt(xt[0:P, half * 512:(half + 1) * 512], xt[0:P, half * 512:(half + 1) * 512], pb[0:P, :], ALU.add)
                c.dma('sp', x_scr[t0:t0 + P, :], xt[0:P, :])


        for l in range(depth):
            phase0(l)
            projections(l)
            if stage < 5 and l == 0 and 'memkv' not in SKIP:
                memkv(l, c.sb("mk_fm_t", [128, 8, 256], BF16), c.sb("mv_tm_t", [128, 2, D], BF16))
            full = stage >= 6
            first = True
            for (stg, k, fn) in ((2, 0, ssd_branch), (3, 1, s5_branch), (4, 2, mlstm_branch), (5, 3, xattn_branch)):
                if stage >= stg and ("br%d" % k) not in SKIP:
                    fn(l)
                    if full:
                        merge_branch(l, k, first)
                        first = False
            if full:
                dump_fm("merged", merged)
                outproj(l)

        gfin = c.sb("gfin", [128, D], F32)
        c.dma('sp', gfin[:], IN("final_norm").partition_broadcast(128))
        for tt, (t0, P) in enumerate(TILES):
            xt = f4.next()
            c.dma('sp', xt[0:P, :], x_scr[t0:t0 + P, :] if stage >= 6 else x_src(0, tt))
            rstd = rms_rstd(xt[0:P, :], P)
            c.stt(xt[0:P, :], xt[0:P, :], rstd, gfin[0:P, :], ALU.mult, ALU.mult)
            if tt < 16:
                c.dma('sp', y_p[t0:t0 + P, :], xt[0:P, :])
            else:
                c.dma('sp', y_s[:, :], xt[0:P, :])
        c.finish()
        print("[build] instructions=%d waits=%d per-engine=%s dma_sems=%d sbuf_left=%s" % (
            c.n_inst, c.n_wait, c.cnt, len(c.dma_sems), nc.sbuf_bytes_remaining() if callable(nc.sbuf_bytes_remaining) else nc.sbuf_bytes_remaining))
    return nc


def make_in_maps(inp, used=None):
    cs = make_consts()
    f32 = lambda a: np.ascontiguousarray(a, np.float32)
    shared = {
        "w_kv": lambda: f32(inp["w_mem_kv"]),
        "w_down": lambda: f32(inp["w_down"]),
        "w_out": lambda: f32(inp["w_out"]),
        "norm_in_c": lambda: colmaj(inp["norm_in"]),
        "mem_norm_c": lambda: colmaj(inp["mem_norm"]),
        "final_norm": lambda: f32(inp["final_norm"]).reshape(1, D),
        "b_gate_c": lambda: colmaj(inp["b_gate"]),
        "c_sel16": lambda: cs['sel16'],
        "c_colmask": lambda: cs['colmask'],
        "conv_w_c": lambda: f32(np.transpose(f32(inp["ssd_conv_w"]).reshape(DEPTH, 4, 12, 128), (0, 3, 2, 1)).reshape(DEPTH, 128, 48)),
        "conv_b_c": lambda: colmaj(inp["ssd_conv_b"]),
        "ssd_hp": lambda: f32(np.stack([f32(inp["ssd_dt_bias"]), f32(inp["ssd_a_log"])], axis=-1)),
        "ssd_d": lambda: f32(inp["ssd_d"]),
        "ssd_norm": lambda: f32(inp["ssd_norm"]),
        "ml_hp": lambda: f32(np.stack([f32(inp["b_igate"]), f32(inp["b_fgate"])], axis=-1)),
        "ml_norm": lambda: f32(inp["ml_norm"]).reshape(DEPTH, D),
        "s5_a_re_t": lambda: f32(np.transpose(f32(inp["s5_a_re"]), (0, 2, 1))),
        "s5_a_im_t": lambda: f32(np.transpose(f32(inp["s5_a_im"]), (0, 2, 1))),
        "s5_log_dt": lambda: f32(inp["s5_log_dt"]),
        "s5_b_re_t": lambda: f32(np.transpose(f32(inp["s5_b_re"]), (0, 2, 1, 3))).reshape(DEPTH, 64, 1024),
        "s5_b_im_t": lambda: f32(np.transpose(f32(inp["s5_b_im"]), (0, 2, 1, 3))).reshape(DEPTH, 64, 1024),
        "s5_c_re_t": lambda: f32(np.transpose(f32(inp["s5_c_re"]), (0, 3, 1, 2))).reshape(DEPTH, 64, 1024),
        "s5_c_im_t": lambda: f32(np.transpose(f32(inp["s5_c_im"]), (0, 3, 1, 2))).reshape(DEPTH, 64, 1024),
        "s5_d_c": lambda: colmaj(inp["s5_d"]),
        "s5_glu_w": lambda: f32(inp["s5_glu_w"]),
        "s5_glu_b_c": lambda: colmaj(inp["s5_glu_b"]),
    }
    for k in CONST_ORDER:
        shared["c_" + k] = (lambda k=k: cs[k])
    for i in range(DEPTH):
        shared["w_in%d" % i] = (lambda i=i: f32(inp["w_in"][i]))
    percore = {
        "x_p": lambda ci, b0: f32(inp["x_prompt"][ci]),
        "x_s": lambda ci, b0: f32(inp["x_sample"][b0:b0 + NB]).reshape(TS, D),
        "mem": lambda ci, b0: f32(inp["mem_prompt"][ci]),
        "ck": lambda ci, b0: f32(inp["cache_mem_k"][:, b0:b0 + NB]).reshape(DEPTH, NB, 256, D),
        "cv": lambda ci, b0: f32(inp["cache_mem_v"][:, b0:b0 + NB]).reshape(DEPTH, NB, 256, D),
        "st_conv": lambda ci, b0: f32(inp["state_ssd_conv"][:, b0:b0 + NB]).reshape(DEPTH, NB * 3, 1536),
        "st_ssd": lambda ci, b0: f32(inp["state_ssd"][:, b0:b0 + NB]).reshape(DEPTH, NB, 1024, 128),
        "st_mlc": lambda ci, b0: f32(inp["state_mlstm_c"][:, b0:b0 + NB]).reshape(DEPTH, NB, 1024, 256),
        "st_mln": lambda ci, b0: f32(inp["state_mlstm_n"][:, b0:b0 + NB]).reshape(DEPTH, NB, 1024),
        "st_mlm": lambda ci, b0: f32(np.transpose(f32(inp["state_mlstm_m"][:, b0:b0 + NB]), (0, 2, 1))),
        "st_s5_t": lambda ci, b0: f32(np.transpose(np.stack([f32(inp["state_s5_re"][:, b0:b0 + NB]), f32(inp["state_s5_im"][:, b0:b0 + NB])], axis=1), (0, 4, 1, 3, 2))),
    }
    names = set(shared) | set(percore)
    if used is not None:
        names &= set(used)
    sh = {k: shared[k]() for k in names if k in shared}
    maps = []
    for ci in range(NCORES):
        m = dict(sh)
        for k in names:
            if k in percore:
                m[k] = percore[k](ci, ci * NB)
        maps.append(m)
    return maps


_NC_CACHE = {}


def run(inp, depth=DEPTH, stage=99, dbg=False):
    key = (depth, stage, dbg)
    if key not in _NC_CACHE:
        _NC_CACHE[key] = build(depth, stage, dbg)
    nc = _NC_CACHE[key]
    maps = make_in_maps(inp, used=set(nc._used_inputs.keys()))
    res = run_bass_kernel_spmd(nc, maps, core_ids=list(range(NCORES)))
    return res.results


def kernel(**inputs):
    r = run(inputs)
    L = DEPTH
    pst = lambda k, shp: np.stack([np.asarray(x[k], np.float32).reshape((L,) + shp) for x in r], axis=1)
    sst = lambda k, shp: np.concatenate([np.asarray(x[k], np.float32).reshape((L, NB) + shp) for x in r], axis=1)
    y_prompt = np.stack([x["y_p"] for x in r], 0)
    y_sample = np.concatenate([x["y_s"].reshape(NB, 4, D) for x in r], 0)
    mm_s = np.concatenate([np.transpose(np.asarray(x["mm_so"], np.float32), (0, 2, 1)) for x in r], axis=1)
    return (y_prompt, y_sample,
            pst("mk_o", (256, 4, 256)), pst("mv_o", (256, 4, 256)),
            pst("conv_po", (3, 1536)), pst("ssd_po", (16, 64, 128)),
            pst("s5re_po", (64, 64)), pst("s5im_po", (64, 64)),
            pst("mc_po", (4, 256, 256)), pst("mn_po", (4, 256)), pst("mm_po", (4,)),
            sst("conv_so", (3, 1536)), sst("ssd_so", (16, 64, 128)),
            sst("s5re_so", (64, 64)), sst("s5im_so", (64, 64)),
            sst("mc_so", (4, 256, 256)), sst("mn_so", (4, 256)), mm_s)
```

```python
import contextlib
import numpy as np
import concourse.bass as bass
import concourse.mybir as mybir
from concourse.bass_utils import run_bass_kernel_spmd

F32 = mybir.dt.float32
BF16 = mybir.dt.bfloat16
AF = mybir.ActivationFunctionType
ALU = mybir.AluOpType
AX = mybir.AxisListType

import os
SKIP = os.environ.get('K_SKIP', '').split(',')
NCORES = 8
D = 1024
DEPTH = 4
TP = 2048
TS = 64
NT = TP + TS
NB = 16
EPS = 1e-6
D_IN = 15896
O_ZSSD, O_XBC, O_DT, O_U, O_ZS5, O_Q, O_K, O_V, O_I, O_F, O_O, O_ZML, O_QXA, O_ZXA, O_GATE = (
    0, 1024, 2560, 2576, 3600, 4624, 5648, 6672, 7696, 7700, 7704, 8728, 9752, 10776, 11800)
TILES = [(128 * i, 128) for i in range(16)] + [(2048, 64)]
STS = [(512 * i, 512) for i in range(4)] + [(2048, 64)]
R_XBC, R_U, R_Q, R_K, R_QXA, R_ZS5, R_GATE = 0, 1536, 2560, 3584, 4608, 5632, 6656
NFM = 6656 + 4096
C_ZSSD, C_V, C_KT, C_O, C_ZML, C_ZXA = 0, 1024, 2048, 3072, 4096, 5120
NTM = 6144


class S:
    def __init__(self, ap, sub):
        self.ap = ap
        self.sub = sub


def _unw(a):
    if isinstance(a, S):
        return a.ap, a.sub
    return a, None


class Ctx:
    def __init__(self, nc):
        self.nc = nc
        self.es = contextlib.ExitStack()
        self.eng = {'pe': nc.tensor, 'act': nc.scalar, 'dve': nc.vector, 'pool': nc.gpsimd, 'sp': nc.sync}
        self.sem = {}
        self.cnt = {}
        self.semobj = {}
        for e in ('pe', 'act', 'dve', 'pool'):
            self.sem[e] = self.es.enter_context(nc.semaphore("sem_" + e))
            self.cnt[e] = 0
            self.semobj["sem_" + e] = self.sem[e]
        self.know = {e: {} for e in self.eng}
        self.reg = {}
        self.vcs = {}
        self.dma_sems = {}
        self.n_wait = 0
        self.n_inst = 0

    def sb(self, name, shape, dt=F32):
        return self.es.enter_context(self.nc.sbuf_tensor(name, list(shape), dt))

    def ps(self, name, shape, dt=F32):
        return self.es.enter_context(self.nc.psum_tensor(name, list(shape), dt))

    def dsem(self, name):
        if name not in self.dma_sems:
            if getattr(self, "free_sems", None):
                s, base = self.free_sems.pop()
            else:
                s, base = self.es.enter_context(self.nc.semaphore("d_%d" % len(self.semobj))), 0
            self.dma_sems[name] = [s, base]
            self.semobj["d_" + name] = s
        return self.dma_sems[name]

    def release(self, names):
        if not hasattr(self, "free_sems"):
            self.free_sems = []
        for n in names:
            for nm in (n, n + "_sw"):
                if nm in self.dma_sems:
                    s, v = self.dma_sems.pop(nm)
                    self.free_sems.append((s, v))

    def _deps(self, name, sub, is_write, eng=None):
        st = self.reg.get(name)
        if not st:
            return []
        keys = list(st.keys()) if sub is None else [k for k in (sub, None) if k in st]
        toks = []
        psum = name.startswith("pb")
        for k in keys:
            w, r = st[k]
            if w is not None:
                toks.append(w)
            if is_write:
                toks.extend(r.items())
            elif psum:
                toks.extend((s, v) for (s, v) in r.items() if s != "sem_" + str(eng))
        return toks

    def _record(self, name, sub, is_write, tok):
        st = self.reg.setdefault(name, {})
        if is_write:
            if sub is None:
                st.clear()
            st[sub] = [tok, {}]
        else:
            ent = st.setdefault(sub, [None, {}])
            s, v = tok
            if ent[1].get(s, -1) < v:
                ent[1][s] = v

    def _sync(self, e, reads, writes):
        toks = []
        for (n, s) in reads:
            toks += self._deps(n, s, False, e)
        for (n, s) in writes:
            toks += self._deps(n, s, True)
        need = {}
        kn = self.know[e]
        for (sname, v) in toks:
            if e == 'pe' and sname == 'sem_pe':
                continue
            if kn.get(sname, 0) >= v:
                continue
            if need.get(sname, 0) < v:
                need[sname] = v
        for sname, v in need.items():
            if sname.startswith("d_"):
                v = max(v, self.dma_sems[sname[2:]][1])
            if kn.get(sname, 0) >= v:
                continue
            self.eng[e].wait_ge(self.semobj[sname], v)
            self.n_wait += 1
            kn[sname] = v
            vc = self.vcs.get((sname, v))
            if vc:
                for k2, v2 in vc.items():
                    if kn.get(k2, 0) < v2:
                        kn[k2] = v2

    def _keys(self, aps):
        out = []
        for a in aps:
            if a is None:
                continue
            ap, sub = _unw(a)
            if not hasattr(ap, 'tensor'):
                continue
            out.append((ap.tensor.name, sub))
        return out

    def op(self, e, fn, reads, writes):
        rk = self._keys(reads)
        wk = self._keys(writes)
        self._sync(e, rk, wk)
        inst = fn(self.eng[e])
        self.cnt[e] += 1
        inst.then_inc(self.sem[e], 1)
        self.n_inst += 1
        sname = "sem_" + e
        tok = (sname, self.cnt[e])
        vc = dict(self.know[e])
        vc[sname] = self.cnt[e]
        self.vcs[tok] = vc
        for (n, s) in rk:
            self._record(n, s, False, tok)
        for (n, s) in wk:
            self._record(n, s, True, tok)
        return tok

    def dma(self, q, out, in_, slot=None, **kw):
        o, _ = _unw(out)
        i, _ = _unw(in_)
        rk = self._keys([in_])
        wk = self._keys([out])
        self._sync(q, rk, wk)
        if slot is None:
            slot = o.tensor.name if 'dram' not in str(type(o.tensor)).lower() else i.tensor.name
        if q == 'pool':
            slot = slot + "_sw"
        sem = self.dsem(slot)
        inst = self.eng[q].dma_start(out=o, in_=i, **kw)
        sem[1] += 16
        inst.then_inc(sem[0], 16)
        self.n_inst += 1
        tok = ("d_" + slot, sem[1])
        self.vcs[tok] = dict(self.know[q])
        for (n, s) in rk:
            self._record(n, s, False, tok)
        for (n, s) in wk:
            self._record(n, s, True, tok)
        return tok

    def barrier(self):
        for e in ('pe', 'act', 'dve', 'pool', 'sp'):
            kn = self.know[e]
            for f in ('pe', 'act', 'dve', 'pool'):
                if f != e and self.cnt[f] > kn.get('sem_' + f, 0):
                    self.eng[e].wait_ge(self.sem[f], self.cnt[f])
                    kn['sem_' + f] = self.cnt[f]
                    self.n_wait += 1
            for name, (s, v) in self.dma_sems.items():
                if v > kn.get('d_' + name, 0):
                    self.eng[e].wait_ge(s, v)
                    kn['d_' + name] = v
                    self.n_wait += 1
        for e in ('act', 'dve', 'pool'):
            if self.cnt[e] > self.know[e].get('sem_' + e, 0):
                self.eng[e].wait_ge(self.sem[e], self.cnt[e])
                self.know[e]['sem_' + e] = self.cnt[e]
                self.n_wait += 1

    def finish(self):
        for e in ('pe', 'act', 'dve', 'pool'):
            if self.cnt[e]:
                self.eng['sp'].wait_ge(self.sem[e], self.cnt[e])
        for name, (s, v) in self.dma_sems.items():
            if v:
                self.eng['sp'].wait_ge(s, v)

    def mm(self, out, lhsT, rhs, start=True, stop=True, **kw):
        o, l, r = _unw(out)[0], _unw(lhsT)[0], _unw(rhs)[0]
        return self.op('pe', lambda e: e.matmul(o, lhsT=l, rhs=r, start=start, stop=stop, **kw), [lhsT, rhs], [out])

    def tr(self, out, in_, ident):
        o, i, d = _unw(out)[0], _unw(in_)[0], _unw(ident)[0]
        return self.op('pe', lambda e: e.transpose(o, i, d), [in_, ident], [out])

    def act(self, out, in_, func, bias=None, scale=None, accum_out=None):
        o, i = _unw(out)[0], _unw(in_)[0]
        kw = {}
        rd = [in_]
        if bias is not None:
            kw['bias'] = _unw(bias)[0]
            rd.append(bias)
        if scale is not None:
            kw['scale'] = _unw(scale)[0]
            rd.append(scale)
        wr = [out]
        if accum_out is not None:
            kw['accum_out'] = _unw(accum_out)[0]
            wr.append(accum_out)
        return self.op('act', lambda e: e.activation(out=o, in_=i, func=func, **kw), rd, wr)

    def tt(self, out, in0, in1, op, eng='dve'):
        o, a, b = _unw(out)[0], _unw(in0)[0], _unw(in1)[0]
        return self.op(eng, lambda e: e.tensor_tensor(out=o, in0=a, in1=b, op=op), [in0, in1], [out])

    def ts(self, out, in0, s1, s2=None, op0=ALU.mult, op1=None, eng='dve'):
        o, a = _unw(out)[0], _unw(in0)[0]
        rd = [in0]
        s1v = _unw(s1)[0]
        if hasattr(s1v, 'tensor'):
            rd.append(s1)
        s2v = _unw(s2)[0] if s2 is not None else None
        if s2v is not None and hasattr(s2v, 'tensor'):
            rd.append(s2)
        kw = {}
        if op1 is not None:
            kw['op1'] = op1
        return self.op(eng, lambda e: e.tensor_scalar(out=o, in0=a, scalar1=s1v, scalar2=s2v, op0=op0, **kw), rd, [out])

    def stt(self, out, in0, scalar, in1, op0, op1):
        o, a, b = _unw(out)[0], _unw(in0)[0], _unw(in1)[0]
        sv = _unw(scalar)[0]
        rd = [in0, in1]
        if hasattr(sv, 'tensor'):
            rd.append(scalar)
        return self.op('dve', lambda e: e.scalar_tensor_tensor(out=o, in0=a, scalar=sv, in1=b, op0=op0, op1=op1), rd, [out])

    def copy(self, out, in_, eng='dve'):
        o, i = _unw(out)[0], _unw(in_)[0]
        if eng == 'act':
            return self.op('act', lambda e: e.copy(out=o, in_=i), [in_], [out])
        return self.op(eng, lambda e: e.tensor_copy(out=o, in_=i), [in_], [out])

    def memset(self, out, val, eng='dve'):
        o = _unw(out)[0]
        return self.op(eng, lambda e: e.memset(o, val), [], [out])

    def scan(self, out, d0, d1, initial, op0, op1):
        o, a, b = _unw(out)[0], _unw(d0)[0], _unw(d1)[0]
        iv = _unw(initial)[0]
        rd = [d0, d1]
        if hasattr(iv, 'tensor'):
            rd.append(initial)
        return self.op('dve', lambda e: e.tensor_tensor_scan(out=o, data0=a, data1=b, initial=iv, op0=op0, op1=op1), rd, [out])

    def reduce(self, out, in_, op, axis=AX.X):
        o, i = _unw(out)[0], _unw(in_)[0]
        return self.op('dve', lambda e: e.tensor_reduce(out=o, in_=i, axis=axis, op=op), [in_], [out])

    def recip(self, out, in_):
        o, i = _unw(out)[0], _unw(in_)[0]
        return self.op('dve', lambda e: e.reciprocal(out=o, in_=i), [in_], [out])


class Rot:
    def __init__(self, bufs):
        self.bufs = bufs
        self.i = 0

    def next(self):
        b = self.bufs[self.i % len(self.bufs)]
        self.i += 1
        return b


def colmaj(v):
    v = np.asarray(v, np.float32)
    j = v.shape[-1] // 128
    return np.ascontiguousarray(np.swapaxes(v.reshape(v.shape[:-1] + (j, 128)), -1, -2))


def make_consts():
    c = {}
    c['ident'] = np.eye(128, dtype=np.float32)
    s = np.arange(128)[:, None]
    l = np.arange(128)[None, :]
    causal = (s <= l)
    blk = (s // 4 == l // 4)
    c['maskneg_p'] = np.where(causal, 0.0, -30000.0).astype(np.float32)
    c['maskneg_s'] = np.where(causal & blk, 0.0, -30000.0).astype(np.float32)
    c['maskbig_p'] = np.where(causal, 0.0, 30000.0).astype(np.float32)
    c['maskbig_s'] = np.where(causal & blk, 0.0, 30000.0).astype(np.float32)
    sel = np.zeros((16, 16, 128), np.float32)
    for h in range(16):
        sel[h, h, :] = 1.0
    c['sel16'] = sel.reshape(16, 16 * 128)
    c['ones'] = np.ones((128, 128), np.float32)
    cm = (np.arange(64)[None, :] // 4 == np.arange(16)[:, None]).astype(np.float32)
    c['colmask'] = np.broadcast_to(cm.reshape(1, 16 * 64), (128, 16 * 64)).copy()
    rm = np.zeros((128, 128), np.float32)
    rm[:64, :16] = cm.T
    c['rowmask'] = rm
    ps = np.zeros((128, 128), np.float32)
    for k in range(16):
        ps[k, :] = ((np.arange(128) // 64) == (k % 2))
    c['parsel'] = ps
    hm = np.zeros((128, 128), np.float32)
    for k in range(16):
        hm[k, k // 2] = 1.0
    c['hmask'] = hm
    bm = np.zeros((128, 128), np.float32)
    for p in range(128):
        bm[p, p // 16] = 1.0
    c['bmask'] = bm
    return c


CONST_ORDER = ['ident', 'maskneg_p', 'maskneg_s', 'maskbig_p', 'maskbig_s', 'ones', 'rowmask', 'parsel', 'hmask', 'bmask']


def build(depth=DEPTH, stage=99, dbg=False):
    nc = bass.Bass("TRN2", target_bir_lowering=False)
    c = Ctx(nc)

    def din(name, shape, dt=F32):
        return nc.dram_tensor(name, list(shape), dt, kind="ExternalInput").ap()

    def dout(name, shape, dt=F32):
        return nc.dram_tensor(name, list(shape), dt, kind="ExternalOutput").ap()

    def dscr(name, shape, dt):
        return nc.dram_tensor(name, list(shape), dt, kind="Internal").ap()

    IN_SHAPES = {
        "x_p": [TP, D], "x_s": [TS, D], "mem": [256, D],
        "ck": [DEPTH, NB, 256, D], "cv": [DEPTH, NB, 256, D],
        "st_conv": [DEPTH, NB * 3, 1536], "st_ssd": [DEPTH, NB, 1024, 128],
        "w_kv": [DEPTH, D, 2048], "w_down": [DEPTH, 4, D, D], "w_out": [DEPTH, D, D],
        "norm_in_c": [DEPTH, 128, 8], "mem_norm_c": [DEPTH, 128, 8], "final_norm": [1, D],
        "b_gate_c": [DEPTH, 128, 32], "c_sel16": [16, 16 * 128], "c_colmask": [128, 16 * 64],
        "conv_w_c": [DEPTH, 128, 48], "conv_b_c": [DEPTH, 128, 12], "ssd_hp": [DEPTH, 16, 2],
        "ssd_d": [DEPTH, 16], "ssd_norm": [DEPTH, D],
        "ml_hp": [DEPTH, 4, 2], "ml_norm": [DEPTH, D], "st_mlc": [DEPTH, NB, 1024, 256], "st_mln": [DEPTH, NB, 1024],
        "st_mlm": [DEPTH, 4, NB],
        "s5_a_re_t": [DEPTH, 64, 64], "s5_a_im_t": [DEPTH, 64, 64], "s5_log_dt": [DEPTH, 64],
        "s5_b_re_t": [DEPTH, 64, 1024], "s5_b_im_t": [DEPTH, 64, 1024], "s5_c_re_t": [DEPTH, 64, 1024], "s5_c_im_t": [DEPTH, 64, 1024],
        "s5_d_c": [DEPTH, 128, 8], "st_s5_t": [DEPTH, 64, 2, 64, NB], "s5_glu_w": [DEPTH, D, D], "s5_glu_b_c": [DEPTH, 128, 8],
    }
    for i in range(DEPTH):
        IN_SHAPES["w_in%d" % i] = [D, D_IN]
    for k in CONST_ORDER:
        IN_SHAPES["c_" + k] = [128, 128]
    _ins = {}

    def IN(name):
        if name not in _ins:
            _ins[name] = din(name, IN_SHAPES[name])
        return _ins[name]

    nc._used_inputs = _ins

    y_p = dout("y_p", [TP, D])
    y_s = dout("y_s", [TS, D])
    mk_o = dout("mk_o", [DEPTH, 256, D])
    mv_o = dout("mv_o", [DEPTH, 256, D])
    conv_po = dout("conv_po", [DEPTH, 3, 1536])
    conv_so = dout("conv_so", [DEPTH, NB, 3, 1536])
    ssd_po = dout("ssd_po", [DEPTH, 1024, 128]) if stage >= 2 else None
    ssd_so = dout("ssd_so", [DEPTH, NB, 1024, 128]) if stage >= 2 else None
    if stage >= 3:
        s5re_po = dout("s5re_po", [DEPTH, 64, 64])
        s5im_po = dout("s5im_po", [DEPTH, 64, 64])
        s5re_so = dout("s5re_so", [DEPTH, NB, 64, 64])
        s5im_so = dout("s5im_so", [DEPTH, NB, 64, 64])
    if stage >= 4:
        mc_po = dout("mc_po", [DEPTH, 1024, 256])
        mn_po = dout("mn_po", [DEPTH, 1024])
        mm_po = dout("mm_po", [DEPTH, 4])
        mc_so = dout("mc_so", [DEPTH, NB, 1024, 256])
        mn_so = dout("mn_so", [DEPTH, NB, 1024])
        mm_so = dout("mm_so", [DEPTH, 4, NB])
    dbg_o = {}

    def DBG(name):
        if name not in dbg_o:
            dbg_o[name] = dout("dbg_" + name, [1024, NT])
        return dbg_o[name]

    sfm = dscr("sfm", [NFM, NT], BF16)
    stm = dscr("stm", [NT, NTM], BF16)
    x_scr = dscr("x_scr", [NT, D], F32)
    g_scr = dscr("g_scr", [24, NT], F32)

    with c.es:
        h_fm = c.sb("h_fm", [128, 8, NT], BF16)
        yk_fm = c.sb("yk_fm", [128, 8, NT], BF16)
        ident = c.sb("ident", [128, 128], F32)
        ident_b = c.sb("ident_b", [128, 128], BF16)
        ones_f = c.sb("ones_f", [128, 128], F32)
        cmask = {k: c.sb(k, [128, 128], F32) for k in ('maskneg_p', 'maskneg_s', 'maskbig_p', 'maskbig_s', 'rowmask', 'parsel', 'hmask')}
        colmask = c.sb("colmask", [128, 16 * 64], BF16)
        gin = c.sb("gin", [128, 8], F32)
        gmem = c.sb("gmem", [128, 8], F32)
        bgate = c.sb("bgate", [128, 32], F32)
        hm_fm = yk_fm[:, :, 0:256]
        wbufs = Rot([c.sb("wb%d" % i, [128, 8, 512], BF16) for i in range(2)])
        stg_b = Rot([c.sb("stgb%d" % i, [128, 512], BF16) for i in range(4)])
        stg_f = Rot([c.sb("stgf%d" % i, [128, 512], F32) for i in range(2)])
        f4 = Rot([c.sb("f4_%d" % i, [128, D], F32) for i in range(4)])
        b2 = Rot([c.sb("b2_%d" % i, [128, D], BF16) for i in range(5)])
        fq = Rot([c.sb("fq_%d" % i, [128, 128], F32) for i in range(6)])
        bq = Rot([c.sb("bq_%d" % i, [128, 128], BF16) for i in range(4)])
        sq_junk = c.sb("sq_junk", [128, D], BF16)
        st_small = Rot([c.sb("sts%d" % i, [128, 4], F32) for i in range(4)])
        banks = [c.ps("pb%d" % i, [128, 512], F32) for i in range(8)]
        pbank = Rot(banks[0:4])
        ptr2 = [banks[4], banks[5]]

        def bfv(bank):
            return bank[:].bitcast(BF16)

        c.dma('sp', ident[:], IN("c_ident")[:, :])
        c.dma('sp', ones_f[:], IN("c_ones")[:, :])
        c.copy(ident_b[:], ident[:])
        if stage >= 2:
            for k in cmask:
                c.dma('sp', cmask[k][:], IN("c_" + k)[:, :])
            c.dma('pool', colmask[:], IN("c_colmask")[:, :])

        def rms_rstd(src, P, n=D):
            st = st_small.next()
            c.act(sq_junk[0:P, 0:n], src, AF.Square, accum_out=st[0:P, 0:1])
            c.ts(st[0:P, 1:2], st[0:P, 0:1], 1.0 / n, EPS, op0=ALU.mult, op1=ALU.add)
            c.act(st[0:P, 2:3], st[0:P, 1:2], AF.Sqrt)
            c.recip(st[0:P, 3:4], st[0:P, 2:3])
            return st[0:P, 3:4]

        def norm_to_fm(src, P, gcol, dst):
            rstd = rms_rstd(src, P)
            xn = f4.next()
            c.ts(xn[0:P, :], src, rstd, None, op0=ALU.mult)
            for j in range(8):
                pt = ptr2[j // 4]
                c.tr(pt[:, (j % 4) * 128:(j % 4) * 128 + P], xn[0:P, j * 128:(j + 1) * 128], ident[0:P, 0:P])
            for hlf in range(2):
                pv = ptr2[hlf][:].rearrange("p (j t) -> p j t", t=128)[:, :, 0:P]
                c.tt(dst[:, 4 * hlf:4 * hlf + 4, :], pv,
                     gcol[:, 4 * hlf:4 * hlf + 4].unsqueeze(2).to_broadcast([128, 4, P]), ALU.mult)

        def tm_to_fm(src_b, P, dst):
            pv = bfv(ptr2[0])
            for j in range(8):
                c.tr(pv[:, j * 128:j * 128 + P], src_b[0:P, j * 128:(j + 1) * 128], ident_b[0:P, 0:P])
            c.copy(dst, pv.rearrange("p (j t) -> p j t", t=128)[:, :, 0:P], eng='act')

        def load_w(src):
            wb = wbufs.next()
            cw = src.shape[1]
            c.dma('pool', wb[:, :, 0:cw], src.rearrange("(kt p) c -> p kt c", p=128))
            return wb

        evac_flip = [0]

        def evac(out, in_, func=None, bias=None, scale=None):
            if func is not None:
                c.act(out, in_, func, bias=bias, scale=scale)
            elif scale is not None:
                if evac_flip[0] % 2 == 0:
                    c.act(out, in_, AF.Copy, scale=scale)
                else:
                    c.ts(out, in_, scale, None, op0=ALU.mult)
                evac_flip[0] += 1
            else:
                c.copy(out, in_, eng='act' if evac_flip[0] % 2 == 0 else 'dve')
                evac_flip[0] += 1

        def proj_fm(l, col0, ncols, row0, func=None, scale=None, biascol=None):
            for c0 in range(0, ncols, 512):
                cw = min(512, ncols - c0)
                wb = load_w(IN("w_in%d" % l)[:, col0 + c0:col0 + c0 + cw])
                for cb in range(cw // 128):
                    for (t0, n) in STS:
                        pb = pbank.next()
                        for kt in range(8):
                            c.mm(pb[:, 0:n], wb[:, kt, cb * 128:(cb + 1) * 128], h_fm[:, kt, t0:t0 + n],
                                 start=(kt == 0), stop=(kt == 7))
                        sg = stg_b.next()
                        b = biascol((c0 + cb * 128) // 128) if biascol is not None else None
                        evac(sg[:, 0:n], pb[:, 0:n], func=func, bias=b, scale=scale)
                        r0 = row0 + c0 + cb * 128
                        c.dma('sp', sfm[r0:r0 + 128, t0:t0 + n], sg[:, 0:n])

        def proj_tm(l, col0, ncols, scol0, func=None, scale=None):
            for c0 in range(0, ncols, 512):
                wb = load_w(IN("w_in%d" % l)[:, col0 + c0:col0 + c0 + 512])
                for (t0, P) in TILES:
                    pb = pbank.next()
                    for kt in range(8):
                        c.mm(pb[0:P, :], h_fm[:, kt, t0:t0 + P], wb[:, kt, :], start=(kt == 0), stop=(kt == 7))
                    sg = stg_b.next()
                    evac(sg[0:P, :], pb[0:P, :], func=func, scale=scale)
                    c.dma('sp', stm[t0:t0 + P, scol0 + c0:scol0 + c0 + 512], sg[0:P, :])

        def proj_small(l, col0, ncols, dst):
            wb = load_w(IN("w_in%d" % l)[:, col0:col0 + ncols])
            for (t0, n) in STS:
                pb = pbank.next()
                for kt in range(8):
                    c.mm(pb[0:ncols, 0:n], wb[:, kt, 0:ncols], h_fm[:, kt, t0:t0 + n], start=(kt == 0), stop=(kt == 7))
                sg = stg_f.next()
                c.copy(sg[0:ncols, 0:n], pb[0:ncols, 0:n], eng='act')
                c.dma('sp', dst[:, t0:t0 + n], sg[0:ncols, 0:n])

        def dump_fm(name, src):
            if dbg:
                for j in range(8):
                    c.dma('pool', DBG(name)[j * 128:(j + 1) * 128, :], src[:, j, :])

        def x_src(l, tt):
            t0, P = TILES[tt]
            if l == 0:
                return IN("x_p")[t0:t0 + P, :] if tt < 16 else IN("x_s")[:, :]
            return x_scr[t0:t0 + P, :]

        def phase0(l):
            c.dma('sp', gin[:], IN("norm_in_c")[l])
            c.dma('sp', gmem[:], IN("mem_norm_c")[l])
            c.dma('sp', bgate[:], IN("b_gate_c")[l])
            for tt, (t0, P) in enumerate(TILES):
                xt = f4.next()
                c.dma('sp', xt[0:P, :], x_src(l, tt))
                norm_to_fm(xt[0:P, :], P, gin, h_fm[:, :, t0:t0 + P])

        def memkv(l, mk_fm, mv_tm):
            for mt in range(2):
                mtile = f4.next()
                c.dma('sp', mtile[:, :], IN("mem")[mt * 128:(mt + 1) * 128, :])
                norm_to_fm(mtile[:, :], 128, gmem, hm_fm[:, :, mt * 128:(mt + 1) * 128])
            for half in ([h_ for h_ in range(2) if ('kv%d' % h_) not in SKIP] if 'kvw' not in SKIP else ()):
                for c0 in range(0, 1024, 512):
                    wb = load_w(IN("w_kv")[l, :, half * 1024 + c0:half * 1024 + c0 + 512])
                    for mt in range(2):
                        pb = pbank.next()
                        for kt in range(8):
                            c.mm(pb[:, :], hm_fm[:, kt, mt * 128:(mt + 1) * 128], wb[:, kt, :], start=(kt == 0), stop=(kt == 7))
                        sg = stg_f.next()
                        c.copy(sg[:, :], pb[:, :], eng='act')
                        dst = mk_o if half == 0 else mv_o
                        c.dma('sp', dst[l, mt * 128:(mt + 1) * 128, c0:c0 + 512], sg[:, :])
                        if half == 1:
                            c.copy(mv_tm[:, mt, c0:c0 + 512], sg[:, :], eng='dve')
                    if half == 0 and 'mkfm' not in SKIP:
                        for cb in range(4):
                            pb = pbank.next()
                            for kt in range(8):
                                c.mm(pb[:, 0:256], wb[:, kt, cb * 128:(cb + 1) * 128], hm_fm[:, kt, :], start=(kt == 0), stop=(kt == 7))
                            c.ts(mk_fm[:, c0 // 128 + cb, :], pb[:, 0:256], 1.0 / 16, None, op0=ALU.mult)

        def projections(l):
            if 'projfm' not in SKIP:
                proj_fm(l, O_XBC, 1536, R_XBC)
            for tt in ((15, 16) if 'convst' not in SKIP else ()):
                t0, P = TILES[tt]
                for c0 in range(0, 1536, 512):
                    wb = load_w(IN("w_in%d" % l)[:, O_XBC + c0:O_XBC + c0 + 512])
                    pb = pbank.next()
                    for kt in range(8):
                        c.mm(pb[0:P, :], h_fm[:, kt, t0:t0 + P], wb[:, kt, :], start=(kt == 0), stop=(kt == 7))
                    sg = stg_f.next()
                    c.copy(sg[0:P, :], pb[0:P, :], eng='act')
                    if tt == 15:
                        c.dma('sp', conv_po[l, :, c0:c0 + 512], sg[125:128, :])
                    else:
                        for b in range(NB):
                            c.dma('sp', conv_so[l, b, :, c0:c0 + 512], sg[4 * b + 1:4 * b + 4, :])
            if 'small' not in SKIP:
                if 's16' not in SKIP:
                    proj_small(l, O_DT, 16, g_scr[0:16, :])
                if 's8' not in SKIP:
                    proj_small(l, O_I, 8, g_scr[16:24, :])
            if stage >= 2:
                proj_tm(l, O_ZSSD, 1024, C_ZSSD, func=AF.Silu)
            if stage >= 3:
                proj_fm(l, O_U, 1024, R_U)
                proj_fm(l, O_ZS5, 1024, R_ZS5, func=AF.Silu)
                proj_fm(l, O_Q, 1024, R_Q)
                proj_fm(l, O_K, 1024, R_K, scale=1.0 / 16)
                proj_tm(l, O_K, 1024, C_KT, scale=1.0 / 16)
                proj_tm(l, O_V, 1024, C_V)
                proj_tm(l, O_O, 1024, C_O, func=AF.Sigmoid)
                proj_tm(l, O_ZML, 1024, C_ZML, func=AF.Silu)
                proj_fm(l, O_QXA, 1024, R_QXA)
                proj_tm(l, O_ZXA, 1024, C_ZXA, func=AF.Silu)
                proj_fm(l, O_GATE, 4096, R_GATE, func=AF.Sigmoid, biascol=lambda j: bgate[:, j:j + 1])

        def ssd_branch(l):
            bes = contextlib.ExitStack()
            uid = "_a%d" % l
            anames = []

            def A(name, shape, dt=F32):
                anames.append(name + uid)
                return bes.enter_context(nc.sbuf_tensor(name + uid, list(shape), dt))

            cw_sb = A("cw_sb", [128, 12, 4])
            cb_sb = A("cb_sb", [128, 12])
            hp16 = A("hp16", [16, 8])
            dbc = A("dbc", [128, 16])
            gbc = A("gbc", [128, D])
            DI = A("DI", [128, 16, 128], BF16)
            xr = A("xr", [128, 12, 515], BF16)
            xr_s = A("xr_s", [128, 12, 16, 7], BF16)
            xc = A("xc", [128, 12, 512], BF16)
            acc = A("acc", [128, 512])
            acs_r = Rot([A("acs%d" % i, [16, 128]) for i in range(2)])
            dt_r = Rot([A("dtt%d" % i, [16, 128]) for i in range(2)])
            tw_r = Rot([A("tww%d" % i, [16, 128]) for i in range(2)])
            rawg_r = Rot([A("rawg%d" % i, [16, 128]) for i in range(2)])
            selm_r = Rot([A("selm%d" % i, [16, 128]) for i in range(3)])
            STs_r = Rot([A("STs%d" % i, [128, D], BF16) for i in range(2)])
            btm_r = Rot([A("btm%d" % i, [128, 256], BF16) for i in range(2)])
            g16 = Rot([A("g16_%d" % i, [16, 128]) for i in range(4)])
            gtm_r = Rot([A("gtm%d" % i, [128, 96]) for i in range(2)])
            eatm_r = Rot([A("eatm%d" % i, [128, 16]) for i in range(2)])
            cbT_r = Rot([A("cbT%d" % i, [128, 2, 128]) for i in range(2)])
            ST = A("ST", [128, D])
            ST_bf = A("ST_bf", [128, D], BF16)
            dvec = A("dvec", [128, 16, 8])
            ers = A("ers", [16, 16, 8])
            els = A("els", [16, 16])
            dcs = A("dcs", [128, 16])
            S_r = Rot([A("S_b%d" % i, [128, 8, 128]) for i in range(2)])
            Cm_r = Rot([A("Cm%d" % i, [128, 2, 64], BF16) for i in range(2)])
            Bm_r = Rot([A("Bm%d" % i, [64, 256], BF16) for i in range(2)])
            c.dma('sp', cw_sb[:].rearrange("p j k -> p (j k)"), IN("conv_w_c")[l])
            c.dma('sp', cb_sb[:], IN("conv_b_c")[l])
            c.dma('sp', hp16[:, 0:2], IN("ssd_hp")[l])
            c.dma('sp', dbc[:], IN("ssd_d")[l:l + 1, :].partition_broadcast(128))
            c.dma('sp', gbc[:], IN("ssd_norm")[l:l + 1, :].partition_broadcast(128))
            c.act(hp16[:, 2:3], hp16[:, 1:2], AF.Exp)
            c.ts(hp16[:, 3:4], hp16[:, 2:3], -1.0, None, op0=ALU.mult)
            dtb, aneg = hp16[:, 0:1], hp16[:, 3:4]
            for h in range(16):
                c.ts(DI[:, h, :], ident[:], dbc[:, h:h + 1], None, op0=ALU.mult)
            c.memset(ST[:], 0.0)
            c.memset(ST_bf[:], 0.0)
            pX, pB, pCB, pYA, pYB, pEA, pEB, pBC = banks
            pbc_r = Rot([pBC, pCB])
            for si, (s0, n) in enumerate(STS):
                sample = (si == 4)
                src_rows = sfm[R_XBC:R_XBC + 1536, :].rearrange("(j p) t -> p j t", p=128)
                if not sample:
                    if s0 == 0:
                        c.memset(xr[:, :, 0:3], 0.0)
                        c.dma('sp', xr[:, :, 3:515], src_rows[:, :, 0:512])
                    else:
                        c.dma('sp', xr[:, :, 0:515], src_rows[:, :, s0 - 3:s0 + 512])
                    for j in range(12):
                        c.ts(acc[:, :], xr[:, j, 3:515], cw_sb[:, j, 3:4], cb_sb[:, j:j + 1], op0=ALU.mult, op1=ALU.add)
                        for k in (2, 1, 0):
                            c.stt(acc[:, :], xr[:, j, k:k + 512], cw_sb[:, j, k:k + 1], acc[:, :], ALU.mult, ALU.add)
                        c.act(xc[:, j, :], acc[:, :], AF.Silu)
                else:
                    raw = b2.next()
                    rawv = raw[:, 0:768].rearrange("p (j t) -> p j t", t=64)
                    c.dma('sp', rawv, src_rows[:, :, 2048:2112])
                    csts = (f4.next(), f4.next())
                    for hlf in range(2):
                        c.dma('sp', csts[hlf][0:48, 0:768], IN("st_conv")[l][:, hlf * 768:(hlf + 1) * 768])
                    for j in range(12):
                        pt = ptr2[j // 6]
                        c.tr(pt[:, (j % 6) * 48:(j % 6) * 48 + 48], csts[j // 6][0:48, (j % 6) * 128:(j % 6 + 1) * 128], ident[0:48, 0:48])
                    for j in range(12):
                        c.copy(xr_s[:, j, :, 0:3], ptr2[j // 6][:, (j % 6) * 48:(j % 6) * 48 + 48].rearrange("p (b k) -> p b k", k=3),
                               eng='act' if j % 2 else 'dve')
                        c.copy(xr_s[:, j, :, 3:7], rawv[:, j, :].rearrange("p (b t) -> p b t", t=4), eng='dve' if j % 2 else 'act')
                    for j in range(12):
                        av = acc[:, 0:64].rearrange("p (b t) -> p b t", t=4)
                        c.ts(av, xr_s[:, j, :, 3:7], cw_sb[:, j, 3:4], cb_sb[:, j:j + 1], op0=ALU.mult, op1=ALU.add)
                        for k in (2, 1, 0):
                            c.stt(av, xr_s[:, j, :, k:k + 4], cw_sb[:, j, k:k + 1], av, ALU.mult, ALU.add)
                        c.act(xc[:, j, 0:64], acc[:, 0:64], AF.Silu)
                for (t0, P) in [t for t in TILES if s0 <= t[0] < s0 + n]:
                    lo = t0 - s0
                    mneg = cmask['maskneg_s'] if sample else cmask['maskneg_p']
                    acs, dtt, tww = acs_r.next(), dt_r.next(), tw_r.next()
                    ge, gla = g16.next(), g16.next()
                    rawg = rawg_r.next()
                    c.dma('sp', rawg[:, 0:P], g_scr[0:16, t0:t0 + P])
                    c.act(ge[:, 0:P], rawg[:, 0:P], AF.Exp, bias=dtb)
                    c.act(dtt[:, 0:P], ge[:, 0:P], AF.Ln, bias=1.0)
                    c.ts(gla[:, 0:P], dtt[:, 0:P], aneg, None, op0=ALU.mult)
                    if not sample:
                        c.scan(acs[:, 0:P], ones_f[0:16, 0:P], gla[:, 0:P], 0.0, ALU.mult, ALU.add)
                        alast = acs[:, P - 1:P]
                        gd = g16.next()
                        c.act(gd[:, 0:P], acs[:, 0:P], AF.Exp, bias=alast, scale=-1.0)
                    else:
                        av = acs[:, 0:64].rearrange("p (b t) -> p b t", t=4)
                        lv = gla[:, 0:64].rearrange("p (b t) -> p b t", t=4)
                        c.copy(av[:, :, 0:1], lv[:, :, 0:1])
                        for t in (1, 2, 3):
                            c.tt(av[:, :, t:t + 1], av[:, :, t - 1:t], lv[:, :, t:t + 1], ALU.add)
                        gd = g16.next()
                        dv = gd[:, 0:64].rearrange("p (b t) -> p b t", t=4)
                        c.tt(dv, av[:, :, 3:4].to_broadcast([16, 16, 4]), av, ALU.subtract)
                        c.act(gd[:, 0:64], gd[:, 0:64], AF.Exp)
                    c.tt(tww[:, 0:P], gd[:, 0:P], dtt[:, 0:P], ALU.mult)
                    for qi, qsrc in enumerate((acs, dtt, tww)):
                        c.tr(pB[0:P, 32 * qi:32 * qi + 16], qsrc[:, 0:P], ident[0:16, 0:16])
                    gtm = gtm_r.next()
                    c.copy(gtm[0:P, :].rearrange("p (q x) -> p q x", x=32)[:, :, 0:16], pB[0:P, 0:96].rearrange("p (q x) -> p q x", x=32)[:, :, 0:16])
                    eatm = eatm_r.next()
                    c.act(eatm[0:P, :], gtm[0:P, 0:16], AF.Exp)
                    pxv = bfv(pX)
                    for j in range(8):
                        c.tr(pxv[0:P, j * 128:(j + 1) * 128], xc[:, j, lo:lo + P], ident_b[:, :])
                    xtm = b2.next()
                    c.copy(xtm[0:P, :], pxv[0:P, :], eng='act')
                    pbv = bfv(pYA)
                    for g in range(2):
                        c.tr(pbv[0:P, g * 128:(g + 1) * 128], xc[:, 8 + g, lo:lo + P], ident_b[:, :])
                    btm = btm_r.next()
                    c.copy(btm[0:P, :], pbv[0:P, 0:256])
                    btms = (btm[:, 0:128], btm[:, 128:256])
                    cbT = cbT_r.next()
                    for g in range(2):
                        c.mm(pCB[0:P, g * 128:g * 128 + P], xc[:, 8 + g, lo:lo + P], xc[:, 10 + g, lo:lo + P])
                    c.copy(cbT[0:P, :, 0:P], pCB[0:P, 0:256].rearrange("p (g t) -> p g t", g=2)[:, :, 0:P], eng='act')
                    for h in range(16):
                        g = h // 8
                        pbc = pbc_r.next()
                        selm = selm_r.next()
                        c.ts(selm[:, 0:P], acs[:, 0:P], ident[0:16, h:h + 1], None, op0=ALU.mult)
                        c.mm(pbc[0:P, 0:P], ones_f[0:16, 0:P], selm[:, 0:P])
                        tsb = fq.next()
                        c.stt(tsb[0:P, 0:P], pbc[0:P, 0:P], gtm[0:P, h:h + 1], mneg[0:P, 0:P], ALU.subtract, ALU.min)
                        E = fq.next()
                        c.act(E[0:P, 0:P], tsb[0:P, 0:P], AF.Exp)
                        MT = bq.next()
                        c.stt(MT[0:P, 0:P], E[0:P, 0:P], gtm[0:P, 32 + h:33 + h], cbT[0:P, g, 0:P], ALU.mult, ALU.mult)
                        py = pYA if h < 8 else pYB
                        oc = (h % 8) * 64
                        c.mm(py[0:P, oc:oc + 64], MT[0:P, 0:P], xtm[0:P, h * 64:(h + 1) * 64], start=True, stop=False)
                        c.mm(py[0:P, oc:oc + 64], DI[0:P, h, 0:P], xtm[0:P, h * 64:(h + 1) * 64], start=False, stop=True)
                    if not sample:
                        for g, pe_ in enumerate((pEA, pEB)):
                            c.mm(pe_[0:P, :], xc[:, 10 + g, lo:lo + P], ST_bf[:, g * 512:(g + 1) * 512])
                    else:
                        for b in range(NB):
                            Sb = S_r.next()
                            c.dma('sp', Sb[:], IN("st_ssd")[l, b].rearrange("(j p) n -> p j n", p=128))
                            for j in range(8):
                                c.tr((pX, pB)[j // 4][:, (j % 4) * 128:(j % 4 + 1) * 128], Sb[:, j, :], ident[:, :])
                            STs = STs_r.next()
                            c.copy(STs[:, 0:512], pX[:, :], eng='act')
                            c.copy(STs[:, 512:1024], pB[:, :], eng='dve')
                            Cm = Cm_r.next()
                            c.tt(Cm[:, :, :], xc[:, 10:12, 0:64],
                                 colmask[:, b * 64:(b + 1) * 64].unsqueeze(1).to_broadcast([128, 2, 64]), ALU.mult)
                            for g, pe_ in enumerate((pEA, pEB)):
                                c.mm(pe_[0:P, :], Cm[:, g, :], STs[:, g * 512:(g + 1) * 512], start=(b == 0), stop=(b == NB - 1))
                            if b == 0:
                                avl = acs[:, 0:64].rearrange("p (b t) -> p b t", t=4)[:, :, 3:4]
                                c.act(els[:, :].unsqueeze(2), avl, AF.Exp)
                                c.tt(ers[:, :, :], els[:, :].unsqueeze(2).to_broadcast([16, 16, 8]),
                                     cmask['hmask'][0:16, 0:8].unsqueeze(1).to_broadcast([16, 16, 8]), ALU.mult)
                                c.mm(pCB[:, 0:128], cmask['parsel'][0:16, :], ers[:, :, :].rearrange("p b j -> p (b j)"))
                                c.copy(dvec[:, :, :], pCB[:, 0:128].rearrange("p (b j) -> p b j", j=8))
                                xw = b2.next()
                                c.tt(xw[0:P, :].rearrange("p (h q) -> p h q", q=64), xtm[0:P, :].rearrange("p (h q) -> p h q", q=64),
                                     gtm[0:P, 64:80].unsqueeze(2).to_broadcast([P, 16, 64]), ALU.mult)
                                xw_keep = xw
                            Bm = Bm_r.next()
                            for g in range(2):
                                c.ts(Bm[0:64, g * 128:(g + 1) * 128], btm[0:64, g * 128:(g + 1) * 128], cmask['rowmask'][0:64, b:b + 1], None, op0=ALU.mult)
                            for j in range(8):
                                pd = pCB if j < 4 else pBC
                                c.mm(pd[:, (j % 4) * 128:(j % 4 + 1) * 128], xw_keep[0:64, j * 128:(j + 1) * 128],
                                     Bm[0:64, (j // 4) * 128:(j // 4 + 1) * 128])
                            c.tt(Sb[:, :, :], Sb[:, :, :], dvec[:, b, :].unsqueeze(2).to_broadcast([128, 8, 128]), ALU.mult)
                            c.tt(Sb[:, 0:4, :], Sb[:, 0:4, :], pCB[:, :].rearrange("p (j n) -> p j n", n=128), ALU.add)
                            c.tt(Sb[:, 4:8, :], Sb[:, 4:8, :], pBC[:, :].rearrange("p (j n) -> p j n", n=128), ALU.add)
                            c.dma('sp', ssd_so[l, b].rearrange("(j p) n -> p j n", p=128), Sb[:, :, :])
                    ty = f4.next()
                    for g, (pe_, py) in enumerate(((pEA, pYA), (pEB, pYB))):
                        tv = ty[0:P, g * 512:(g + 1) * 512]
                        c.tt(tv.rearrange("p (h q) -> p h q", q=64), pe_[0:P, :].rearrange("p (h q) -> p h q", q=64),
                             eatm[0:P, g * 8:(g + 1) * 8].unsqueeze(2).to_broadcast([P, 8, 64]), ALU.mult)
                        c.tt(tv, tv, py[0:P, :], ALU.add)
                    ztm = b2.next()
                    c.dma('sp', ztm[0:P, :], stm[t0:t0 + P, C_ZSSD:C_ZSSD + 1024])
                    c.tt(ty[0:P, :], ty[0:P, :], ztm[0:P, :], ALU.mult)
                    rstd = rms_rstd(ty[0:P, :], P)
                    yn = b2.next()
                    c.stt(yn[0:P, :], ty[0:P, :], rstd, gbc[0:P, :], ALU.mult, ALU.mult)
                    tm_to_fm(yn, P, yk_fm[:, :, t0:t0 + P])
                    if not sample:
                        xw = b2.next()
                        c.tt(xw[0:P, :].rearrange("p (h q) -> p h q", q=64), xtm[0:P, :].rearrange("p (h q) -> p h q", q=64),
                             gtm[0:P, 64:80].unsqueeze(2).to_broadcast([P, 16, 64]), ALU.mult)
                        for g, pe_ in enumerate((pEA, pEB)):
                            c.mm(pe_[:, :], btm[0:P, g * 128:(g + 1) * 128], xw[0:P, g * 512:(g + 1) * 512])
                        e16 = g16.next()
                        c.act(e16[:, 0:1], alast, AF.Exp)
                        ed = g16.next()
                        c.ts(ed[:, 0:16], ident[0:16, 0:16], e16[:, 0:1], None, op0=ALU.mult)
                        c.mm(pX[:, 0:16], ones_f[0:16, :], ed[:, 0:16])
                        c.copy(dcs[:, :], pX[:, 0:16])
                        c.tt(ST[:, :].rearrange("p (h q) -> p h q", q=64), ST[:, :].rearrange("p (h q) -> p h q", q=64),
                             dcs[:, :].unsqueeze(2).to_broadcast([128, 16, 64]), ALU.mult)
                        for g, pe_ in enumerate((pEA, pEB)):
                            c.tt(ST[:, g * 512:(g + 1) * 512], ST[:, g * 512:(g + 1) * 512], pe_[:, :], ALU.add)
                        c.copy(ST_bf[:, :], ST[:, :], eng='act')
                        if t0 == 1920:
                            for j in range(8):
                                c.tr(ptr2[j // 4][:, (j % 4) * 128:(j % 4 + 1) * 128], ST[:, j * 128:(j + 1) * 128], ident[:, :])
                            so = f4.next()
                            c.copy(so[:, 0:512], ptr2[0][:, :], eng='act')
                            c.copy(so[:, 512:1024], ptr2[1][:, :], eng='dve')
                            c.dma('sp', ssd_po[l].rearrange("(j p) n -> p j n", p=128), so[:, :].rearrange("p (j n) -> p j n", n=128))
            dump_fm("y_a", yk_fm)
            c.barrier()
            c.release(anames)
            bes.close()
        def s5_branch(l):
            bes = contextlib.ExitStack()
            uid = "_b%d" % l
            anames = []

            def A(name, shape, dt=F32):
                anames.append(name + uid)
                return bes.enter_context(nc.sbuf_tensor(name + uid, list(shape), dt))

            TWO_PI = 6.283185307179586
            PI = 3.141592653589793
            par = A("par", [64, 3, 64])
            tb = [A("tb%d" % i, [64, 64]) for i in range(12)]
            tbi = A("tbi", [64, 64], mybir.dt.int32)
            A2 = A("A2", [64, 2, 64])
            Bc = A("Bc", [64, 2, 64])
            sre, sim = A("sre", [64, 64]), A("sim", [64, 64])
            big = A("big", [64, 2 * 64 * 16])
            Bb = big[:, :].rearrange("p (r x) -> p r x", r=2)
            CT = A("CT", [64, 2, 64 * 16])
            BT = A("BT", [128, 8, 2, 8, 64], BF16)
            bmask = A("bmask", [128, 8])
            Dd = A("Dd", [128, 8, 128], BF16)
            dcol = A("dcol", [128, 8])
            SUB = 16
            u_st = A("u_st", [128, 8, 256], BF16)
            bu1 = A("bu1", [64, 2 * 64 * SUB])
            b4 = lambda t: t[:, :].rearrange("p (r g t) -> p r g t", r=2, g=64)
            bu_r = Rot([b4(big), b4(bu1)])
            hist_r = Rot([b4(A("hist%d" % i, [64, 2 * 64 * SUB])) for i in range(2)])
            t1_r = Rot([A("t1_0", [64, 2, 64 * 4])])
            t2_r = Rot([A("t2_0", [64, 2, 64 * 4])])
            H0s = A("H0s", [64, 2, 64, 4])
            yg_r = Rot([A("yg%d" % i, [SUB, D], BF16) for i in range(2)])
            hs = A("hs", [64, 128])

            c.dma('sp', par[:, 0, :], IN("s5_a_re_t")[l])
            c.dma('sp', par[:, 1, :], IN("s5_a_im_t")[l])
            c.dma('sp', par[:, 2, :], IN("s5_log_dt")[l:l + 1, :].partition_broadcast(64))
            c.dma('sp', Bb[:, 0, :], IN("s5_b_re_t")[l])
            c.dma('sp', Bb[:, 1, :], IN("s5_b_im_t")[l])
            c.dma('sp', CT[:, 0, :], IN("s5_c_re_t")[l])
            c.dma('sp', CT[:, 1, :], IN("s5_c_im_t")[l])
            c.dma('sp', bmask[:], IN("c_bmask")[:, 0:8])
            c.dma('sp', dcol[:], IN("s5_d_c")[l])
            c.ts(CT[:, 1, :], CT[:, 1, :], -1.0, None, op0=ALU.mult)
            for jb in range(8):
                c.ts(Dd[:, jb, :], ident[:, :], dcol[:, jb:jb + 1], None, op0=ALU.mult)
            are, aim = par[:, 0, :], par[:, 1, :]
            dtt, lr, th, mag, red, sn, cs, t_a, t_b, t_c, den, rden = tb
            c.act(dtt[:], par[:, 2, :], AF.Exp)
            c.tt(lr[:], are, dtt[:], ALU.mult)
            c.tt(th[:], aim, dtt[:], ALU.mult)
            c.act(mag[:], lr[:], AF.Exp)

            def sin_of(dst, ang, shift):
                c.ts(red[:], ang, shift, None, op0=ALU.add)
                c.ts(t_a[:], red[:], 1.0 / TWO_PI, None, op0=ALU.mult)
                c.copy(tbi[:], t_a[:])
                c.copy(t_a[:], tbi[:])
                c.stt(red[:], t_a[:], -TWO_PI, red[:], ALU.mult, ALU.add)
                c.ts(t_b[:], red[:], PI, -TWO_PI, op0=ALU.is_gt, op1=ALU.mult)
                c.tt(red[:], red[:], t_b[:], ALU.add)
                c.ts(t_b[:], red[:], -PI, TWO_PI, op0=ALU.is_lt, op1=ALU.mult)
                c.tt(red[:], red[:], t_b[:], ALU.add)
                c.ts(red[:], red[:], PI, -PI, op0=ALU.min, op1=ALU.max)
                c.act(dst, red[:], AF.Sin)

            sin_of(sn[:], th[:], 0.0)
            sin_of(cs[:], th[:], PI / 2)
            c.tt(A2[:, 0, :], mag[:], cs[:], ALU.mult)
            c.copy(A2[:, 1, :], A2[:, 0, :])
            c.tt(Bc[:, 1, :], mag[:], sn[:], ALU.mult)
            c.ts(Bc[:, 0, :], Bc[:, 1, :], -1.0, None, op0=ALU.mult)
            lre, lim = A2[:, 0, :], Bc[:, 1, :]
            c.ts(t_a[:], lre, -1.0, None, op0=ALU.add)
            c.tt(den[:], are, are, ALU.mult)
            c.tt(t_b[:], aim, aim, ALU.mult)
            c.tt(den[:], den[:], t_b[:], ALU.add)
            c.recip(rden[:], den[:])
            c.tt(t_b[:], t_a[:], are, ALU.mult)
            c.tt(t_c[:], lim, aim, ALU.mult)
            c.tt(t_b[:], t_b[:], t_c[:], ALU.add)
            c.tt(sre[:], t_b[:], rden[:], ALU.mult)
            c.tt(t_b[:], lim, are, ALU.mult)
            c.tt(t_c[:], t_a[:], aim, ALU.mult)
            c.tt(t_b[:], t_b[:], t_c[:], ALU.subtract)
            c.tt(sim[:], t_b[:], rden[:], ALU.mult)
            v3 = lambda ap: ap.rearrange("p (g c) -> p g c", c=16)
            sb_ = lambda ap: ap.unsqueeze(2).to_broadcast([64, 64, 16])
            T1 = bu1[:, 0:1024]
            T2 = bu1[:, 1024:2048]
            c.tt(v3(T1), v3(Bb[:, 0, :]), sb_(sim[:]), ALU.mult)
            c.tt(v3(T2), v3(Bb[:, 1, :]), sb_(sim[:]), ALU.mult)
            c.tt(v3(Bb[:, 0, :]), v3(Bb[:, 0, :]), sb_(sre[:]), ALU.mult)
            c.tt(Bb[:, 0, :], Bb[:, 0, :], T2, ALU.subtract)
            c.tt(v3(Bb[:, 1, :]), v3(Bb[:, 1, :]), sb_(sre[:]), ALU.mult)
            c.tt(Bb[:, 1, :], Bb[:, 1, :], T1, ALU.add)
            pW = banks[0]
            for jb in range(8):
                for ri in range(2):
                    c.tr(pW[:, ri * 64:(ri + 1) * 64], Bb[:, ri, jb * 128:(jb + 1) * 128], ident[0:64, 0:64])
                for ri in range(2):
                    c.tt(BT[:, jb, ri, :, :], pW[:, ri * 64:(ri + 1) * 64].unsqueeze(1).to_broadcast([128, 8, 64]),
                         bmask[:, :].unsqueeze(2).to_broadcast([128, 8, 64]), ALU.mult)

            pBU = Rot([banks[1], banks[2]])
            pY = [(banks[3], banks[4]), (banks[5], banks[6])]
            pyi = [0]
            pTo = banks[7]
            nsub = 0
            prev_hist = None
            for si, (s0, n) in enumerate([(256 * i, 256) for i in range(8)] + [(2048, 64)]):
                sample = (si == 8)
                c.dma('sp', u_st[:, :, 0:n], sfm[R_U:R_U + 1024, s0:s0 + n].rearrange("(j p) t -> p j t", p=128))
                def emit_bu(q0):
                    bu = bu_r.next()
                    for jb in range(8):
                        pb = pBU.next()
                        for ri in range(2):
                            for g in range(8):
                                o = (ri * 8 + g) * SUB
                                c.mm(pb[0:64, o:o + SUB], BT[:, jb, ri, g, :], u_st[:, jb, q0:q0 + SUB])
                        c.copy(bu[:, :, jb * 8:(jb + 1) * 8, :], pb[0:64, 0:16 * SUB].rearrange("p (r g t) -> p r g t", r=2, g=8), eng='act')
                    return bu

                nxt_bu = emit_bu(0)
                for q0 in range(0, n, SUB):
                    bu = nxt_bu
                    if q0 + SUB < n:
                        nxt_bu = emit_bu(q0 + SUB)
                    hist = hist_r.next()
                    if not sample:
                        for t in range(SUB):
                            if nsub == 0 and t == 0:
                                c.copy(hist[:, :, :, 0], bu[:, :, :, 0])
                                continue
                            hp = prev_hist[:, :, :, SUB - 1] if t == 0 else hist[:, :, :, t - 1]
                            t1, t2 = t1_r.next(), t2_r.next()
                            t1v = t1[:, :, 0:64]
                            t2v = t2[:, :, 0:64]
                            c.tt(t1v, A2[:, :, :], hp, ALU.mult)
                            c.tt(t2v[:, 0, :], Bc[:, 0, :], hp[:, 1, :], ALU.mult, eng='pool')
                            c.tt(t2v[:, 1, :], Bc[:, 1, :], hp[:, 0, :], ALU.mult, eng='pool')
                            c.tt(t1v, t1v, t2v, ALU.add)
                            c.tt(hist[:, :, :, t], t1v, bu[:, :, :, t], ALU.add)
                    else:
                        bh = q0 // SUB
                        c.dma('sp', H0s[:, :, :, :], IN("st_s5_t")[l][:, :, :, bh * 4:(bh + 1) * 4])
                        hv = hist[:, :, :, :].rearrange("p r g (b t) -> p r g b t", t=4)
                        bv = bu[:, :, :, :].rearrange("p r g (b t) -> p r g b t", t=4)
                        for t in range(4):
                            t1, t2 = t1_r.next(), t2_r.next()
                            t1v = t1[:, :, :].rearrange("p r (g b) -> p r g b", b=4)
                            t2v = t2[:, :, :].rearrange("p r (g b) -> p r g b", b=4)
                            for ri in range(2):
                                hp_r = H0s[:, ri, :, :] if t == 0 else hv[:, ri, :, :, t - 1]
                                hp_o = H0s[:, 1 - ri, :, :] if t == 0 else hv[:, 1 - ri, :, :, t - 1]
                                c.tt(t1v[:, ri], A2[:, ri, :].unsqueeze(2).to_broadcast([64, 64, 4]), hp_r, ALU.mult)
                                c.tt(t2v[:, ri], Bc[:, ri, :].unsqueeze(2).to_broadcast([64, 64, 4]), hp_o, ALU.mult, eng='pool')
                            for ri in range(2):
                                c.tt(t1v[:, ri], t1v[:, ri], t2v[:, ri], ALU.add)
                                c.tt(hv[:, ri, :, :, t], t1v[:, ri], bv[:, ri, :, :, t], ALU.add)
                        for ri, dst in enumerate((s5re_so, s5im_so)):
                            for b in range(4):
                                c.copy(hs[:, (b % 2) * 64:(b % 2 + 1) * 64], hv[:, ri, :, b, 3])
                                c.tr(pTo[0:64, (b % 2) * 64:(b % 2 + 1) * 64], hs[:, (b % 2) * 64:(b % 2 + 1) * 64], ident[0:64, 0:64])
                                so = stg_f.next()
                                c.copy(so[0:64, 0:64], pTo[0:64, (b % 2) * 64:(b % 2 + 1) * 64], eng='act')
                                c.dma('sp', dst[l, bh * 4 + b], so[0:64, 0:64])
                    pya, pyb = pY[pyi[0] % 2]
                    pyi[0] += 1
                    for jb in range(8):
                        py = pya if jb < 4 else pyb
                        oc = (jb % 4) * 128
                        for g in range(8):
                            gg = jb * 8 + g
                            oo = oc + g * 16
                            c.mm(py[0:SUB, oo:oo + 16], u_st[:, jb, q0:q0 + SUB], Dd[:, jb, g * 16:(g + 1) * 16], start=True, stop=False)
                            for ri in range(2):
                                c.mm(py[0:SUB, oo:oo + 16], hist[:, ri, gg, :], CT[:, ri, gg * 16:(gg + 1) * 16], start=False, stop=(ri == 1))
                    yg = yg_r.next()
                    c.act(yg[:, 0:512], pya[0:SUB, :], AF.Gelu)
                    c.act(yg[:, 512:1024], pyb[0:SUB, :], AF.Gelu)
                    tm_to_fm(yg, SUB, yk_fm[:, :, s0 + q0:s0 + q0 + SUB])
                    prev_hist = hist
                    nsub += 1
                if si == 7:
                    for ri, dst in enumerate((s5re_po, s5im_po)):
                        c.copy(hs[:, 0:64], prev_hist[:, ri, :, SUB - 1])
                        c.tr(pTo[0:64, 0:64], hs[:, 0:64], ident[0:64, 0:64])
                        so = stg_f.next()
                        c.copy(so[0:64, 0:64], pTo[0:64, 0:64], eng='act')
                        c.dma('sp', dst[l], so[0:64, 0:64])
            dump_fm("y_b0g", yk_fm)
            c.barrier()
            c.release(anames)
            bes.close()
            c.dma('sp', bgate[:, 0:8], IN("s5_glu_b_c")[l])
            for c0 in (0, 512):
                wb = load_w(IN("s5_glu_w")[l, :, c0:c0 + 512])
                for cb in range(4):
                    jb = c0 // 128 + cb
                    for (t0, n) in STS:
                        pb = pbank.next()
                        for kt in range(8):
                            c.mm(pb[:, 0:n], wb[:, kt, cb * 128:(cb + 1) * 128], yk_fm[:, kt, t0:t0 + n], start=(kt == 0), stop=(kt == 7))
                        sg = stg_b.next()
                        c.act(sg[:, 0:n], pb[:, 0:n], AF.Sigmoid, bias=bgate[:, jb:jb + 1])
                        zt = stg_b.next()
                        c.dma('sp', zt[:, 0:n], sfm[R_ZS5 + jb * 128:R_ZS5 + (jb + 1) * 128, t0:t0 + n])
                        c.tt(sg[:, 0:n], sg[:, 0:n], zt[:, 0:n], ALU.mult)
                        c.tt(sg[:, 0:n], sg[:, 0:n], yk_fm[:, jb, t0:t0 + n], ALU.mult)
                        c.dma('sp', sfm[R_U + jb * 128:R_U + (jb + 1) * 128, t0:t0 + n], sg[:, 0:n])
            c.dma('sp', bgate[:], IN("b_gate_c")[l])
            for jb in range(8):
                c.dma('sp', yk_fm[:, jb, :], sfm[R_U + jb * 128:R_U + (jb + 1) * 128, :])
            dump_fm("y_b", yk_fm)

        def mlstm_branch(l):
            bes = contextlib.ExitStack()
            uid = "_c%d" % l
            anames = []

            def A(name, shape, dt=F32):
                anames.append(name + uid)
                return bes.enter_context(nc.sbuf_tensor(name + uid, list(shape), dt))

            hp4 = A("hp4", [4, 8])
            gml = A("gml", [128, D])
            q_st = A("q_st", [128, 8, 512], BF16)
            k_st = A("k_st", [128, 8, 512], BF16)
            Cst = A("Cst", [128, 4, 2, 257])
            Cbf = A("Cbf", [128, 4, 2, 257], BF16)
            mprev = A("mprev", [4, 16])
            mprev_s = A("mprev_s", [4, 16])
            g4 = Rot([A("g4_%d" % i, [4, 128]) for i in range(14)])
            gtm_r = Rot([A("mgtm%d" % i, [128, 128]) for i in range(2)])
            vaug_r = Rot([A("vaug%d" % i, [128, 4, 257], BF16) for i in range(2)])
            qs_r = Rot([A("qs%d" % i, [128, 2, 128], BF16) for i in range(2)])
            kw_r = Rot([A("kw%d" % i, [128, 256], BF16) for i in range(2)])
            hc_r = Rot([A("hc%d" % i, [128, 256]) for i in range(2)])
            dec_sb = A("dec_sb", [128, 64])
            dg = A("dg", [4, 64])
            Cb_r = Rot([A("Cb%d" % i, [128, 2, 257]) for i in range(2)])
            Cbb_r = Rot([A("Cbb%d" % i, [128, 2, 257], BF16) for i in range(2)])
            qsb_r = Rot([A("qsb%d" % i, [128, 2, 64], BF16) for i in range(2)])
            for v in vaug_r.bufs:
                c.memset(v[:, :, 256:257], 1.0)
            c.dma('sp', hp4[:, 0:2], IN("ml_hp")[l])
            c.ts(hp4[:, 2:3], hp4[:, 1:2], -1.0, None, op0=ALU.mult)
            c.dma('sp', gml[:], IN("ml_norm")[l:l + 1, :].partition_broadcast(128))
            c.memset(Cst[:, :, :, :].rearrange("p a b c -> p (a b c)"), 0.0)
            c.memset(Cbf[:, :, :, :].rearrange("p a b c -> p (a b c)"), 0.0)
            c.memset(mprev[:, 0:1], 0.0)
            bi, nbf = hp4[:, 0:1], hp4[:, 2:3]
            pQK, pBC, pN0, pN1, pC0, pC1, pT, pBC2 = banks
            pn_r = Rot([pN0, pN1])
            for si, (s0, n) in enumerate(STS):
                sample = (si == 4)
                c.dma('sp', q_st[:, :, 0:n], sfm[R_Q:R_Q + 1024, s0:s0 + n].rearrange("(j p) t -> p j t", p=128))
                c.dma('sp', k_st[:, :, 0:n], sfm[R_K:R_K + 1024, s0:s0 + n].rearrange("(j p) t -> p j t", p=128))
                if sample:
                    c.dma('sp', mprev_s[:, :], IN("st_mlm")[l])
                for (t0, P) in [t for t in TILES if s0 <= t[0] < s0 + n]:
                    lo = t0 - s0
                    mbig = cmask['maskbig_s'] if sample else cmask['maskbig_p']
                    ir, fr, e1, lnp, bcn, a_, cm, mx, wi, ml_, enm, wen = [g4.next() for _ in range(12)]
                    c.dma('sp', ir[:, 0:P], g_scr[16:20, t0:t0 + P])
                    c.dma('sp', fr[:, 0:P], g_scr[20:24, t0:t0 + P])
                    c.act(e1[:, 0:P], fr[:, 0:P], AF.Exp, bias=nbf, scale=-1.0)
                    c.act(lnp[:, 0:P], e1[:, 0:P], AF.Ln, bias=1.0)
                    if not sample:
                        c.scan(bcn[:, 0:P], ones_f[0:4, 0:P], lnp[:, 0:P], 0.0, ALU.mult, ALU.add)
                        c.stt(a_[:, 0:P], ir[:, 0:P], bi, bcn[:, 0:P], ALU.add, ALU.add)
                        c.scan(cm[:, 0:P], a_[:, 0:P], a_[:, 0:P], -1e30, ALU.max, ALU.max)
                        c.ts(mx[:, 0:P], cm[:, 0:P], mprev[:, 0:1], None, op0=ALU.max)
                        c.act(wi[:, 0:P], mx[:, 0:P], AF.Exp, bias=mprev[:, 0:1], scale=-1.0)
                        c.tt(ml_[:, 0:P], mx[:, 0:P], bcn[:, 0:P], ALU.subtract)
                        c.act(enm[:, 0:P], ml_[:, 0:P], AF.Exp, scale=-1.0)
                        nml = g4.next()
                        c.ts(nml[:, 0:1], mx[:, P - 1:P], -1.0, None, op0=ALU.mult)
                        c.act(wen[:, 0:P], a_[:, 0:P], AF.Exp, bias=nml[:, 0:1])
                    else:
                        v3 = lambda tl: tl[:, 0:64].rearrange("p (b t) -> p b t", t=4)
                        c.copy(v3(bcn)[:, :, 0:1], v3(lnp)[:, :, 0:1])
                        for t in (1, 2, 3):
                            c.tt(v3(bcn)[:, :, t:t + 1], v3(bcn)[:, :, t - 1:t], v3(lnp)[:, :, t:t + 1], ALU.add)
                        c.stt(a_[:, 0:P], ir[:, 0:P], bi, bcn[:, 0:P], ALU.add, ALU.add)
                        c.copy(v3(cm)[:, :, 0:1], v3(a_)[:, :, 0:1])
                        for t in (1, 2, 3):
                            c.tt(v3(cm)[:, :, t:t + 1], v3(cm)[:, :, t - 1:t], v3(a_)[:, :, t:t + 1], ALU.max)
                        mpb = mprev_s[:, :].unsqueeze(2).to_broadcast([4, 16, 4])
                        c.tt(v3(mx), v3(cm), mpb, ALU.max)
                        c.tt(v3(wi), mpb, v3(mx), ALU.subtract)
                        c.act(wi[:, 0:P], wi[:, 0:P], AF.Exp)
                        c.tt(ml_[:, 0:P], mx[:, 0:P], bcn[:, 0:P], ALU.subtract)
                        c.act(enm[:, 0:P], ml_[:, 0:P], AF.Exp, scale=-1.0)
                        c.tt(v3(wen), v3(a_), v3(mx)[:, :, 3:4].to_broadcast([4, 16, 4]), ALU.subtract)
                        c.act(wen[:, 0:P], wen[:, 0:P], AF.Exp)
                    for qi, qsrc in enumerate((a_, wi, enm, wen)):
                        c.tr(pT[0:P, 32 * qi:32 * qi + 4], qsrc[:, 0:P], ident[0:4, 0:4])
                    gtm = gtm_r.next()
                    c.copy(gtm[0:P, :].rearrange("p (q x) -> p q x", x=32)[:, :, 0:4], pT[0:P, 0:128].rearrange("p (q x) -> p q x", x=32)[:, :, 0:4])
                    vaug = vaug_r.next()
                    c.dma('sp', vaug[0:P, :, 0:256], stm[t0:t0 + P, C_V:C_V + 1024].rearrange("t (h e) -> t h e", e=256))
                    ktm = b2.next()
                    c.dma('sp', ktm[0:P, :], stm[t0:t0 + P, C_KT:C_KT + 1024])
                    otm = b2.next()
                    c.dma('sp', otm[0:P, :], stm[t0:t0 + P, C_O:C_O + 1024])
                    ztm = b2.next()
                    c.dma('sp', ztm[0:P, :], stm[t0:t0 + P, C_ZML:C_ZML + 1024])
                    yc = b2.next()
                    if sample:
                        wv = v3(wi)[:, :, 3:4]
                        dgv = dg[:, :].rearrange("p (b h) -> p b h", h=4)
                        c.tt(dgv, wv.to_broadcast([4, 16, 4]), ident[0:4, 0:4].unsqueeze(1).to_broadcast([4, 16, 4]), ALU.mult)
                        c.mm(pBC2[:, 0:64], ones_f[0:4, :], dg[:, :])
                        c.copy(dec_sb[:, :], pBC2[:, 0:64])
                    for hh in range(4):
                        for j in range(2):
                            c.mm(pQK[0:P, 0:P], k_st[:, 2 * hh + j, lo:lo + P], q_st[:, 2 * hh + j, lo:lo + P], start=(j == 0), stop=(j == 1))
                        selm = g4.next()
                        c.ts(selm[:, 0:P], mx[:, 0:P], ident[0:4, hh:hh + 1], None, op0=ALU.mult)
                        c.mm(pBC[0:P, 0:P], ones_f[0:4, 0:P], selm[:, 0:P])
                        tsb = fq.next()
                        c.stt(tsb[0:P, 0:P], pBC[0:P, 0:P], gtm[0:P, hh:hh + 1], mbig[0:P, 0:P], ALU.subtract, ALU.max)
                        E = fq.next()
                        c.act(E[0:P, 0:P], tsb[0:P, 0:P], AF.Exp, scale=-1.0)
                        WT = bq.next()
                        c.tt(WT[0:P, 0:P], E[0:P, 0:P], pQK[0:P, 0:P], ALU.mult)
                        selw = g4.next()
                        c.ts(selw[:, 0:P], wi[:, 0:P], ident[0:4, hh:hh + 1], None, op0=ALU.mult)
                        c.mm(pBC2[:, 0:P], ones_f[0:4, :], selw[:, 0:P])
                        qs = qs_r.next()
                        for j in range(2):
                            c.tt(qs[:, j, 0:P], q_st[:, 2 * hh + j, lo:lo + P], pBC2[:, 0:P], ALU.mult)
                        pn = pn_r.next()
                        c.mm(pn[0:P, 0:257], WT[0:P, 0:P], vaug[0:P, hh, :], start=True, stop=False)
                        if not sample:
                            for j in range(2):
                                c.mm(pn[0:P, 0:257], qs[:, j, 0:P], Cbf[:, hh, j, :], start=False, stop=(j == 1))
                        else:
                            kw = kw_r.next()
                            for b in range(NB):
                                Cb = Cb_r.next()
                                Cbb = Cbb_r.next()
                                src_c = IN("st_mlc")[l, b, hh * 256:(hh + 1) * 256, :].rearrange("(j p) e -> p j e", p=128)
                                src_n = IN("st_mln")[l, b, hh * 256:(hh + 1) * 256].rearrange("(j p o) -> p j o", p=128, o=1)
                                c.dma('sp', Cb[:, :, 0:256], src_c)
                                c.dma('sp', Cb[:, :, 256:257], src_n, allow_slow_non_contiguous=True)
                                c.copy(Cbb[:, :, :], Cb[:, :, :], eng='act')
                                qsb = qsb_r.next()
                                c.tt(qsb[:, :, :], qs[:, :, 0:64], colmask[:, b * 64:(b + 1) * 64].unsqueeze(1).to_broadcast([128, 2, 64]), ALU.mult)
                                for j in range(2):
                                    c.mm(pn[0:P, 0:257], qsb[:, j, :], Cbb[:, j, :], start=False, stop=(b == NB - 1 and j == 1))
                                c.ts(kw[0:64, :], ktm[0:64, hh * 256:(hh + 1) * 256], gtm[0:64, 96 + hh:97 + hh], cmask['rowmask'][0:64, b:b + 1], op0=ALU.mult, op1=ALU.mult)
                                for j, pc in enumerate((pC0, pC1)):
                                    c.mm(pc[:, 0:257], kw[0:64, j * 128:(j + 1) * 128], vaug[0:64, hh, :])
                                for j, pc in enumerate((pC0, pC1)):
                                    c.stt(Cb[:, j, :], Cb[:, j, :], dec_sb[:, b * 4 + hh:b * 4 + hh + 1], pc[:, 0:257], ALU.mult, ALU.add)
                                c.dma('sp', mc_so[l, b, hh * 256:(hh + 1) * 256, :].rearrange("(j p) e -> p j e", p=128), Cb[:, :, 0:256])
                                c.dma('sp', mn_so[l, b, hh * 256:(hh + 1) * 256].rearrange("(j p o) -> p j o", p=128, o=1), Cb[:, :, 256:257], allow_slow_non_contiguous=True)
                        st = st_small.next()
                        c.act(st[0:P, 2:3], pn[0:P, 256:257], AF.Abs)
                        c.ts(st[0:P, 0:1], st[0:P, 2:3], gtm[0:P, 64 + hh:65 + hh], None, op0=ALU.max)
                        c.recip(st[0:P, 1:2], st[0:P, 0:1])
                        hc = hc_r.next()
                        c.stt(hc[0:P, :], pn[0:P, 0:256], st[0:P, 1:2], otm[0:P, hh * 256:(hh + 1) * 256], ALU.mult, ALU.mult)
                        rstd = rms_rstd(hc[0:P, :], P, n=256)
                        c.stt(hc[0:P, :], hc[0:P, :], rstd, gml[0:P, hh * 256:(hh + 1) * 256], ALU.mult, ALU.mult)
                        c.tt(yc[0:P, hh * 256:(hh + 1) * 256], hc[0:P, :], ztm[0:P, hh * 256:(hh + 1) * 256], ALU.mult)
                    tm_to_fm(yc, P, yk_fm[:, :, t0:t0 + P])
                    if not sample:
                        c.ts(dg[:, 0:4], ident[0:4, 0:4], wi[:, P - 1:P], None, op0=ALU.mult)
                        c.mm(pBC2[:, 0:4], ones_f[0:4, :], dg[:, 0:4])
                        c.copy(dec_sb[:, 0:4], pBC2[:, 0:4])
                        for hh in range(4):
                            kw = kw_r.next()
                            c.ts(kw[0:P, :], ktm[0:P, hh * 256:(hh + 1) * 256], gtm[0:P, 96 + hh:97 + hh], None, op0=ALU.mult)
                            for j, pc in enumerate((pC0, pC1)):
                                c.mm(pc[:, 0:257], kw[0:P, j * 128:(j + 1) * 128], vaug[0:P, hh, :])
                            for j, pc in enumerate((pC0, pC1)):
                                c.stt(Cst[:, hh, j, :], Cst[:, hh, j, :], dec_sb[:, hh:hh + 1], pc[:, 0:257], ALU.mult, ALU.add)
                        c.copy(Cbf[:, :, :, :].rearrange("p a b c -> p (a b c)"), Cst[:, :, :, :].rearrange("p a b c -> p (a b c)"), eng='act')
                        c.copy(mprev[:, 0:1], ml_[:, P - 1:P])
                    else:
                        mlast = g4.next()
                        c.copy(mlast[:, 0:16].unsqueeze(2), v3(ml_)[:, :, 3:4])
                        c.dma('sp', mm_so[l], mlast[:, 0:16])
            for hh in range(4):
                c.dma('sp', mc_po[l, hh * 256:(hh + 1) * 256, :].rearrange("(j p) e -> p j e", p=128), Cst[:, hh, :, 0:256])
                c.dma('sp', mn_po[l, hh * 256:(hh + 1) * 256].rearrange("(j p o) -> p j o", p=128, o=1), Cst[:, hh, :, 256:257], allow_slow_non_contiguous=True)
            c.dma('sp', mm_po[l].rearrange("(h o) -> h o", o=1), mprev[:, 0:1])
            dump_fm("y_c", yk_fm)
            c.barrier()
            c.release(anames)
            bes.close()

        def xattn_branch(l):
            bes = contextlib.ExitStack()
            uid = "_d%d" % l
            anames = []

            def A(name, shape, dt=F32):
                anames.append(name + uid)
                return bes.enter_context(nc.sbuf_tensor(name + uid, list(shape), dt))

            mk_fm = A("mk_fm", [128, 8, 256], BF16)
            mv_tm = A("mv_tm", [128, 2, D], BF16)
            memkv(l, mk_fm, mv_tm)
            qx_st = A("qx_st", [128, 8, 512], BF16)
            p_r = Rot([A("pp%d" % i, [128, 256], BF16) for i in range(2)])
            pT_r = Rot([A("ppT%d" % i, [128, 2, 128], BF16) for i in range(5)])
            kl_r = Rot([A("kl%d" % i, [128, 2, D]) for i in range(2)])
            mkT_r = Rot([A("mkT%d" % i, [128, 8, 256], BF16) for i in range(2)])
            mvb_r = Rot([A("mvb%d" % i, [128, 2, D], BF16) for i in range(2)])
            qm_r = Rot([A("qm%d" % i, [128, 8, 64], BF16) for i in range(2)])
            pTm_r = Rot([A("pTm%d" % i, [128, 2, 64], BF16) for i in range(3)])
            pS = banks[0:4]
            pTr = banks[4:6]
            pV = Rot(banks[6:8])

            def softmax_T(sc, P):
                st = st_small.next()
                c.reduce(st[0:P, 0:1], sc[0:P, 0:256], ALU.max)
                c.ts(st[0:P, 1:2], st[0:P, 0:1], -1.0, None, op0=ALU.mult)
                p = p_r.next()
                c.act(p[0:P, :], sc[0:P, 0:256], AF.Exp, bias=st[0:P, 1:2], accum_out=st[0:P, 2:3])
                c.recip(st[0:P, 3:4], st[0:P, 2:3])
                pv = bfv(pTr[0])
                for mt in range(2):
                    c.tr(pv[:, mt * 128:mt * 128 + P], p[0:P, mt * 128:(mt + 1) * 128], ident_b[0:P, 0:P])
                pT = pT_r.next()
                c.copy(pT[:, :, 0:P], pv[:, 0:256].rearrange("p (m t) -> p m t", t=128)[:, :, 0:P], eng='act')
                return pT, st[0:P, 3:4]

            for si, (s0, n) in enumerate(STS):
                sample = (si == 4)
                c.dma('sp', qx_st[:, :, 0:n], sfm[R_QXA:R_QXA + 1024, s0:s0 + n].rearrange("(j p) t -> p j t", p=128))
                for (t0, P) in [t for t in TILES if s0 <= t[0] < s0 + n]:
                    lo = t0 - s0
                    ztm = b2.next()
                    c.dma('sp', ztm[0:P, :], stm[t0:t0 + P, C_ZXA:C_ZXA + 1024])
                    yd = b2.next()
                    if not sample:
                        for hh in range(4):
                            sc = pS[hh % 2]
                            for j in range(2):
                                c.mm(sc[0:P, 0:256], qx_st[:, 2 * hh + j, lo:lo + P], mk_fm[:, 2 * hh + j, :], start=(j == 0), stop=(j == 1))
                            pT, rinv = softmax_T(sc, P)
                            pvb = pV.next()
                            for mt in range(2):
                                c.mm(pvb[0:P, 0:256], pT[:, mt, 0:P], mv_tm[:, mt, hh * 256:(hh + 1) * 256], start=(mt == 0), stop=(mt == 1))
                            c.stt(yd[0:P, hh * 256:(hh + 1) * 256], pvb[0:P, 0:256], rinv, ztm[0:P, hh * 256:(hh + 1) * 256], ALU.mult, ALU.mult)
                    else:
                        for b in range(NB):
                            kl = kl_r.next()
                            c.dma('sp', kl[:, :, :], IN("ck")[l, b].rearrange("(mt p) d -> p mt d", p=128))
                            mkT = mkT_r.next()
                            for rnd in range(2):
                                for q4 in range(8):
                                    blk = rnd * 4 + q4 // 2
                                    mt = q4 % 2
                                    c.tr(pTr[q4 // 4][:, (q4 % 4) * 128:(q4 % 4 + 1) * 128], kl[:, mt, blk * 128:(blk + 1) * 128], ident[:, :])
                                for hf in range(2):
                                    blks = mkT[:, rnd * 4 + 2 * hf:rnd * 4 + 2 * hf + 2, :]
                                    src = pTr[hf][:, :].rearrange("p (b m) -> p b m", m=256)
                                    if hf == 0:
                                        c.act(blks, src, AF.Copy, scale=1.0 / 16)
                                    else:
                                        c.ts(blks, src, 1.0 / 16, None, op0=ALU.mult)
                            qm = qm_r.next()
                            c.tt(qm[:, :, :], qx_st[:, :, 0:64], colmask[:, b * 64:(b + 1) * 64].unsqueeze(1).to_broadcast([128, 8, 64]), ALU.mult)
                            for hh in range(4):
                                for j in range(2):
                                    c.mm(pS[hh][0:64, 0:256], qm[:, 2 * hh + j, :], mkT[:, 2 * hh + j, :],
                                         start=(b == 0 and j == 0), stop=(b == NB - 1 and j == 1))
                        pTs, rinvs = [], []
                        for hh in range(4):
                            pT, rinv = softmax_T(pS[hh], 64)
                            pTs.append(pT)
                            rinvs.append(rinv)
                        for b in range(NB):
                            mvb = mvb_r.next()
                            c.dma('pool', mvb[:, :, :], IN("cv")[l, b].rearrange("(mt p) d -> p mt d", p=128))
                            for hh in range(4):
                                pTm = pTm_r.next()
                                c.tt(pTm[:, :, :], pTs[hh][:, :, 0:64], colmask[:, b * 64:(b + 1) * 64].unsqueeze(1).to_broadcast([128, 2, 64]), ALU.mult)
                                for mt in range(2):
                                    c.mm(pS[hh][0:64, 0:256], pTm[:, mt, :], mvb[:, mt, hh * 256:(hh + 1) * 256],
                                         start=(b == 0 and mt == 0), stop=(b == NB - 1 and mt == 1))
                        for hh in range(4):
                            c.stt(yd[0:64, hh * 256:(hh + 1) * 256], pS[hh][0:64, 0:256], rinvs[hh], ztm[0:64, hh * 256:(hh + 1) * 256], ALU.mult, ALU.mult)
                    tm_to_fm(yd, P, yk_fm[:, :, t0:t0 + P])
            dump_fm("y_d", yk_fm)
            c.barrier()
            c.release(anames)
            bes.close()

        merged = h_fm

        def merge_branch(l, k, first):
            for c0 in (0, 512):
                wb = load_w(IN("w_down")[l, k, :, c0:c0 + 512])
                for cb in range(4):
                    jb = c0 // 128 + cb
                    for (t0, n) in STS:
                        pb = pbank.next()
                        for kt in range(8):
                            c.mm(pb[:, 0:n], wb[:, kt, cb * 128:(cb + 1) * 128], yk_fm[:, kt, t0:t0 + n], start=(kt == 0), stop=(kt == 7))
                        gt = stg_b.next()
                        r0 = R_GATE + k * 1024 + jb * 128
                        c.dma('sp', gt[:, 0:n], sfm[r0:r0 + 128, t0:t0 + n])
                        if first:
                            c.tt(merged[:, jb, t0:t0 + n], gt[:, 0:n], pb[:, 0:n], ALU.mult)
                        else:
                            tmp = stg_f.next()
                            c.tt(tmp[:, 0:n], gt[:, 0:n], pb[:, 0:n], ALU.mult)
                            c.tt(merged[:, jb, t0:t0 + n], merged[:, jb, t0:t0 + n], tmp[:, 0:n], ALU.add, eng='pool')

        def outproj(l):
            wos = [load_w(IN("w_out")[l, :, c0:c0 + 512]) for c0 in (0, 512)]
            for tt, (t0, P) in enumerate(TILES):
                xt = f4.next()
                c.dma('sp', xt[0:P, :], x_src(l, tt))
                for half in range(2):
                    pb = pbank.next()
                    for kt in range(8):
                        c.mm(pb[0:P, :], merged[:, kt, t0:t0 + P], wos[half][:, kt, :], start=(kt == 0), stop=(kt == 7))
                    c.tt(xt[0:P, half * 512:(half + 1) * 512], xt[0:P, half * 512:(half + 1) * 512], pb[0:P, :], ALU.add)
                c.dma('sp', x_scr[t0:t0 + P, :], xt[0:P, :])


        for l in range(depth):
            phase0(l)
            projections(l)
            if stage < 5 and l == 0 and 'memkv' not in SKIP:
                memkv(l, c.sb("mk_fm_t", [128, 8, 256], BF16), c.sb("mv_tm_t", [128, 2, D], BF16))
            full = stage >= 6
            first = True
            for (stg, k, fn) in ((2, 0, ssd_branch), (3, 1, s5_branch), (4, 2, mlstm_branch), (5, 3, xattn_branch)):
                if stage >= stg and ("br%d" % k) not in SKIP:
                    fn(l)
                    if full:
                        merge_branch(l, k, first)
                        first = False
            if full:
                dump_fm("merged", merged)
                outproj(l)

        gfin = c.sb("gfin", [128, D], F32)
        c.dma('sp', gfin[:], IN("final_norm").partition_broadcast(128))
        for tt, (t0, P) in enumerate(TILES):
            xt = f4.next()
            c.dma('sp', xt[0:P, :], x_scr[t0:t0 + P, :] if stage >= 6 else x_src(0, tt))
            rstd = rms_rstd(xt[0:P, :], P)
            c.stt(xt[0:P, :], xt[0:P, :], rstd, gfin[0:P, :], ALU.mult, ALU.mult)
            if tt < 16:
                c.dma('sp', y_p[t0:t0 + P, :], xt[0:P, :])
            else:
                c.dma('sp', y_s[:, :], xt[0:P, :])
        c.finish()
        print("[build] instructions=%d waits=%d per-engine=%s dma_sems=%d sbuf_left=%s" % (
            c.n_inst, c.n_wait, c.cnt, len(c.dma_sems), nc.sbuf_bytes_remaining() if callable(nc.sbuf_bytes_remaining) else nc.sbuf_bytes_remaining))
    return nc


def make_in_maps(inp, used=None):
    cs = make_consts()
    f32 = lambda a: np.ascontiguousarray(a, np.float32)
    shared = {
        "w_kv": lambda: f32(inp["w_mem_kv"]),
        "w_down": lambda: f32(inp["w_down"]),
        "w_out": lambda: f32(inp["w_out"]),
        "norm_in_c": lambda: colmaj(inp["norm_in"]),
        "mem_norm_c": lambda: colmaj(inp["mem_norm"]),
        "final_norm": lambda: f32(inp["final_norm"]).reshape(1, D),
        "b_gate_c": lambda: colmaj(inp["b_gate"]),
        "c_sel16": lambda: cs['sel16'],
        "c_colmask": lambda: cs['colmask'],
        "conv_w_c": lambda: f32(np.transpose(f32(inp["ssd_conv_w"]).reshape(DEPTH, 4, 12, 128), (0, 3, 2, 1)).reshape(DEPTH, 128, 48)),
        "conv_b_c": lambda: colmaj(inp["ssd_conv_b"]),
        "ssd_hp": lambda: f32(np.stack([f32(inp["ssd_dt_bias"]), f32(inp["ssd_a_log"])], axis=-1)),
        "ssd_d": lambda: f32(inp["ssd_d"]),
        "ssd_norm": lambda: f32(inp["ssd_norm"]),
        "ml_hp": lambda: f32(np.stack([f32(inp["b_igate"]), f32(inp["b_fgate"])], axis=-1)),
        "ml_norm": lambda: f32(inp["ml_norm"]).reshape(DEPTH, D),
        "s5_a_re_t": lambda: f32(np.transpose(f32(inp["s5_a_re"]), (0, 2, 1))),
        "s5_a_im_t": lambda: f32(np.transpose(f32(inp["s5_a_im"]), (0, 2, 1))),
        "s5_log_dt": lambda: f32(inp["s5_log_dt"]),
        "s5_b_re_t": lambda: f32(np.transpose(f32(inp["s5_b_re"]), (0, 2, 1, 3))).reshape(DEPTH, 64, 1024),
        "s5_b_im_t": lambda: f32(np.transpose(f32(inp["s5_b_im"]), (0, 2, 1, 3))).reshape(DEPTH, 64, 1024),
        "s5_c_re_t": lambda: f32(np.transpose(f32(inp["s5_c_re"]), (0, 3, 1, 2))).reshape(DEPTH, 64, 1024),
        "s5_c_im_t": lambda: f32(np.transpose(f32(inp["s5_c_im"]), (0, 3, 1, 2))).reshape(DEPTH, 64, 1024),
        "s5_d_c": lambda: colmaj(inp["s5_d"]),
        "s5_glu_w": lambda: f32(inp["s5_glu_w"]),
        "s5_glu_b_c": lambda: colmaj(inp["s5_glu_b"]),
    }
    for k in CONST_ORDER:
        shared["c_" + k] = (lambda k=k: cs[k])
    for i in range(DEPTH):
        shared["w_in%d" % i] = (lambda i=i: f32(inp["w_in"][i]))
    percore = {
        "x_p": lambda ci, b0: f32(inp["x_prompt"][ci]),
        "x_s": lambda ci, b0: f32(inp["x_sample"][b0:b0 + NB]).reshape(TS, D),
        "mem": lambda ci, b0: f32(inp["mem_prompt"][ci]),
        "ck": lambda ci, b0: f32(inp["cache_mem_k"][:, b0:b0 + NB]).reshape(DEPTH, NB, 256, D),
        "cv": lambda ci, b0: f32(inp["cache_mem_v"][:, b0:b0 + NB]).reshape(DEPTH, NB, 256, D),
        "st_conv": lambda ci, b0: f32(inp["state_ssd_conv"][:, b0:b0 + NB]).reshape(DEPTH, NB * 3, 1536),
        "st_ssd": lambda ci, b0: f32(inp["state_ssd"][:, b0:b0 + NB]).reshape(DEPTH, NB, 1024, 128),
        "st_mlc": lambda ci, b0: f32(inp["state_mlstm_c"][:, b0:b0 + NB]).reshape(DEPTH, NB, 1024, 256),
        "st_mln": lambda ci, b0: f32(inp["state_mlstm_n"][:, b0:b0 + NB]).reshape(DEPTH, NB, 1024),
        "st_mlm": lambda ci, b0: f32(np.transpose(f32(inp["state_mlstm_m"][:, b0:b0 + NB]), (0, 2, 1))),
        "st_s5_t": lambda ci, b0: f32(np.transpose(np.stack([f32(inp["state_s5_re"][:, b0:b0 + NB]), f32(inp["state_s5_im"][:, b0:b0 + NB])], axis=1), (0, 4, 1, 3, 2))),
    }
    names = set(shared) | set(percore)
    if used is not None:
        names &= set(used)
    sh = {k: shared[k]() for k in names if k in shared}
    maps = []
    for ci in range(NCORES):
        m = dict(sh)
        for k in names:
            if k in percore:
                m[k] = percore[k](ci, ci * NB)
        maps.append(m)
    return maps


_NC_CACHE = {}


def run(inp, depth=DEPTH, stage=99, dbg=False):
    key = (depth, stage, dbg)
    if key not in _NC_CACHE:
        _NC_CACHE[key] = build(depth, stage, dbg)
    nc = _NC_CACHE[key]
    maps = make_in_maps(inp, used=set(nc._used_inputs.keys()))
    res = run_bass_kernel_spmd(nc, maps, core_ids=list(range(NCORES)))
    return res.results


def kernel(**inputs):
    r = run(inputs)
    L = DEPTH
    pst = lambda k, shp: np.stack([np.asarray(x[k], np.float32).reshape((L,) + shp) for x in r], axis=1)
    sst = lambda k, shp: np.concatenate([np.asarray(x[k], np.float32).reshape((L, NB) + shp) for x in r], axis=1)
    y_prompt = np.stack([x["y_p"] for x in r], 0)
    y_sample = np.concatenate([x["y_s"].reshape(NB, 4, D) for x in r], 0)
    mm_s = np.concatenate([np.transpose(np.asarray(x["mm_so"], np.float32), (0, 2, 1)) for x in r], axis=1)
    return (y_prompt, y_sample,
            pst("mk_o", (256, 4, 256)), pst("mv_o", (256, 4, 256)),
            pst("conv_po", (3, 1536)), pst("ssd_po", (16, 64, 128)),
            pst("s5re_po", (64, 64)), pst("s5im_po", (64, 64)),
            pst("mc_po", (4, 256, 256)), pst("mn_po", (4, 256)), pst("mm_po", (4,)),
            sst("conv_so", (3, 1536)), sst("ssd_so", (16, 64, 128)),
            sst("s5re_so", (64, 64)), sst("s5im_so", (64, 64)),
            sst("mc_so", (4, 256, 256)), sst("mn_so", (4, 256)), mm_s)
```

```python
import contextlib
import numpy as np
import concourse.bass as bass
import concourse.mybir as mybir
from concourse.bass_utils import run_bass_kernel_spmd

F32 = mybir.dt.float32
BF16 = mybir.dt.bfloat16
AF = mybir.ActivationFunctionType
ALU = mybir.AluOpType
AX = mybir.AxisListType

import os
SKIP = os.environ.get('K_SKIP', '').split(',')
NCORES = 8
D = 1024
DEPTH = 4
TP = 2048
TS = 64
NT = TP + TS
NB = 16
EPS = 1e-6
D_IN = 15896
O_ZSSD, O_XBC, O_DT, O_U, O_ZS5, O_Q, O_K, O_V, O_I, O_F, O_O, O_ZML, O_QXA, O_ZXA, O_GATE = (
    0, 1024, 2560, 2576, 3600, 4624, 5648, 6672, 7696, 7700, 7704, 8728, 9752, 10776, 11800)
TILES = [(128 * i, 128) for i in range(16)] + [(2048, 64)]
STS = [(512 * i, 512) for i in range(4)] + [(2048, 64)]
R_XBC, R_U, R_Q, R_K, R_QXA, R_ZS5, R_GATE = 0, 1536, 2560, 3584, 4608, 5632, 6656
NFM = 6656 + 4096
C_ZSSD, C_V, C_KT, C_O, C_ZML, C_ZXA = 0, 1024, 2048, 3072, 4096, 5120
NTM = 6144


class S:
    def __init__(self, ap, sub):
        self.ap = ap
        self.sub = sub


def _unw(a):
    if isinstance(a, S):
        return a.ap, a.sub
    return a, None


class Ctx:
    def __init__(self, nc):
        self.nc = nc
        self.es = contextlib.ExitStack()
        self.eng = {'pe': nc.tensor, 'act': nc.scalar, 'dve': nc.vector, 'pool': nc.gpsimd, 'sp': nc.sync}
        self.sem = {}
        self.cnt = {}
        self.semobj = {}
        for e in ('pe', 'act', 'dve', 'pool'):
            self.sem[e] = self.es.enter_context(nc.semaphore("sem_" + e))
            self.cnt[e] = 0
            self.semobj["sem_" + e] = self.sem[e]
        self.know = {e: {} for e in self.eng}
        self.reg = {}
        self.vcs = {}
        self.dma_sems = {}
        self.n_wait = 0
        self.n_inst = 0

    def sb(self, name, shape, dt=F32):
        return self.es.enter_context(self.nc.sbuf_tensor(name, list(shape), dt))

    def ps(self, name, shape, dt=F32):
        return self.es.enter_context(self.nc.psum_tensor(name, list(shape), dt))

    def dsem(self, name):
        if name not in self.dma_sems:
            if getattr(self, "free_sems", None):
                s, base = self.free_sems.pop()
            else:
                s, base = self.es.enter_context(self.nc.semaphore("d_%d" % len(self.semobj))), 0
            self.dma_sems[name] = [s, base]
            self.semobj["d_" + name] = s
        return self.dma_sems[name]

    def release(self, names):
        if not hasattr(self, "free_sems"):
            self.free_sems = []
        for n in names:
            for nm in (n, n + "_sw"):
                if nm in self.dma_sems:
                    s, v = self.dma_sems.pop(nm)
                    self.free_sems.append((s, v))

    def _deps(self, name, sub, is_write, eng=None):
        st = self.reg.get(name)
        if not st:
            return []
        keys = list(st.keys()) if sub is None else [k for k in (sub, None) if k in st]
        toks = []
        psum = name.startswith("pb")
        for k in keys:
            w, r = st[k]
            if w is not None:
                toks.append(w)
            if is_write:
                toks.extend(r.items())
            elif psum:
                toks.extend((s, v) for (s, v) in r.items() if s != "sem_" + str(eng))
        return toks

    def _record(self, name, sub, is_write, tok):
        st = self.reg.setdefault(name, {})
        if is_write:
            if sub is None:
                st.clear()
            st[sub] = [tok, {}]
        else:
            ent = st.setdefault(sub, [None, {}])
            s, v = tok
            if ent[1].get(s, -1) < v:
                ent[1][s] = v

    def _sync(self, e, reads, writes):
        toks = []
        for (n, s) in reads:
            toks += self._deps(n, s, False, e)
        for (n, s) in writes:
            toks += self._deps(n, s, True)
        need = {}
        kn = self.know[e]
        for (sname, v) in toks:
            if e == 'pe' and sname == 'sem_pe':
                continue
            if kn.get(sname, 0) >= v:
                continue
            if need.get(sname, 0) < v:
                need[sname] = v
        for sname, v in need.items():
            if sname.startswith("d_"):
                v = max(v, self.dma_sems[sname[2:]][1])
            if kn.get(sname, 0) >= v:
                continue
            self.eng[e].wait_ge(self.semobj[sname], v)
            self.n_wait += 1
            kn[sname] = v
            vc = self.vcs.get((sname, v))
            if vc:
                for k2, v2 in vc.items():
                    if kn.get(k2, 0) < v2:
                        kn[k2] = v2

    def _keys(self, aps):
        out = []
        for a in aps:
            if a is None:
                continue
            ap, sub = _unw(a)
            if not hasattr(ap, 'tensor'):
                continue
            out.append((ap.tensor.name, sub))
        return out

    def op(self, e, fn, reads, writes):
        rk = self._keys(reads)
        wk = self._keys(writes)
        self._sync(e, rk, wk)
        inst = fn(self.eng[e])
        self.cnt[e] += 1
        inst.then_inc(self.sem[e], 1)
        self.n_inst += 1
        sname = "sem_" + e
        tok = (sname, self.cnt[e])
        vc = dict(self.know[e])
        vc[sname] = self.cnt[e]
        self.vcs[tok] = vc
        for (n, s) in rk:
            self._record(n, s, False, tok)
        for (n, s) in wk:
            self._record(n, s, True, tok)
        return tok

    def dma(self, q, out, in_, slot=None, **kw):
        o, _ = _unw(out)
        i, _ = _unw(in_)
        rk = self._keys([in_])
        wk = self._keys([out])
        self._sync(q, rk, wk)
        if slot is None:
            slot = o.tensor.name if 'dram' not in str(type(o.tensor)).lower() else i.tensor.name
        if q == 'pool':
            slot = slot + "_sw"
        sem = self.dsem(slot)
        inst = self.eng[q].dma_start(out=o, in_=i, **kw)
        sem[1] += 16
        inst.then_inc(sem[0], 16)
        self.n_inst += 1
        tok = ("d_" + slot, sem[1])
        self.vcs[tok] = dict(self.know[q])
        for (n, s) in rk:
            self._record(n, s, False, tok)
        for (n, s) in wk:
            self._record(n, s, True, tok)
        return tok

    def barrier(self):
        for e in ('pe', 'act', 'dve', 'pool', 'sp'):
            kn = self.know[e]
            for f in ('pe', 'act', 'dve', 'pool'):
                if f != e and self.cnt[f] > kn.get('sem_' + f, 0):
                    self.eng[e].wait_ge(self.sem[f], self.cnt[f])
                    kn['sem_' + f] = self.cnt[f]
                    self.n_wait += 1
            for name, (s, v) in self.dma_sems.items():
                if v > kn.get('d_' + name, 0):
                    self.eng[e].wait_ge(s, v)
                    kn['d_' + name] = v
                    self.n_wait += 1
        for e in ('act', 'dve', 'pool'):
            if self.cnt[e] > self.know[e].get('sem_' + e, 0):
                self.eng[e].wait_ge(self.sem[e], self.cnt[e])
                self.know[e]['sem_' + e] = self.cnt[e]
                self.n_wait += 1

    def finish(self):
        for e in ('pe', 'act', 'dve', 'pool'):
            if self.cnt[e]:
                self.eng['sp'].wait_ge(self.sem[e], self.cnt[e])
        for name, (s, v) in self.dma_sems.items():
            if v:
                self.eng['sp'].wait_ge(s, v)

    def mm(self, out, lhsT, rhs, start=True, stop=True, **kw):
        o, l, r = _unw(out)[0], _unw(lhsT)[0], _unw(rhs)[0]
        return self.op('pe', lambda e: e.matmul(o, lhsT=l, rhs=r, start=start, stop=stop, **kw), [lhsT, rhs], [out])

    def tr(self, out, in_, ident):
        o, i, d = _unw(out)[0], _unw(in_)[0], _unw(ident)[0]
        return self.op('pe', lambda e: e.transpose(o, i, d), [in_, ident], [out])

    def act(self, out, in_, func, bias=None, scale=None, accum_out=None):
        o, i = _unw(out)[0], _unw(in_)[0]
        kw = {}
        rd = [in_]
        if bias is not None:
            kw['bias'] = _unw(bias)[0]
            rd.append(bias)
        if scale is not None:
            kw['scale'] = _unw(scale)[0]
            rd.append(scale)
        wr = [out]
        if accum_out is not None:
            kw['accum_out'] = _unw(accum_out)[0]
            wr.append(accum_out)
        return self.op('act', lambda e: e.activation(out=o, in_=i, func=func, **kw), rd, wr)

    def tt(self, out, in0, in1, op, eng='dve'):
        o, a, b = _unw(out)[0], _unw(in0)[0], _unw(in1)[0]
        return self.op(eng, lambda e: e.tensor_tensor(out=o, in0=a, in1=b, op=op), [in0, in1], [out])

    def ts(self, out, in0, s1, s2=None, op0=ALU.mult, op1=None, eng='dve'):
        o, a = _unw(out)[0], _unw(in0)[0]
        rd = [in0]
        s1v = _unw(s1)[0]
        if hasattr(s1v, 'tensor'):
            rd.append(s1)
        s2v = _unw(s2)[0] if s2 is not None else None
        if s2v is not None and hasattr(s2v, 'tensor'):
            rd.append(s2)
        kw = {}
        if op1 is not None:
            kw['op1'] = op1
        return self.op(eng, lambda e: e.tensor_scalar(out=o, in0=a, scalar1=s1v, scalar2=s2v, op0=op0, **kw), rd, [out])

    def stt(self, out, in0, scalar, in1, op0, op1):
        o, a, b = _unw(out)[0], _unw(in0)[0], _unw(in1)[0]
        sv = _unw(scalar)[0]
        rd = [in0, in1]
        if hasattr(sv, 'tensor'):
            rd.append(scalar)
        return self.op('dve', lambda e: e.scalar_tensor_tensor(out=o, in0=a, scalar=sv, in1=b, op0=op0, op1=op1), rd, [out])

    def copy(self, out, in_, eng='dve'):
        o, i = _unw(out)[0], _unw(in_)[0]
        if eng == 'act':
            return self.op('act', lambda e: e.copy(out=o, in_=i), [in_], [out])
        return self.op(eng, lambda e: e.tensor_copy(out=o, in_=i), [in_], [out])

    def memset(self, out, val, eng='dve'):
        o = _unw(out)[0]
        return self.op(eng, lambda e: e.memset(o, val), [], [out])

    def scan(self, out, d0, d1, initial, op0, op1):
        o, a, b = _unw(out)[0], _unw(d0)[0], _unw(d1)[0]
        iv = _unw(initial)[0]
        rd = [d0, d1]
        if hasattr(iv, 'tensor'):
            rd.append(initial)
        return self.op('dve', lambda e: e.tensor_tensor_scan(out=o, data0=a, data1=b, initial=iv, op0=op0, op1=op1), rd, [out])

    def reduce(self, out, in_, op, axis=AX.X):
        o, i = _unw(out)[0], _unw(in_)[0]
        return self.op('dve', lambda e: e.tensor_reduce(out=o, in_=i, axis=axis, op=op), [in_], [out])

    def recip(self, out, in_):
        o, i = _unw(out)[0], _unw(in_)[0]
        return self.op('dve', lambda e: e.reciprocal(out=o, in_=i), [in_], [out])


class Rot:
    def __init__(self, bufs):
        self.bufs = bufs
        self.i = 0

    def next(self):
        b = self.bufs[self.i % len(self.bufs)]
        self.i += 1
        return b


def colmaj(v):
    v = np.asarray(v, np.float32)
    j = v.shape[-1] // 128
    return np.ascontiguousarray(np.swapaxes(v.reshape(v.shape[:-1] + (j, 128)), -1, -2))


def make_consts():
    c = {}
    c['ident'] = np.eye(128, dtype=np.float32)
    s = np.arange(128)[:, None]
    l = np.arange(128)[None, :]
    causal = (s <= l)
    blk = (s // 4 == l // 4)
    c['maskneg_p'] = np.where(causal, 0.0, -30000.0).astype(np.float32)
    c['maskneg_s'] = np.where(causal & blk, 0.0, -30000.0).astype(np.float32)
    c['maskbig_p'] = np.where(causal, 0.0, 30000.0).astype(np.float32)
    c['maskbig_s'] = np.where(causal & blk, 0.0, 30000.0).astype(np.float32)
    sel = np.zeros((16, 16, 128), np.float32)
    for h in range(16):
        sel[h, h, :] = 1.0
    c['sel16'] = sel.reshape(16, 16 * 128)
    c['ones'] = np.ones((128, 128), np.float32)
    cm = (np.arange(64)[None, :] // 4 == np.arange(16)[:, None]).astype(np.float32)
    c['colmask'] = np.broadcast_to(cm.reshape(1, 16 * 64), (128, 16 * 64)).copy()
    rm = np.zeros((128, 128), np.float32)
    rm[:64, :16] = cm.T
    c['rowmask'] = rm
    ps = np.zeros((128, 128), np.float32)
    for k in range(16):
        ps[k, :] = ((np.arange(128) // 64) == (k % 2))
    c['parsel'] = ps
    hm = np.zeros((128, 128), np.float32)
    for k in range(16):
        hm[k, k // 2] = 1.0
    c['hmask'] = hm
    bm = np.zeros((128, 128), np.float32)
    for p in range(128):
        bm[p, p // 16] = 1.0
    c['bmask'] = bm
    return c


CONST_ORDER = ['ident', 'maskneg_p', 'maskneg_s', 'maskbig_p', 'maskbig_s', 'ones', 'rowmask', 'parsel', 'hmask', 'bmask']


def build(depth=DEPTH, stage=99, dbg=False):
    nc = bass.Bass("TRN2", target_bir_lowering=False)
    c = Ctx(nc)

    def din(name, shape, dt=F32):
        return nc.dram_tensor(name, list(shape), dt, kind="ExternalInput").ap()

    def dout(name, shape, dt=F32):
        return nc.dram_tensor(name, list(shape), dt, kind="ExternalOutput").ap()

    def dscr(name, shape, dt):
        return nc.dram_tensor(name, list(shape), dt, kind="Internal").ap()

    IN_SHAPES = {
        "x_p": [TP, D], "x_s": [TS, D], "mem": [256, D],
        "ck": [DEPTH, NB, 256, D], "cv": [DEPTH, NB, 256, D],
        "st_conv": [DEPTH, NB * 3, 1536], "st_ssd": [DEPTH, NB, 1024, 128],
        "w_kv": [DEPTH, D, 2048], "w_down": [DEPTH, 4, D, D], "w_out": [DEPTH, D, D],
        "norm_in_c": [DEPTH, 128, 8], "mem_norm_c": [DEPTH, 128, 8], "final_norm": [1, D],
        "b_gate_c": [DEPTH, 128, 32], "c_sel16": [16, 16 * 128], "c_colmask": [128, 16 * 64],
        "conv_w_c": [DEPTH, 128, 48], "conv_b_c": [DEPTH, 128, 12], "ssd_hp": [DEPTH, 16, 2],
        "ssd_d": [DEPTH, 16], "ssd_norm": [DEPTH, D],
        "ml_hp": [DEPTH, 4, 2], "ml_norm": [DEPTH, D], "st_mlc": [DEPTH, NB, 1024, 256], "st_mln": [DEPTH, NB, 1024],
        "st_mlm": [DEPTH, 4, NB],
        "s5_a_re_t": [DEPTH, 64, 64], "s5_a_im_t": [DEPTH, 64, 64], "s5_log_dt": [DEPTH, 64],
        "s5_b_re_t": [DEPTH, 64, 1024], "s5_b_im_t": [DEPTH, 64, 1024], "s5_c_re_t": [DEPTH, 64, 1024], "s5_c_im_t": [DEPTH, 64, 1024],
        "s5_d_c": [DEPTH, 128, 8], "st_s5_t": [DEPTH, 64, 2, 64, NB], "s5_glu_w": [DEPTH, D, D], "s5_glu_b_c": [DEPTH, 128, 8],
    }
    for i in range(DEPTH):
        IN_SHAPES["w_in%d" % i] = [D, D_IN]
    for k in CONST_ORDER:
        IN_SHAPES["c_" + k] = [128, 128]
    _ins = {}

    def IN(name):
        if name not in _ins:
            _ins[name] = din(name, IN_SHAPES[name])
        return _ins[name]

    nc._used_inputs = _ins

    y_p = dout("y_p", [TP, D])
    y_s = dout("y_s", [TS, D])
    mk_o = dout("mk_o", [DEPTH, 256, D])
    mv_o = dout("mv_o", [DEPTH, 256, D])
    conv_po = dout("conv_po", [DEPTH, 3, 1536])
    conv_so = dout("conv_so", [DEPTH, NB, 3, 1536])
    ssd_po = dout("ssd_po", [DEPTH, 1024, 128]) if stage >= 2 else None
    ssd_so = dout("ssd_so", [DEPTH, NB, 1024, 128]) if stage >= 2 else None
    if stage >= 3:
        s5re_po = dout("s5re_po", [DEPTH, 64, 64])
        s5im_po = dout("s5im_po", [DEPTH, 64, 64])
        s5re_so = dout("s5re_so", [DEPTH, NB, 64, 64])
        s5im_so = dout("s5im_so", [DEPTH, NB, 64, 64])
    if stage >= 4:
        mc_po = dout("mc_po", [DEPTH, 1024, 256])
        mn_po = dout("mn_po", [DEPTH, 1024])
        mm_po = dout("mm_po", [DEPTH, 4])
        mc_so = dout("mc_so", [DEPTH, NB, 1024, 256])
        mn_so = dout("mn_so", [DEPTH, NB, 1024])
        mm_so = dout("mm_so", [DEPTH, 4, NB])
    dbg_o = {}

    def DBG(name):
        if name not in dbg_o:
            dbg_o[name] = dout("dbg_" + name, [1024, NT])
        return dbg_o[name]

    sfm = dscr("sfm", [NFM, NT], BF16)
    stm = dscr("stm", [NT, NTM], BF16)
    x_scr = dscr("x_scr", [NT, D], F32)
    g_scr = dscr("g_scr", [24, NT], F32)

    with c.es:
        h_fm = c.sb("h_fm", [128, 8, NT], BF16)
        yk_fm = c.sb("yk_fm", [128, 8, NT], BF16)
        ident = c.sb("ident", [128, 128], F32)
        ident_b = c.sb("ident_b", [128, 128], BF16)
        ones_f = c.sb("ones_f", [128, 128], F32)
        cmask = {k: c.sb(k, [128, 128], F32) for k in ('maskneg_p', 'maskneg_s', 'maskbig_p', 'maskbig_s', 'rowmask', 'parsel', 'hmask')}
        colmask = c.sb("colmask", [128, 16 * 64], BF16)
        gin = c.sb("gin", [128, 8], F32)
        gmem = c.sb("gmem", [128, 8], F32)
        bgate = c.sb("bgate", [128, 32], F32)
        hm_fm = yk_fm[:, :, 0:256]
        wbufs = Rot([c.sb("wb%d" % i, [128, 8, 512], BF16) for i in range(2)])
        stg_b = Rot([c.sb("stgb%d" % i, [128, 512], BF16) for i in range(4)])
        stg_f = Rot([c.sb("stgf%d" % i, [128, 512], F32) for i in range(2)])
        f4 = Rot([c.sb("f4_%d" % i, [128, D], F32) for i in range(4)])
        b2 = Rot([c.sb("b2_%d" % i, [128, D], BF16) for i in range(5)])
        fq = Rot([c.sb("fq_%d" % i, [128, 128], F32) for i in range(6)])
        bq = Rot([c.sb("bq_%d" % i, [128, 128], BF16) for i in range(4)])
        sq_junk = c.sb("sq_junk", [128, D], BF16)
        st_small = Rot([c.sb("sts%d" % i, [128, 4], F32) for i in range(4)])
        banks = [c.ps("pb%d" % i, [128, 512], F32) for i in range(8)]
        pbank = Rot(banks[0:4])
        ptr2 = [banks[4], banks[5]]

        def bfv(bank):
            return bank[:].bitcast(BF16)

        c.dma('sp', ident[:], IN("c_ident")[:, :])
        c.dma('sp', ones_f[:], IN("c_ones")[:, :])
        c.copy(ident_b[:], ident[:])
        if stage >= 2:
            for k in cmask:
                c.dma('sp', cmask[k][:], IN("c_" + k)[:, :])
            c.dma('pool', colmask[:], IN("c_colmask")[:, :])

        def rms_rstd(src, P, n=D):
            st = st_small.next()
            c.act(sq_junk[0:P, 0:n], src, AF.Square, accum_out=st[0:P, 0:1])
            c.ts(st[0:P, 1:2], st[0:P, 0:1], 1.0 / n, EPS, op0=ALU.mult, op1=ALU.add)
            c.act(st[0:P, 2:3], st[0:P, 1:2], AF.Sqrt)
            c.recip(st[0:P, 3:4], st[0:P, 2:3])
            return st[0:P, 3:4]

        def norm_to_fm(src, P, gcol, dst):
            rstd = rms_rstd(src, P)
            xn = f4.next()
            c.ts(xn[0:P, :], src, rstd, None, op0=ALU.mult)
            for j in range(8):
                pt = ptr2[j // 4]
                c.tr(pt[:, (j % 4) * 128:(j % 4) * 128 + P], xn[0:P, j * 128:(j + 1) * 128], ident[0:P, 0:P])
            for hlf in range(2):
                pv = ptr2[hlf][:].rearrange("p (j t) -> p j t", t=128)[:, :, 0:P]
                c.tt(dst[:, 4 * hlf:4 * hlf + 4, :], pv,
                     gcol[:, 4 * hlf:4 * hlf + 4].unsqueeze(2).to_broadcast([128, 4, P]), ALU.mult)

        def tm_to_fm(src_b, P, dst):
            pv = bfv(ptr2[0])
            for j in range(8):
                c.tr(pv[:, j * 128:j * 128 + P], src_b[0:P, j * 128:(j + 1) * 128], ident_b[0:P, 0:P])
            c.copy(dst, pv.rearrange("p (j t) -> p j t", t=128)[:, :, 0:P], eng='act')

        def load_w(src):
            wb = wbufs.next()
            cw = src.shape[1]
            c.dma('pool', wb[:, :, 0:cw], src.rearrange("(kt p) c -> p kt c", p=128))
            return wb

        evac_flip = [0]

        def evac(out, in_, func=None, bias=None, scale=None):
            if func is not None:
                c.act(out, in_, func, bias=bias, scale=scale)
            elif scale is not None:
                if evac_flip[0] % 2 == 0:
                    c.act(out, in_, AF.Copy, scale=scale)
                else:
                    c.ts(out, in_, scale, None, op0=ALU.mult)
                evac_flip[0] += 1
            else:
                c.copy(out, in_, eng='act' if evac_flip[0] % 2 == 0 else 'dve')
                evac_flip[0] += 1

        def proj_fm(l, col0, ncols, row0, func=None, scale=None, biascol=None):
            for c0 in range(0, ncols, 512):
                cw = min(512, ncols - c0)
                wb = load_w(IN("w_in%d" % l)[:, col0 + c0:col0 + c0 + cw])
                for cb in range(cw // 128):
                    for (t0, n) in STS:
                        pb = pbank.next()
                        for kt in range(8):
                            c.mm(pb[:, 0:n], wb[:, kt, cb * 128:(cb + 1) * 128], h_fm[:, kt, t0:t0 + n],
                                 start=(kt == 0), stop=(kt == 7))
                        sg = stg_b.next()
                        b = biascol((c0 + cb * 128) // 128) if biascol is not None else None
                        evac(sg[:, 0:n], pb[:, 0:n], func=func, bias=b, scale=scale)
                        r0 = row0 + c0 + cb * 128
                        c.dma('sp', sfm[r0:r0 + 128, t0:t0 + n], sg[:, 0:n])

        def proj_tm(l, col0, ncols, scol0, func=None, scale=None):
            for c0 in range(0, ncols, 512):
                wb = load_w(IN("w_in%d" % l)[:, col0 + c0:col0 + c0 + 512])
                for (t0, P) in TILES:
                    pb = pbank.next()
                    for kt in range(8):
                        c.mm(pb[0:P, :], h_fm[:, kt, t0:t0 + P], wb[:, kt, :], start=(kt == 0), stop=(kt == 7))
                    sg = stg_b.next()
                    evac(sg[0:P, :], pb[0:P, :], func=func, scale=scale)
                    c.dma('sp', stm[t0:t0 + P, scol0 + c0:scol0 + c0 + 512], sg[0:P, :])

        def proj_small(l, col0, ncols, dst):
            wb = load_w(IN("w_in%d" % l)[:, col0:col0 + ncols])
            for (t0, n) in STS:
                pb = pbank.next()
                for kt in range(8):
                    c.mm(pb[0:ncols, 0:n], wb[:, kt, 0:ncols], h_fm[:, kt, t0:t0 + n], start=(kt == 0), stop=(kt == 7))
                sg = stg_f.next()
                c.copy(sg[0:ncols, 0:n], pb[0:ncols, 0:n], eng='act')
                c.dma('sp', dst[:, t0:t0 + n], sg[0:ncols, 0:n])

        def dump_fm(name, src):
            if dbg:
                for j in range(8):
                    c.dma('pool', DBG(name)[j * 128:(j + 1) * 128, :], src[:, j, :])

        def x_src(l, tt):
            t0, P = TILES[tt]
            if l == 0:
                return IN("x_p")[t0:t0 + P, :] if tt < 16 else IN("x_s")[:, :]
            return x_scr[t0:t0 + P, :]

        def phase0(l):
            c.dma('sp', gin[:], IN("norm_in_c")[l])
            c.dma('sp', gmem[:], IN("mem_norm_c")[l])
            c.dma('sp', bgate[:], IN("b_gate_c")[l])
            for tt, (t0, P) in enumerate(TILES):
                xt = f4.next()
                c.dma('sp', xt[0:P, :], x_src(l, tt))
                norm_to_fm(xt[0:P, :], P, gin, h_fm[:, :, t0:t0 + P])

        def memkv(l, mk_fm, mv_tm):
            for mt in range(2):
                mtile = f4.next()
                c.dma('sp', mtile[:, :], IN("mem")[mt * 128:(mt + 1) * 128, :])
                norm_to_fm(mtile[:, :], 128, gmem, hm_fm[:, :, mt * 128:(mt + 1) * 128])
            for half in ([h_ for h_ in range(2) if ('kv%d' % h_) not in SKIP] if 'kvw' not in SKIP else ()):
                for c0 in range(0, 1024, 512):
                    wb = load_w(IN("w_kv")[l, :, half * 1024 + c0:half * 1024 + c0 + 512])
                    for mt in range(2):
                        pb = pbank.next()
                        for kt in range(8):
                            c.mm(pb[:, :], hm_fm[:, kt, mt * 128:(mt + 1) * 128], wb[:, kt, :], start=(kt == 0), stop=(kt == 7))
                        sg = stg_f.next()
                        c.copy(sg[:, :], pb[:, :], eng='act')
                        dst = mk_o if half == 0 else mv_o
                        c.dma('sp', dst[l, mt * 128:(mt + 1) * 128, c0:c0 + 512], sg[:, :])
                        if half == 1:
                            c.copy(mv_tm[:, mt, c0:c0 + 512], sg[:, :], eng='dve')
                    if half == 0 and 'mkfm' not in SKIP:
                        for cb in range(4):
                            pb = pbank.next()
                            for kt in range(8):
                                c.mm(pb[:, 0:256], wb[:, kt, cb * 128:(cb + 1) * 128], hm_fm[:, kt, :], start=(kt == 0), stop=(kt == 7))
                            c.ts(mk_fm[:, c0 // 128 + cb, :], pb[:, 0:256], 1.0 / 16, None, op0=ALU.mult)

        def projections(l):
            if 'projfm' not in SKIP:
                proj_fm(l, O_XBC, 1536, R_XBC)
            for tt in ((15, 16) if 'convst' not in SKIP else ()):
                t0, P = TILES[tt]
                for c0 in range(0, 1536, 512):
                    wb = load_w(IN("w_in%d" % l)[:, O_XBC + c0:O_XBC + c0 + 512])
                    pb = pbank.next()
                    for kt in range(8):
                        c.mm(pb[0:P, :], h_fm[:, kt, t0:t0 + P], wb[:, kt, :], start=(kt == 0), stop=(kt == 7))
                    sg = stg_f.next()
                    c.copy(sg[0:P, :], pb[0:P, :], eng='act')
                    if tt == 15:
                        c.dma('sp', conv_po[l, :, c0:c0 + 512], sg[125:128, :])
                    else:
                        for b in range(NB):
                            c.dma('sp', conv_so[l, b, :, c0:c0 + 512], sg[4 * b + 1:4 * b + 4, :])
            if 'small' not in SKIP:
                if 's16' not in SKIP:
                    proj_small(l, O_DT, 16, g_scr[0:16, :])
                if 's8' not in SKIP:
                    proj_small(l, O_I, 8, g_scr[16:24, :])
            if stage >= 2:
                proj_tm(l, O_ZSSD, 1024, C_ZSSD, func=AF.Silu)
            if stage >= 3:
                proj_fm(l, O_U, 1024, R_U)
                proj_fm(l, O_ZS5, 1024, R_ZS5, func=AF.Silu)
                proj_fm(l, O_Q, 1024, R_Q)
                proj_fm(l, O_K, 1024, R_K, scale=1.0 / 16)
                proj_tm(l, O_K, 1024, C_KT, scale=1.0 / 16)
                proj_tm(l, O_V, 1024, C_V)
                proj_tm(l, O_O, 1024, C_O, func=AF.Sigmoid)
                proj_tm(l, O_ZML, 1024, C_ZML, func=AF.Silu)
                proj_fm(l, O_QXA, 1024, R_QXA)
                proj_tm(l, O_ZXA, 1024, C_ZXA, func=AF.Silu)
                proj_fm(l, O_GATE, 4096, R_GATE, func=AF.Sigmoid, biascol=lambda j: bgate[:, j:j + 1])

        def ssd_branch(l):
            bes = contextlib.ExitStack()
            uid = "_a%d" % l
            anames = []

            def A(name, shape, dt=F32):
                anames.append(name + uid)
                return bes.enter_context(nc.sbuf_tensor(name + uid, list(shape), dt))

            cw_sb = A("cw_sb", [128, 12, 4])
            cb_sb = A("cb_sb", [128, 12])
            hp16 = A("hp16", [16, 8])
            dbc = A("dbc", [128, 16])
            gbc = A("gbc", [128, D])
            DI = A("DI", [128, 16, 128], BF16)
            xr = A("xr", [128, 12, 515], BF16)
            xr_s = A("xr_s", [128, 12, 16, 7], BF16)
            xc = A("xc", [128, 12, 512], BF16)
            acc = A("acc", [128, 512])
            acs_r = Rot([A("acs%d" % i, [16, 128]) for i in range(2)])
            dt_r = Rot([A("dtt%d" % i, [16, 128]) for i in range(2)])
            tw_r = Rot([A("tww%d" % i, [16, 128]) for i in range(2)])
            rawg_r = Rot([A("rawg%d" % i, [16, 128]) for i in range(2)])
            selm_r = Rot([A("selm%d" % i, [16, 128]) for i in range(3)])
            STs_r = Rot([A("STs%d" % i, [128, D], BF16) for i in range(2)])
            btm_r = Rot([A("btm%d" % i, [128, 256], BF16) for i in range(2)])
            g16 = Rot([A("g16_%d" % i, [16, 128]) for i in range(4)])
            gtm_r = Rot([A("gtm%d" % i, [128, 96]) for i in range(2)])
            eatm_r = Rot([A("eatm%d" % i, [128, 16]) for i in range(2)])
            cbT_r = Rot([A("cbT%d" % i, [128, 2, 128]) for i in range(2)])
            ST = A("ST", [128, D])
            ST_bf = A("ST_bf", [128, D], BF16)
            dvec = A("dvec", [128, 16, 8])
            ers = A("ers", [16, 16, 8])
            els = A("els", [16, 16])
            dcs = A("dcs", [128, 16])
            S_r = Rot([A("S_b%d" % i, [128, 8, 128]) for i in range(2)])
            Cm_r = Rot([A("Cm%d" % i, [128, 2, 64], BF16) for i in range(2)])
            Bm_r = Rot([A("Bm%d" % i, [64, 256], BF16) for i in range(2)])
            c.dma('sp', cw_sb[:].rearrange("p j k -> p (j k)"), IN("conv_w_c")[l])
            c.dma('sp', cb_sb[:], IN("conv_b_c")[l])
            c.dma('sp', hp16[:, 0:2], IN("ssd_hp")[l])
            c.dma('sp', dbc[:], IN("ssd_d")[l:l + 1, :].partition_broadcast(128))
            c.dma('sp', gbc[:], IN("ssd_norm")[l:l + 1, :].partition_broadcast(128))
            c.act(hp16[:, 2:3], hp16[:, 1:2], AF.Exp)
            c.ts(hp16[:, 3:4], hp16[:, 2:3], -1.0, None, op0=ALU.mult)
            dtb, aneg = hp16[:, 0:1], hp16[:, 3:4]
            for h in range(16):
                c.ts(DI[:, h, :], ident[:], dbc[:, h:h + 1], None, op0=ALU.mult)
            c.memset(ST[:], 0.0)
            c.memset(ST_bf[:], 0.0)
            pX, pB, pCB, pYA, pYB, pEA, pEB, pBC = banks
            pbc_r = Rot([pBC, pCB])
            for si, (s0, n) in enumerate(STS):
                sample = (si == 4)
                src_rows = sfm[R_XBC:R_XBC + 1536, :].rearrange("(j p) t -> p j t", p=128)
                if not sample:
                    if s0 == 0:
                        c.memset(xr[:, :, 0:3], 0.0)
                        c.dma('sp', xr[:, :, 3:515], src_rows[:, :, 0:512])
                    else:
                        c.dma('sp', xr[:, :, 0:515], src_rows[:, :, s0 - 3:s0 + 512])
                    for j in range(12):
                        c.ts(acc[:, :], xr[:, j, 3:515], cw_sb[:, j, 3:4], cb_sb[:, j:j + 1], op0=ALU.mult, op1=ALU.add)
                        for k in (2, 1, 0):
                            c.stt(acc[:, :], xr[:, j, k:k + 512], cw_sb[:, j, k:k + 1], acc[:, :], ALU.mult, ALU.add)
                        c.act(xc[:, j, :], acc[:, :], AF.Silu)
                else:
                    raw = b2.next()
                    rawv = raw[:, 0:768].rearrange("p (j t) -> p j t", t=64)
                    c.dma('sp', rawv, src_rows[:, :, 2048:2112])
                    csts = (f4.next(), f4.next())
                    for hlf in range(2):
                        c.dma('sp', csts[hlf][0:48, 0:768], IN("st_conv")[l][:, hlf * 768:(hlf + 1) * 768])
                    for j in range(12):
                        pt = ptr2[j // 6]
                        c.tr(pt[:, (j % 6) * 48:(j % 6) * 48 + 48], csts[j // 6][0:48, (j % 6) * 128:(j % 6 + 1) * 128], ident[0:48, 0:48])
                    for j in range(12):
                        c.copy(xr_s[:, j, :, 0:3], ptr2[j // 6][:, (j % 6) * 48:(j % 6) * 48 + 48].rearrange("p (b k) -> p b k", k=3),
                               eng='act' if j % 2 else 'dve')
                        c.copy(xr_s[:, j, :, 3:7], rawv[:, j, :].rearrange("p (b t) -> p b t", t=4), eng='dve' if j % 2 else 'act')
                    for j in range(12):
                        av = acc[:, 0:64].rearrange("p (b t) -> p b t", t=4)
                        c.ts(av, xr_s[:, j, :, 3:7], cw_sb[:, j, 3:4], cb_sb[:, j:j + 1], op0=ALU.mult, op1=ALU.add)
                        for k in (2, 1, 0):
                            c.stt(av, xr_s[:, j, :, k:k + 4], cw_sb[:, j, k:k + 1], av, ALU.mult, ALU.add)
                        c.act(xc[:, j, 0:64], acc[:, 0:64], AF.Silu)
                for (t0, P) in [t for t in TILES if s0 <= t[0] < s0 + n]:
                    lo = t0 - s0
                    mneg = cmask['maskneg_s'] if sample else cmask['maskneg_p']
                    acs, dtt, tww = acs_r.next(), dt_r.next(), tw_r.next()
                    ge, gla = g16.next(), g16.next()
                    rawg = rawg_r.next()
                    c.dma('sp', rawg[:, 0:P], g_scr[0:16, t0:t0 + P])
                    c.act(ge[:, 0:P], rawg[:, 0:P], AF.Exp, bias=dtb)
                    c.act(dtt[:, 0:P], ge[:, 0:P], AF.Ln, bias=1.0)
                    c.ts(gla[:, 0:P], dtt[:, 0:P], aneg, None, op0=ALU.mult)
                    if not sample:
                        c.scan(acs[:, 0:P], ones_f[0:16, 0:P], gla[:, 0:P], 0.0, ALU.mult, ALU.add)
                        alast = acs[:, P - 1:P]
                        gd = g16.next()
                        c.act(gd[:, 0:P], acs[:, 0:P], AF.Exp, bias=alast, scale=-1.0)
                    else:
                        av = acs[:, 0:64].rearrange("p (b t) -> p b t", t=4)
                        lv = gla[:, 0:64].rearrange("p (b t) -> p b t", t=4)
                        c.copy(av[:, :, 0:1], lv[:, :, 0:1])
                        for t in (1, 2, 3):
                            c.tt(av[:, :, t:t + 1], av[:, :, t - 1:t], lv[:, :, t:t + 1], ALU.add)
                        gd = g16.next()
                        dv = gd[:, 0:64].rearrange("p (b t) -> p b t", t=4)
                        c.tt(dv, av[:, :, 3:4].to_broadcast([16, 16, 4]), av, ALU.subtract)
                        c.act(gd[:, 0:64], gd[:, 0:64], AF.Exp)
                    c.tt(tww[:, 0:P], gd[:, 0:P], dtt[:, 0:P], ALU.mult)
                    for qi, qsrc in enumerate((acs, dtt, tww)):
                        c.tr(pB[0:P, 32 * qi:32 * qi + 16], qsrc[:, 0:P], ident[0:16, 0:16])
                    gtm = gtm_r.next()
                    c.copy(gtm[0:P, :].rearrange("p (q x) -> p q x", x=32)[:, :, 0:16], pB[0:P, 0:96].rearrange("p (q x) -> p q x", x=32)[:, :, 0:16])
                    eatm = eatm_r.next()
                    c.act(eatm[0:P, :], gtm[0:P, 0:16], AF.Exp)
                    pxv = bfv(pX)
                    for j in range(8):
                        c.tr(pxv[0:P, j * 128:(j + 1) * 128], xc[:, j, lo:lo + P], ident_b[:, :])
                    xtm = b2.next()
                    c.copy(xtm[0:P, :], pxv[0:P, :], eng='act')
                    pbv = bfv(pYA)
                    for g in range(2):
                        c.tr(pbv[0:P, g * 128:(g + 1) * 128], xc[:, 8 + g, lo:lo + P], ident_b[:, :])
                    btm = btm_r.next()
                    c.copy(btm[0:P, :], pbv[0:P, 0:256])
                    btms = (btm[:, 0:128], btm[:, 128:256])
                    cbT = cbT_r.next()
                    for g in range(2):
                        c.mm(pCB[0:P, g * 128:g * 128 + P], xc[:, 8 + g, lo:lo + P], xc[:, 10 + g, lo:lo + P])
                    c.copy(cbT[0:P, :, 0:P], pCB[0:P, 0:256].rearrange("p (g t) -> p g t", g=2)[:, :, 0:P], eng='act')
                    for h in range(16):
                        g = h // 8
                        pbc = pbc_r.next()
                        selm = selm_r.next()
                        c.ts(selm[:, 0:P], acs[:, 0:P], ident[0:16, h:h + 1], None, op0=ALU.mult)
                        c.mm(pbc[0:P, 0:P], ones_f[0:16, 0:P], selm[:, 0:P])
                        tsb = fq.next()
                        c.stt(tsb[0:P, 0:P], pbc[0:P, 0:P], gtm[0:P, h:h + 1], mneg[0:P, 0:P], ALU.subtract, ALU.min)
                        E = fq.next()
                        c.act(E[0:P, 0:P], tsb[0:P, 0:P], AF.Exp)
                        MT = bq.next()
                        c.stt(MT[0:P, 0:P], E[0:P, 0:P], gtm[0:P, 32 + h:33 + h], cbT[0:P, g, 0:P], ALU.mult, ALU.mult)
                        py = pYA if h < 8 else pYB
                        oc = (h % 8) * 64
                        c.mm(py[0:P, oc:oc + 64], MT[0:P, 0:P], xtm[0:P, h * 64:(h + 1) * 64], start=True, stop=False)
                        c.mm(py[0:P, oc:oc + 64], DI[0:P, h, 0:P], xtm[0:P, h * 64:(h + 1) * 64], start=False, stop=True)
                    if not sample:
                        for g, pe_ in enumerate((pEA, pEB)):
                            c.mm(pe_[0:P, :], xc[:, 10 + g, lo:lo + P], ST_bf[:, g * 512:(g + 1) * 512])
                    else:
                        for b in range(NB):
                            Sb = S_r.next()
                            c.dma('sp', Sb[:], IN("st_ssd")[l, b].rearrange("(j p) n -> p j n", p=128))
                            for j in range(8):
                                c.tr((pX, pB)[j // 4][:, (j % 4) * 128:(j % 4 + 1) * 128], Sb[:, j, :], ident[:, :])
                            STs = STs_r.next()
                            c.copy(STs[:, 0:512], pX[:, :], eng='act')
                            c.copy(STs[:, 512:1024], pB[:, :], eng='dve')
                            Cm = Cm_r.next()
                            c.tt(Cm[:, :, :], xc[:, 10:12, 0:64],
                                 colmask[:, b * 64:(b + 1) * 64].unsqueeze(1).to_broadcast([128, 2, 64]), ALU.mult)
                            for g, pe_ in enumerate((pEA, pEB)):
                                c.mm(pe_[0:P, :], Cm[:, g, :], STs[:, g * 512:(g + 1) * 512], start=(b == 0), stop=(b == NB - 1))
                            if b == 0:
                                avl = acs[:, 0:64].rearrange("p (b t) -> p b t", t=4)[:, :, 3:4]
                                c.act(els[:, :].unsqueeze(2), avl, AF.Exp)
                                c.tt(ers[:, :, :], els[:, :].unsqueeze(2).to_broadcast([16, 16, 8]),
                                     cmask['hmask'][0:16, 0:8].unsqueeze(1).to_broadcast([16, 16, 8]), ALU.mult)
                                c.mm(pCB[:, 0:128], cmask['parsel'][0:16, :], ers[:, :, :].rearrange("p b j -> p (b j)"))
                                c.copy(dvec[:, :, :], pCB[:, 0:128].rearrange("p (b j) -> p b j", j=8))
                                xw = b2.next()
                                c.tt(xw[0:P, :].rearrange("p (h q) -> p h q", q=64), xtm[0:P, :].rearrange("p (h q) -> p h q", q=64),
                                     gtm[0:P, 64:80].unsqueeze(2).to_broadcast([P, 16, 64]), ALU.mult)
                                xw_keep = xw
                            Bm = Bm_r.next()
                            for g in range(2):
                                c.ts(Bm[0:64, g * 128:(g + 1) * 128], btm[0:64, g * 128:(g + 1) * 128], cmask['rowmask'][0:64, b:b + 1], None, op0=ALU.mult)
                            for j in range(8):
                                pd = pCB if j < 4 else pBC
                                c.mm(pd[:, (j % 4) * 128:(j % 4 + 1) * 128], xw_keep[0:64, j * 128:(j + 1) * 128],
                                     Bm[0:64, (j // 4) * 128:(j // 4 + 1) * 128])
                            c.tt(Sb[:, :, :], Sb[:, :, :], dvec[:, b, :].unsqueeze(2).to_broadcast([128, 8, 128]), ALU.mult)
                            c.tt(Sb[:, 0:4, :], Sb[:, 0:4, :], pCB[:, :].rearrange("p (j n) -> p j n", n=128), ALU.add)
                            c.tt(Sb[:, 4:8, :], Sb[:, 4:8, :], pBC[:, :].rearrange("p (j n) -> p j n", n=128), ALU.add)
                            c.dma('sp', ssd_so[l, b].rearrange("(j p) n -> p j n", p=128), Sb[:, :, :])
                    ty = f4.next()
                    for g, (pe_, py) in enumerate(((pEA, pYA), (pEB, pYB))):
                        tv = ty[0:P, g * 512:(g + 1) * 512]
                        c.tt(tv.rearrange("p (h q) -> p h q", q=64), pe_[0:P, :].rearrange("p (h q) -> p h q", q=64),
                             eatm[0:P, g * 8:(g + 1) * 8].unsqueeze(2).to_broadcast([P, 8, 64]), ALU.mult)
                        c.tt(tv, tv, py[0:P, :], ALU.add)
                    ztm = b2.next()
                    c.dma('sp', ztm[0:P, :], stm[t0:t0 + P, C_ZSSD:C_ZSSD + 1024])
                    c.tt(ty[0:P, :], ty[0:P, :], ztm[0:P, :], ALU.mult)
                    rstd = rms_rstd(ty[0:P, :], P)
                    yn = b2.next()
                    c.stt(yn[0:P, :], ty[0:P, :], rstd, gbc[0:P, :], ALU.mult, ALU.mult)
                    tm_to_fm(yn, P, yk_fm[:, :, t0:t0 + P])
                    if not sample:
                        xw = b2.next()
                        c.tt(xw[0:P, :].rearrange("p (h q) -> p h q", q=64), xtm[0:P, :].rearrange("p (h q) -> p h q", q=64),
                             gtm[0:P, 64:80].unsqueeze(2).to_broadcast([P, 16, 64]), ALU.mult)
                        for g, pe_ in enumerate((pEA, pEB)):
                            c.mm(pe_[:, :], btm[0:P, g * 128:(g + 1) * 128], xw[0:P, g * 512:(g + 1) * 512])
                        e16 = g16.next()
                        c.act(e16[:, 0:1], alast, AF.Exp)
                        ed = g16.next()
                        c.ts(ed[:, 0:16], ident[0:16, 0:16], e16[:, 0:1], None, op0=ALU.mult)
                        c.mm(pX[:, 0:16], ones_f[0:16, :], ed[:, 0:16])
                        c.copy(dcs[:, :], pX[:, 0:16])
                        c.tt(ST[:, :].rearrange("p (h q) -> p h q", q=64), ST[:, :].rearrange("p (h q) -> p h q", q=64),
                             dcs[:, :].unsqueeze(2).to_broadcast([128, 16, 64]), ALU.mult)
                        for g, pe_ in enumerate((pEA, pEB)):
                            c.tt(ST[:, g * 512:(g + 1) * 512], ST[:, g * 512:(g + 1) * 512], pe_[:, :], ALU.add)
                        c.copy(ST_bf[:, :], ST[:, :], eng='act')
                        if t0 == 1920:
                            for j in range(8):
                                c.tr(ptr2[j // 4][:, (j % 4) * 128:(j % 4 + 1) * 128], ST[:, j * 128:(j + 1) * 128], ident[:, :])
                            so = f4.next()
                            c.copy(so[:, 0:512], ptr2[0][:, :], eng='act')
                            c.copy(so[:, 512:1024], ptr2[1][:, :], eng='dve')
                            c.dma('sp', ssd_po[l].rearrange("(j p) n -> p j n", p=128), so[:, :].rearrange("p (j n) -> p j n", n=128))
            dump_fm("y_a", yk_fm)
            c.barrier()
            c.release(anames)
            bes.close()
        def s5_branch(l):
            bes = contextlib.ExitStack()
            uid = "_b%d" % l
            anames = []

            def A(name, shape, dt=F32):
                anames.append(name + uid)
                return bes.enter_context(nc.sbuf_tensor(name + uid, list(shape), dt))

            TWO_PI = 6.283185307179586
            PI = 3.141592653589793
            par = A("par", [64, 3, 64])
            tb = [A("tb%d" % i, [64, 64]) for i in range(12)]
            tbi = A("tbi", [64, 64], mybir.dt.int32)
            A2 = A("A2", [64, 2, 64])
            Bc = A("Bc", [64, 2, 64])
            sre, sim = A("sre", [64, 64]), A("sim", [64, 64])
            big = A("big", [64, 2 * 64 * 16])
            Bb = big[:, :].rearrange("p (r x) -> p r x", r=2)
            CT = A("CT", [64, 2, 64 * 16])
            BT = A("BT", [128, 8, 2, 8, 64], BF16)
            bmask = A("bmask", [128, 8])
            Dd = A("Dd", [128, 8, 128], BF16)
            dcol = A("dcol", [128, 8])
            SUB = 16
            u_st = A("u_st", [128, 8, 256], BF16)
            bu1 = A("bu1", [64, 2 * 64 * SUB])
            b4 = lambda t: t[:, :].rearrange("p (r g t) -> p r g t", r=2, g=64)
            bu_r = Rot([b4(big), b4(bu1)])
            hist_r = Rot([b4(A("hist%d" % i, [64, 2 * 64 * SUB])) for i in range(2)])
            t1_r = Rot([A("t1_0", [64, 2, 64 * 4])])
            t2_r = Rot([A("t2_0", [64, 2, 64 * 4])])
            H0s = A("H0s", [64, 2, 64, 4])
            yg_r = Rot([A("yg%d" % i, [SUB, D], BF16) for i in range(2)])
            hs = A("hs", [64, 128])

            c.dma('sp', par[:, 0, :], IN("s5_a_re_t")[l])
            c.dma('sp', par[:, 1, :], IN("s5_a_im_t")[l])
            c.dma('sp', par[:, 2, :], IN("s5_log_dt")[l:l + 1, :].partition_broadcast(64))
            c.dma('sp', Bb[:, 0, :], IN("s5_b_re_t")[l])
            c.dma('sp', Bb[:, 1, :], IN("s5_b_im_t")[l])
            c.dma('sp', CT[:, 0, :], IN("s5_c_re_t")[l])
            c.dma('sp', CT[:, 1, :], IN("s5_c_im_t")[l])
            c.dma('sp', bmask[:], IN("c_bmask")[:, 0:8])
            c.dma('sp', dcol[:], IN("s5_d_c")[l])
            c.ts(CT[:, 1, :], CT[:, 1, :], -1.0, None, op0=ALU.mult)
            for jb in range(8):
                c.ts(Dd[:, jb, :], ident[:, :], dcol[:, jb:jb + 1], None, op0=ALU.mult)
            are, aim = par[:, 0, :], par[:, 1, :]
            dtt, lr, th, mag, red, sn, cs, t_a, t_b, t_c, den, rden = tb
            c.act(dtt[:], par[:, 2, :], AF.Exp)
            c.tt(lr[:], are, dtt[:], ALU.mult)
            c.tt(th[:], aim, dtt[:], ALU.mult)
            c.act(mag[:], lr[:], AF.Exp)

            def sin_of(dst, ang, shift):
                c.ts(red[:], ang, shift, None, op0=ALU.add)
                c.ts(t_a[:], red[:], 1.0 / TWO_PI, None, op0=ALU.mult)
                c.copy(tbi[:], t_a[:])
                c.copy(t_a[:], tbi[:])
                c.stt(red[:], t_a[:], -TWO_PI, red[:], ALU.mult, ALU.add)
                c.ts(t_b[:], red[:], PI, -TWO_PI, op0=ALU.is_gt, op1=ALU.mult)
                c.tt(red[:], red[:], t_b[:], ALU.add)
                c.ts(t_b[:], red[:], -PI, TWO_PI, op0=ALU.is_lt, op1=ALU.mult)
                c.tt(red[:], red[:], t_b[:], ALU.add)
                c.ts(red[:], red[:], PI, -PI, op0=ALU.min, op1=ALU.max)
                c.act(dst, red[:], AF.Sin)

            sin_of(sn[:], th[:], 0.0)
            sin_of(cs[:], th[:], PI / 2)
            c.tt(A2[:, 0, :], mag[:], cs[:], ALU.mult)
            c.copy(A2[:, 1, :], A2[:, 0, :])
            c.tt(Bc[:, 1, :], mag[:], sn[:], ALU.mult)
            c.ts(Bc[:, 0, :], Bc[:, 1, :], -1.0, None, op0=ALU.mult)
            lre, lim = A2[:, 0, :], Bc[:, 1, :]
            c.ts(t_a[:], lre, -1.0, None, op0=ALU.add)
            c.tt(den[:], are, are, ALU.mult)
            c.tt(t_b[:], aim, aim, ALU.mult)
            c.tt(den[:], den[:], t_b[:], ALU.add)
            c.recip(rden[:], den[:])
            c.tt(t_b[:], t_a[:], are, ALU.mult)
            c.tt(t_c[:], lim, aim, ALU.mult)
            c.tt(t_b[:], t_b[:], t_c[:], ALU.add)
            c.tt(sre[:], t_b[:], rden[:], ALU.mult)
            c.tt(t_b[:], lim, are, ALU.mult)
            c.tt(t_c[:], t_a[:], aim, ALU.mult)
            c.tt(t_b[:], t_b[:], t_c[:], ALU.subtract)
            c.tt(sim[:], t_b[:], rden[:], ALU.mult)
            v3 = lambda ap: ap.rearrange("p (g c) -> p g c", c=16)
            sb_ = lambda ap: ap.unsqueeze(2).to_broadcast([64, 64, 16])
            T1 = bu1[:, 0:1024]
            T2 = bu1[:, 1024:2048]
            c.tt(v3(T1), v3(Bb[:, 0, :]), sb_(sim[:]), ALU.mult)
            c.tt(v3(T2), v3(Bb[:, 1, :]), sb_(sim[:]), ALU.mult)
            c.tt(v3(Bb[:, 0, :]), v3(Bb[:, 0, :]), sb_(sre[:]), ALU.mult)
            c.tt(Bb[:, 0, :], Bb[:, 0, :], T2, ALU.subtract)
            c.tt(v3(Bb[:, 1, :]), v3(Bb[:, 1, :]), sb_(sre[:]), ALU.mult)
            c.tt(Bb[:, 1, :], Bb[:, 1, :], T1, ALU.add)
            pW = banks[0]
            for jb in range(8):
                for ri in range(2):
                    c.tr(pW[:, ri * 64:(ri + 1) * 64], Bb[:, ri, jb * 128:(jb + 1) * 128], ident[0:64, 0:64])
                for ri in range(2):
                    c.tt(BT[:, jb, ri, :, :], pW[:, ri * 64:(ri + 1) * 64].unsqueeze(1).to_broadcast([128, 8, 64]),
                         bmask[:, :].unsqueeze(2).to_broadcast([128, 8, 64]), ALU.mult)

            pBU = Rot([banks[1], banks[2]])
            pY = [(banks[3], banks[4]), (banks[5], banks[6])]
            pyi = [0]
            pTo = banks[7]
            nsub = 0
            prev_hist = None
            for si, (s0, n) in enumerate([(256 * i, 256) for i in range(8)] + [(2048, 64)]):
                sample = (si == 8)
                c.dma('sp', u_st[:, :, 0:n], sfm[R_U:R_U + 1024, s0:s0 + n].rearrange("(j p) t -> p j t", p=128))
                def emit_bu(q0):
                    bu = bu_r.next()
                    for jb in range(8):
                        pb = pBU.next()
                        for ri in range(2):
                            for g in range(8):
                                o = (ri * 8 + g) * SUB
                                c.mm(pb[0:64, o:o + SUB], BT[:, jb, ri, g, :], u_st[:, jb, q0:q0 + SUB])
                        c.copy(bu[:, :, jb * 8:(jb + 1) * 8, :], pb[0:64, 0:16 * SUB].rearrange("p (r g t) -> p r g t", r=2, g=8), eng='act')
                    return bu

                nxt_bu = emit_bu(0)
                for q0 in range(0, n, SUB):
                    bu = nxt_bu
                    if q0 + SUB < n:
                        nxt_bu = emit_bu(q0 + SUB)
                    hist = hist_r.next()
                    if not sample:
                        for t in range(SUB):
                            if nsub == 0 and t == 0:
                                c.copy(hist[:, :, :, 0], bu[:, :, :, 0])
                                continue
                            hp = prev_hist[:, :, :, SUB - 1] if t == 0 else hist[:, :, :, t - 1]
                            t1, t2 = t1_r.next(), t2_r.next()
                            t1v = t1[:, :, 0:64]
                            t2v = t2[:, :, 0:64]
                            c.tt(t1v, A2[:, :, :], hp, ALU.mult)
                            c.tt(t2v[:, 0, :], Bc[:, 0, :], hp[:, 1, :], ALU.mult)
                            c.tt(t2v[:, 1, :], Bc[:, 1, :], hp[:, 0, :], ALU.mult)
                            c.tt(t1v, t1v, t2v, ALU.add)
                            c.tt(hist[:, :, :, t], t1v, bu[:, :, :, t], ALU.add)
                    else:
                        bh = q0 // SUB
                        c.dma('sp', H0s[:, :, :, :], IN("st_s5_t")[l][:, :, :, bh * 4:(bh + 1) * 4])
                        hv = hist[:, :, :, :].rearrange("p r g (b t) -> p r g b t", t=4)
                        bv = bu[:, :, :, :].rearrange("p r g (b t) -> p r g b t", t=4)
                        for t in range(4):
                            t1, t2 = t1_r.next(), t2_r.next()
                            t1v = t1[:, :, :].rearrange("p r (g b) -> p r g b", b=4)
                            t2v = t2[:, :, :].rearrange("p r (g b) -> p r g b", b=4)
                            for ri in range(2):
                                hp_r = H0s[:, ri, :, :] if t == 0 else hv[:, ri, :, :, t - 1]
                                hp_o = H0s[:, 1 - ri, :, :] if t == 0 else hv[:, 1 - ri, :, :, t - 1]
                                c.tt(t1v[:, ri], A2[:, ri, :].unsqueeze(2).to_broadcast([64, 64, 4]), hp_r, ALU.mult)
                                c.tt(t2v[:, ri], Bc[:, ri, :].unsqueeze(2).to_broadcast([64, 64, 4]), hp_o, ALU.mult)
                            for ri in range(2):
                                c.tt(t1v[:, ri], t1v[:, ri], t2v[:, ri], ALU.add)
                                c.tt(hv[:, ri, :, :, t], t1v[:, ri], bv[:, ri, :, :, t], ALU.add)
                        for ri, dst in enumerate((s5re_so, s5im_so)):
                            for b in range(4):
                                c.copy(hs[:, (b % 2) * 64:(b % 2 + 1) * 64], hv[:, ri, :, b, 3])
                                c.tr(pTo[0:64, (b % 2) * 64:(b % 2 + 1) * 64], hs[:, (b % 2) * 64:(b % 2 + 1) * 64], ident[0:64, 0:64])
                                so = stg_f.next()
                                c.copy(so[0:64, 0:64], pTo[0:64, (b % 2) * 64:(b % 2 + 1) * 64], eng='act')
                                c.dma('sp', dst[l, bh * 4 + b], so[0:64, 0:64])
                    pya, pyb = pY[pyi[0] % 2]
                    pyi[0] += 1
                    for jb in range(8):
                        py = pya if jb < 4 else pyb
                        oc = (jb % 4) * 128
                        for g in range(8):
                            gg = jb * 8 + g
                            oo = oc + g * 16
                            c.mm(py[0:SUB, oo:oo + 16], u_st[:, jb, q0:q0 + SUB], Dd[:, jb, g * 16:(g + 1) * 16], start=True, stop=False)
                            for ri in range(2):
                                c.mm(py[0:SUB, oo:oo + 16], hist[:, ri, gg, :], CT[:, ri, gg * 16:(gg + 1) * 16], start=False, stop=(ri == 1))
                    yg = yg_r.next()
                    c.act(yg[:, 0:512], pya[0:SUB, :], AF.Gelu)
                    c.act(yg[:, 512:1024], pyb[0:SUB, :], AF.Gelu)
                    tm_to_fm(yg, SUB, yk_fm[:, :, s0 + q0:s0 + q0 + SUB])
                    prev_hist = hist
                    nsub += 1
                if si == 7:
                    for ri, dst in enumerate((s5re_po, s5im_po)):
                        c.copy(hs[:, 0:64], prev_hist[:, ri, :, SUB - 1])
                        c.tr(pTo[0:64, 0:64], hs[:, 0:64], ident[0:64, 0:64])
                        so = stg_f.next()
                        c.copy(so[0:64, 0:64], pTo[0:64, 0:64], eng='act')
                        c.dma('sp', dst[l], so[0:64, 0:64])
            dump_fm("y_b0g", yk_fm)
            c.barrier()
            c.release(anames)
            bes.close()
            c.dma('sp', bgate[:, 0:8], IN("s5_glu_b_c")[l])
            for c0 in (0, 512):
                wb = load_w(IN("s5_glu_w")[l, :, c0:c0 + 512])
                for cb in range(4):
                    jb = c0 // 128 + cb
                    for (t0, n) in STS:
                        pb = pbank.next()
                        for kt in range(8):
                            c.mm(pb[:, 0:n], wb[:, kt, cb * 128:(cb + 1) * 128], yk_fm[:, kt, t0:t0 + n], start=(kt == 0), stop=(kt == 7))
                        sg = stg_b.next()
                        c.act(sg[:, 0:n], pb[:, 0:n], AF.Sigmoid, bias=bgate[:, jb:jb + 1])
                        zt = stg_b.next()
                        c.dma('sp', zt[:, 0:n], sfm[R_ZS5 + jb * 128:R_ZS5 + (jb + 1) * 128, t0:t0 + n])
                        c.tt(sg[:, 0:n], sg[:, 0:n], zt[:, 0:n], ALU.mult)
                        c.tt(sg[:, 0:n], sg[:, 0:n], yk_fm[:, jb, t0:t0 + n], ALU.mult)
                        c.dma('sp', sfm[R_U + jb * 128:R_U + (jb + 1) * 128, t0:t0 + n], sg[:, 0:n])
            c.dma('sp', bgate[:], IN("b_gate_c")[l])
            for jb in range(8):
                c.dma('sp', yk_fm[:, jb, :], sfm[R_U + jb * 128:R_U + (jb + 1) * 128, :])
            dump_fm("y_b", yk_fm)

        def mlstm_branch(l):
            bes = contextlib.ExitStack()
            uid = "_c%d" % l
            anames = []

            def A(name, shape, dt=F32):
                anames.append(name + uid)
                return bes.enter_context(nc.sbuf_tensor(name + uid, list(shape), dt))

            hp4 = A("hp4", [4, 8])
            gml = A("gml", [128, D])
            q_st = A("q_st", [128, 8, 512], BF16)
            k_st = A("k_st", [128, 8, 512], BF16)
            Cst = A("Cst", [128, 4, 2, 257])
            Cbf = A("Cbf", [128, 4, 2, 257], BF16)
            mprev = A("mprev", [4, 16])
            mprev_s = A("mprev_s", [4, 16])
            g4 = Rot([A("g4_%d" % i, [4, 128]) for i in range(14)])
            gtm_r = Rot([A("mgtm%d" % i, [128, 128]) for i in range(2)])
            vaug_r = Rot([A("vaug%d" % i, [128, 4, 257], BF16) for i in range(2)])
            qs_r = Rot([A("qs%d" % i, [128, 2, 128], BF16) for i in range(2)])
            kw_r = Rot([A("kw%d" % i, [128, 256], BF16) for i in range(2)])
            hc_r = Rot([A("hc%d" % i, [128, 256]) for i in range(2)])
            dec_sb = A("dec_sb", [128, 64])
            dg = A("dg", [4, 64])
            Cb_r = Rot([A("Cb%d" % i, [128, 2, 257]) for i in range(2)])
            Cbb_r = Rot([A("Cbb%d" % i, [128, 2, 257], BF16) for i in range(2)])
            qsb_r = Rot([A("qsb%d" % i, [128, 2, 64], BF16) for i in range(2)])
            for v in vaug_r.bufs:
                c.memset(v[:, :, 256:257], 1.0)
            c.dma('sp', hp4[:, 0:2], IN("ml_hp")[l])
            c.ts(hp4[:, 2:3], hp4[:, 1:2], -1.0, None, op0=ALU.mult)
            c.dma('sp', gml[:], IN("ml_norm")[l:l + 1, :].partition_broadcast(128))
            c.memset(Cst[:, :, :, :].rearrange("p a b c -> p (a b c)"), 0.0)
            c.memset(Cbf[:, :, :, :].rearrange("p a b c -> p (a b c)"), 0.0)
            c.memset(mprev[:, 0:1], 0.0)
            bi, nbf = hp4[:, 0:1], hp4[:, 2:3]
            pQK, pBC, pN0, pN1, pC0, pC1, pT, pBC2 = banks
            pn_r = Rot([pN0, pN1])
            for si, (s0, n) in enumerate(STS):
                sample = (si == 4)
                c.dma('sp', q_st[:, :, 0:n], sfm[R_Q:R_Q + 1024, s0:s0 + n].rearrange("(j p) t -> p j t", p=128))
                c.dma('sp', k_st[:, :, 0:n], sfm[R_K:R_K + 1024, s0:s0 + n].rearrange("(j p) t -> p j t", p=128))
                if sample:
                    c.dma('sp', mprev_s[:, :], IN("st_mlm")[l])
                for (t0, P) in [t for t in TILES if s0 <= t[0] < s0 + n]:
                    lo = t0 - s0
                    mbig = cmask['maskbig_s'] if sample else cmask['maskbig_p']
                    ir, fr, e1, lnp, bcn, a_, cm, mx, wi, ml_, enm, wen = [g4.next() for _ in range(12)]
                    c.dma('sp', ir[:, 0:P], g_scr[16:20, t0:t0 + P])
                    c.dma('sp', fr[:, 0:P], g_scr[20:24, t0:t0 + P])
                    c.act(e1[:, 0:P], fr[:, 0:P], AF.Exp, bias=nbf, scale=-1.0)
                    c.act(lnp[:, 0:P], e1[:, 0:P], AF.Ln, bias=1.0)
                    if not sample:
                        c.scan(bcn[:, 0:P], ones_f[0:4, 0:P], lnp[:, 0:P], 0.0, ALU.mult, ALU.add)
                        c.stt(a_[:, 0:P], ir[:, 0:P], bi, bcn[:, 0:P], ALU.add, ALU.add)
                        c.scan(cm[:, 0:P], a_[:, 0:P], a_[:, 0:P], -1e30, ALU.max, ALU.max)
                        c.ts(mx[:, 0:P], cm[:, 0:P], mprev[:, 0:1], None, op0=ALU.max)
                        c.act(wi[:, 0:P], mx[:, 0:P], AF.Exp, bias=mprev[:, 0:1], scale=-1.0)
                        c.tt(ml_[:, 0:P], mx[:, 0:P], bcn[:, 0:P], ALU.subtract)
                        c.act(enm[:, 0:P], ml_[:, 0:P], AF.Exp, scale=-1.0)
                        nml = g4.next()
                        c.ts(nml[:, 0:1], mx[:, P - 1:P], -1.0, None, op0=ALU.mult)
                        c.act(wen[:, 0:P], a_[:, 0:P], AF.Exp, bias=nml[:, 0:1])
                    else:
                        v3 = lambda tl: tl[:, 0:64].rearrange("p (b t) -> p b t", t=4)
                        c.copy(v3(bcn)[:, :, 0:1], v3(lnp)[:, :, 0:1])
                        for t in (1, 2, 3):
                            c.tt(v3(bcn)[:, :, t:t + 1], v3(bcn)[:, :, t - 1:t], v3(lnp)[:, :, t:t + 1], ALU.add)
                        c.stt(a_[:, 0:P], ir[:, 0:P], bi, bcn[:, 0:P], ALU.add, ALU.add)
                        c.copy(v3(cm)[:, :, 0:1], v3(a_)[:, :, 0:1])
                        for t in (1, 2, 3):
                            c.tt(v3(cm)[:, :, t:t + 1], v3(cm)[:, :, t - 1:t], v3(a_)[:, :, t:t + 1], ALU.max)
                        mpb = mprev_s[:, :].unsqueeze(2).to_broadcast([4, 16, 4])
                        c.tt(v3(mx), v3(cm), mpb, ALU.max)
                        c.tt(v3(wi), mpb, v3(mx), ALU.subtract)
                        c.act(wi[:, 0:P], wi[:, 0:P], AF.Exp)
                        c.tt(ml_[:, 0:P], mx[:, 0:P], bcn[:, 0:P], ALU.subtract)
                        c.act(enm[:, 0:P], ml_[:, 0:P], AF.Exp, scale=-1.0)
                        c.tt(v3(wen), v3(a_), v3(mx)[:, :, 3:4].to_broadcast([4, 16, 4]), ALU.subtract)
                        c.act(wen[:, 0:P], wen[:, 0:P], AF.Exp)
                    for qi, qsrc in enumerate((a_, wi, enm, wen)):
                        c.tr(pT[0:P, 32 * qi:32 * qi + 4], qsrc[:, 0:P], ident[0:4, 0:4])
                    gtm = gtm_r.next()
                    c.copy(gtm[0:P, :].rearrange("p (q x) -> p q x", x=32)[:, :, 0:4], pT[0:P, 0:128].rearrange("p (q x) -> p q x", x=32)[:, :, 0:4])
                    vaug = vaug_r.next()
                    c.dma('sp', vaug[0:P, :, 0:256], stm[t0:t0 + P, C_V:C_V + 1024].rearrange("t (h e) -> t h e", e=256))
                    ktm = b2.next()
                    c.dma('sp', ktm[0:P, :], stm[t0:t0 + P, C_KT:C_KT + 1024])
                    otm = b2.next()
                    c.dma('sp', otm[0:P, :], stm[t0:t0 + P, C_O:C_O + 1024])
                    ztm = b2.next()
                    c.dma('sp', ztm[0:P, :], stm[t0:t0 + P, C_ZML:C_ZML + 1024])
                    yc = b2.next()
                    if sample:
                        wv = v3(wi)[:, :, 3:4]
                        dgv = dg[:, :].rearrange("p (b h) -> p b h", h=4)
                        c.tt(dgv, wv.to_broadcast([4, 16, 4]), ident[0:4, 0:4].unsqueeze(1).to_broadcast([4, 16, 4]), ALU.mult)
                        c.mm(pBC2[:, 0:64], ones_f[0:4, :], dg[:, :])
                        c.copy(dec_sb[:, :], pBC2[:, 0:64])
                    for hh in range(4):
                        for j in range(2):
                            c.mm(pQK[0:P, 0:P], k_st[:, 2 * hh + j, lo:lo + P], q_st[:, 2 * hh + j, lo:lo + P], start=(j == 0), stop=(j == 1))
                        selm = g4.next()
                        c.ts(selm[:, 0:P], mx[:, 0:P], ident[0:4, hh:hh + 1], None, op0=ALU.mult)
                        c.mm(pBC[0:P, 0:P], ones_f[0:4, 0:P], selm[:, 0:P])
                        tsb = fq.next()
                        c.stt(tsb[0:P, 0:P], pBC[0:P, 0:P], gtm[0:P, hh:hh + 1], mbig[0:P, 0:P], ALU.subtract, ALU.max)
                        E = fq.next()
                        c.act(E[0:P, 0:P], tsb[0:P, 0:P], AF.Exp, scale=-1.0)
                        WT = bq.next()
                        c.tt(WT[0:P, 0:P], E[0:P, 0:P], pQK[0:P, 0:P], ALU.mult)
                        selw = g4.next()
                        c.ts(selw[:, 0:P], wi[:, 0:P], ident[0:4, hh:hh + 1], None, op0=ALU.mult)
                        c.mm(pBC2[:, 0:P], ones_f[0:4, :], selw[:, 0:P])
                        qs = qs_r.next()
                        for j in range(2):
                            c.tt(qs[:, j, 0:P], q_st[:, 2 * hh + j, lo:lo + P], pBC2[:, 0:P], ALU.mult)
                        pn = pn_r.next()
                        c.mm(pn[0:P, 0:257], WT[0:P, 0:P], vaug[0:P, hh, :], start=True, stop=False)
                        if not sample:
                            for j in range(2):
                                c.mm(pn[0:P, 0:257], qs[:, j, 0:P], Cbf[:, hh, j, :], start=False, stop=(j == 1))
                        else:
                            kw = kw_r.next()
                            for b in range(NB):
                                Cb = Cb_r.next()
                                Cbb = Cbb_r.next()
                                src_c = IN("st_mlc")[l, b, hh * 256:(hh + 1) * 256, :].rearrange("(j p) e -> p j e", p=128)
                                src_n = IN("st_mln")[l, b, hh * 256:(hh + 1) * 256].rearrange("(j p o) -> p j o", p=128, o=1)
                                c.dma('sp', Cb[:, :, 0:256], src_c)
                                c.dma('sp', Cb[:, :, 256:257], src_n, allow_slow_non_contiguous=True)
                                c.copy(Cbb[:, :, :], Cb[:, :, :], eng='act')
                                qsb = qsb_r.next()
                                c.tt(qsb[:, :, :], qs[:, :, 0:64], colmask[:, b * 64:(b + 1) * 64].unsqueeze(1).to_broadcast([128, 2, 64]), ALU.mult)
                                for j in range(2):
                                    c.mm(pn[0:P, 0:257], qsb[:, j, :], Cbb[:, j, :], start=False, stop=(b == NB - 1 and j == 1))
                                c.ts(kw[0:64, :], ktm[0:64, hh * 256:(hh + 1) * 256], gtm[0:64, 96 + hh:97 + hh], cmask['rowmask'][0:64, b:b + 1], op0=ALU.mult, op1=ALU.mult)
                                for j, pc in enumerate((pC0, pC1)):
                                    c.mm(pc[:, 0:257], kw[0:64, j * 128:(j + 1) * 128], vaug[0:64, hh, :])
                                for j, pc in enumerate((pC0, pC1)):
                                    c.stt(Cb[:, j, :], Cb[:, j, :], dec_sb[:, b * 4 + hh:b * 4 + hh + 1], pc[:, 0:257], ALU.mult, ALU.add)
                                c.dma('sp', mc_so[l, b, hh * 256:(hh + 1) * 256, :].rearrange("(j p) e -> p j e", p=128), Cb[:, :, 0:256])
                                c.dma('sp', mn_so[l, b, hh * 256:(hh + 1) * 256].rearrange("(j p o) -> p j o", p=128, o=1), Cb[:, :, 256:257], allow_slow_non_contiguous=True)
                        st = st_small.next()
                        c.act(st[0:P, 2:3], pn[0:P, 256:257], AF.Abs)
                        c.ts(st[0:P, 0:1], st[0:P, 2:3], gtm[0:P, 64 + hh:65 + hh], None, op0=ALU.max)
                        c.recip(st[0:P, 1:2], st[0:P, 0:1])
                        hc = hc_r.next()
                        c.stt(hc[0:P, :], pn[0:P, 0:256], st[0:P, 1:2], otm[0:P, hh * 256:(hh + 1) * 256], ALU.mult, ALU.mult)
                        rstd = rms_rstd(hc[0:P, :], P, n=256)
                        c.stt(hc[0:P, :], hc[0:P, :], rstd, gml[0:P, hh * 256:(hh + 1) * 256], ALU.mult, ALU.mult)
                        c.tt(yc[0:P, hh * 256:(hh + 1) * 256], hc[0:P, :], ztm[0:P, hh * 256:(hh + 1) * 256], ALU.mult)
                    tm_to_fm(yc, P, yk_fm[:, :, t0:t0 + P])
                    if not sample:
                        c.ts(dg[:, 0:4], ident[0:4, 0:4], wi[:, P - 1:P], None, op0=ALU.mult)
                        c.mm(pBC2[:, 0:4], ones_f[0:4, :], dg[:, 0:4])
                        c.copy(dec_sb[:, 0:4], pBC2[:, 0:4])
                        for hh in range(4):
                            kw = kw_r.next()
                            c.ts(kw[0:P, :], ktm[0:P, hh * 256:(hh + 1) * 256], gtm[0:P, 96 + hh:97 + hh], None, op0=ALU.mult)
                            for j, pc in enumerate((pC0, pC1)):
                                c.mm(pc[:, 0:257], kw[0:P, j * 128:(j + 1) * 128], vaug[0:P, hh, :])
                            for j, pc in enumerate((pC0, pC1)):
                                c.stt(Cst[:, hh, j, :], Cst[:, hh, j, :], dec_sb[:, hh:hh + 1], pc[:, 0:257], ALU.mult, ALU.add)
                        c.copy(Cbf[:, :, :, :].rearrange("p a b c -> p (a b c)"), Cst[:, :, :, :].rearrange("p a b c -> p (a b c)"), eng='act')
                        c.copy(mprev[:, 0:1], ml_[:, P - 1:P])
                    else:
                        mlast = g4.next()
                        c.copy(mlast[:, 0:16].unsqueeze(2), v3(ml_)[:, :, 3:4])
                        c.dma('sp', mm_so[l], mlast[:, 0:16])
            for hh in range(4):
                c.dma('sp', mc_po[l, hh * 256:(hh + 1) * 256, :].rearrange("(j p) e -> p j e", p=128), Cst[:, hh, :, 0:256])
                c.dma('sp', mn_po[l, hh * 256:(hh + 1) * 256].rearrange("(j p o) -> p j o", p=128, o=1), Cst[:, hh, :, 256:257], allow_slow_non_contiguous=True)
            c.dma('sp', mm_po[l].rearrange("(h o) -> h o", o=1), mprev[:, 0:1])
            dump_fm("y_c", yk_fm)
            c.barrier()
            c.release(anames)
            bes.close()

        def xattn_branch(l):
            bes = contextlib.ExitStack()
            uid = "_d%d" % l
            anames = []

            def A(name, shape, dt=F32):
                anames.append(name + uid)
                return bes.enter_context(nc.sbuf_tensor(name + uid, list(shape), dt))

            mk_fm = A("mk_fm", [128, 8, 256], BF16)
            mv_tm = A("mv_tm", [128, 2, D], BF16)
            memkv(l, mk_fm, mv_tm)
            qx_st = A("qx_st", [128, 8, 512], BF16)
            p_r = Rot([A("pp%d" % i, [128, 256], BF16) for i in range(2)])
            pT_r = Rot([A("ppT%d" % i, [128, 2, 128], BF16) for i in range(5)])
            kl_r = Rot([A("kl%d" % i, [128, 2, D]) for i in range(2)])
            mkT_r = Rot([A("mkT%d" % i, [128, 8, 256], BF16) for i in range(2)])
            mvb_r = Rot([A("mvb%d" % i, [128, 2, D], BF16) for i in range(2)])
            qm_r = Rot([A("qm%d" % i, [128, 8, 64], BF16) for i in range(2)])
            pTm_r = Rot([A("pTm%d" % i, [128, 2, 64], BF16) for i in range(3)])
            pS = banks[0:4]
            pTr = banks[4:6]
            pV = Rot(banks[6:8])

            def softmax_T(sc, P):
                st = st_small.next()
                c.reduce(st[0:P, 0:1], sc[0:P, 0:256], ALU.max)
                c.ts(st[0:P, 1:2], st[0:P, 0:1], -1.0, None, op0=ALU.mult)
                p = p_r.next()
                c.act(p[0:P, :], sc[0:P, 0:256], AF.Exp, bias=st[0:P, 1:2], accum_out=st[0:P, 2:3])
                c.recip(st[0:P, 3:4], st[0:P, 2:3])
                pv = bfv(pTr[0])
                for mt in range(2):
                    c.tr(pv[:, mt * 128:mt * 128 + P], p[0:P, mt * 128:(mt + 1) * 128], ident_b[0:P, 0:P])
                pT = pT_r.next()
                c.copy(pT[:, :, 0:P], pv[:, 0:256].rearrange("p (m t) -> p m t", t=128)[:, :, 0:P], eng='act')
                return pT, st[0:P, 3:4]

            for si, (s0, n) in enumerate(STS):
                sample = (si == 4)
                c.dma('sp', qx_st[:, :, 0:n], sfm[R_QXA:R_QXA + 1024, s0:s0 + n].rearrange("(j p) t -> p j t", p=128))
                for (t0, P) in [t for t in TILES if s0 <= t[0] < s0 + n]:
                    lo = t0 - s0
                    ztm = b2.next()
                    c.dma('sp', ztm[0:P, :], stm[t0:t0 + P, C_ZXA:C_ZXA + 1024])
                    yd = b2.next()
                    if not sample:
                        for hh in range(4):
                            sc = pS[hh % 2]
                            for j in range(2):
                                c.mm(sc[0:P, 0:256], qx_st[:, 2 * hh + j, lo:lo + P], mk_fm[:, 2 * hh + j, :], start=(j == 0), stop=(j == 1))
                            pT, rinv = softmax_T(sc, P)
                            pvb = pV.next()
                            for mt in range(2):
                                c.mm(pvb[0:P, 0:256], pT[:, mt, 0:P], mv_tm[:, mt, hh * 256:(hh + 1) * 256], start=(mt == 0), stop=(mt == 1))
                            c.stt(yd[0:P, hh * 256:(hh + 1) * 256], pvb[0:P, 0:256], rinv, ztm[0:P, hh * 256:(hh + 1) * 256], ALU.mult, ALU.mult)
                    else:
                        for b in range(NB):
                            kl = kl_r.next()
                            c.dma('sp', kl[:, :, :], IN("ck")[l, b].rearrange("(mt p) d -> p mt d", p=128))
                            mkT = mkT_r.next()
                            for rnd in range(2):
                                for q4 in range(8):
                                    blk = rnd * 4 + q4 // 2
                                    mt = q4 % 2
                                    c.tr(pTr[q4 // 4][:, (q4 % 4) * 128:(q4 % 4 + 1) * 128], kl[:, mt, blk * 128:(blk + 1) * 128], ident[:, :])
                                for hf in range(2):
                                    blks = mkT[:, rnd * 4 + 2 * hf:rnd * 4 + 2 * hf + 2, :]
                                    src = pTr[hf][:, :].rearrange("p (b m) -> p b m", m=256)
                                    if hf == 0:
                                        c.act(blks, src, AF.Copy, scale=1.0 / 16)
                                    else:
                                        c.ts(blks, src, 1.0 / 16, None, op0=ALU.mult)
                            qm = qm_r.next()
                            c.tt(qm[:, :, :], qx_st[:, :, 0:64], colmask[:, b * 64:(b + 1) * 64].unsqueeze(1).to_broadcast([128, 8, 64]), ALU.mult)
                            for hh in range(4):
                                for j in range(2):
                                    c.mm(pS[hh][0:64, 0:256], qm[:, 2 * hh + j, :], mkT[:, 2 * hh + j, :],
                                         start=(b == 0 and j == 0), stop=(b == NB - 1 and j == 1))
                        pTs, rinvs = [], []
                        for hh in range(4):
                            pT, rinv = softmax_T(pS[hh], 64)
                            pTs.append(pT)
                            rinvs.append(rinv)
                        for b in range(NB):
                            mvb = mvb_r.next()
                            c.dma('pool', mvb[:, :, :], IN("cv")[l, b].rearrange("(mt p) d -> p mt d", p=128))
                            for hh in range(4):
                                pTm = pTm_r.next()
                                c.tt(pTm[:, :, :], pTs[hh][:, :, 0:64], colmask[:, b * 64:(b + 1) * 64].unsqueeze(1).to_broadcast([128, 2, 64]), ALU.mult)
                                for mt in range(2):
                                    c.mm(pS[hh][0:64, 0:256], pTm[:, mt, :], mvb[:, mt, hh * 256:(hh + 1) * 256],
                                         start=(b == 0 and mt == 0), stop=(b == NB - 1 and mt == 1))
                        for hh in range(4):
                            c.stt(yd[0:64, hh * 256:(hh + 1) * 256], pS[hh][0:64, 0:256], rinvs[hh], ztm[0:64, hh * 256:(hh + 1) * 256], ALU.mult, ALU.mult)
                    tm_to_fm(yd, P, yk_fm[:, :, t0:t0 + P])
            dump_fm("y_d", yk_fm)
            c.barrier()
            c.release(anames)
            bes.close()

        merged = h_fm

        def merge_branch(l, k, first):
            for c0 in (0, 512):
                wb = load_w(IN("w_down")[l, k, :, c0:c0 + 512])
                for cb in range(4):
                    jb = c0 // 128 + cb
                    for (t0, n) in STS:
                        pb = pbank.next()
                        for kt in range(8):
                            c.mm(pb[:, 0:n], wb[:, kt, cb * 128:(cb + 1) * 128], yk_fm[:, kt, t0:t0 + n], start=(kt == 0), stop=(kt == 7))
                        gt = stg_b.next()
                        r0 = R_GATE + k * 1024 + jb * 128
                        c.dma('sp', gt[:, 0:n], sfm[r0:r0 + 128, t0:t0 + n])
                        if first:
                            c.tt(merged[:, jb, t0:t0 + n], gt[:, 0:n], pb[:, 0:n], ALU.mult)
                        else:
                            tmp = stg_f.next()
                            c.tt(tmp[:, 0:n], gt[:, 0:n], pb[:, 0:n], ALU.mult)
                            c.tt(merged[:, jb, t0:t0 + n], merged[:, jb, t0:t0 + n], tmp[:, 0:n], ALU.add, eng='pool')

        def outproj(l):
            wos = [load_w(IN("w_out")[l, :, c0:c0 + 512]) for c0 in (0, 512)]
            for tt, (t0, P) in enumerate(TILES):
                xt = f4.next()
                c.dma('sp', xt[0:P, :], x_src(l, tt))
                for half in range(2):
                    pb = pbank.next()
                    for kt in range(8):
                        c.mm(pb[0:P, :], merged[:, kt, t0:t0 + P], wos[half][:, kt, :], start=(kt == 0), stop=(kt == 7))
                    c.tt(xt[0:P, half * 512:(half + 1) * 512], xt[0:P, half * 512:(half + 1) * 512], pb[0:P, :], ALU.add)
                c.dma('sp', x_scr[t0:t0 + P, :], xt[0:P, :])


        for l in range(depth):
            phase0(l)
            projections(l)
            if stage < 5 and l == 0 and 'memkv' not in SKIP:
                memkv(l, c.sb("mk_fm_t", [128, 8, 256], BF16), c.sb("mv_tm_t", [128, 2, D], BF16))
            full = stage >= 6
            first = True
            for (stg, k, fn) in ((2, 0, ssd_branch), (3, 1, s5_branch), (4, 2, mlstm_branch), (5, 3, xattn_branch)):
                if stage >= stg and ("br%d" % k) not in SKIP:
                    fn(l)
                    if full:
                        merge_branch(l, k, first)
                        first = False
            if full:
                dump_fm("merged", merged)
                outproj(l)

        gfin = c.sb("gfin", [128, D], F32)
        c.dma('sp', gfin[:], IN("final_norm").partition_broadcast(128))
        for tt, (t0, P) in enumerate(TILES):
            xt = f4.next()
            c.dma('sp', xt[0:P, :], x_scr[t0:t0 + P, :] if stage >= 6 else x_src(0, tt))
            rstd = rms_rstd(xt[0:P, :], P)
            c.stt(xt[0:P, :], xt[0:P, :], rstd, gfin[0:P, :], ALU.mult, ALU.mult)
            if tt < 16:
                c.dma('sp', y_p[t0:t0 + P, :], xt[0:P, :])
            else:
                c.dma('sp', y_s[:, :], xt[0:P, :])
        c.finish()
        print("[build] instructions=%d waits=%d per-engine=%s dma_sems=%d sbuf_left=%s" % (
            c.n_inst, c.n_wait, c.cnt, len(c.dma_sems), nc.sbuf_bytes_remaining() if callable(nc.sbuf_bytes_remaining) else nc.sbuf_bytes_remaining))
    return nc


def make_in_maps(inp, used=None):
    cs = make_consts()
    f32 = lambda a: np.ascontiguousarray(a, np.float32)
    shared = {
        "w_kv": lambda: f32(inp["w_mem_kv"]),
        "w_down": lambda: f32(inp["w_down"]),
        "w_out": lambda: f32(inp["w_out"]),
        "norm_in_c": lambda: colmaj(inp["norm_in"]),
        "mem_norm_c": lambda: colmaj(inp["mem_norm"]),
        "final_norm": lambda: f32(inp["final_norm"]).reshape(1, D),
        "b_gate_c": lambda: colmaj(inp["b_gate"]),
        "c_sel16": lambda: cs['sel16'],
        "c_colmask": lambda: cs['colmask'],
        "conv_w_c": lambda: f32(np.transpose(f32(inp["ssd_conv_w"]).reshape(DEPTH, 4, 12, 128), (0, 3, 2, 1)).reshape(DEPTH, 128, 48)),
        "conv_b_c": lambda: colmaj(inp["ssd_conv_b"]),
        "ssd_hp": lambda: f32(np.stack([f32(inp["ssd_dt_bias"]), f32(inp["ssd_a_log"])], axis=-1)),
        "ssd_d": lambda: f32(inp["ssd_d"]),
        "ssd_norm": lambda: f32(inp["ssd_norm"]),
        "ml_hp": lambda: f32(np.stack([f32(inp["b_igate"]), f32(inp["b_fgate"])], axis=-1)),
        "ml_norm": lambda: f32(inp["ml_norm"]).reshape(DEPTH, D),
        "s5_a_re_t": lambda: f32(np.transpose(f32(inp["s5_a_re"]), (0, 2, 1))),
        "s5_a_im_t": lambda: f32(np.transpose(f32(inp["s5_a_im"]), (0, 2, 1))),
        "s5_log_dt": lambda: f32(inp["s5_log_dt"]),
        "s5_b_re_t": lambda: f32(np.transpose(f32(inp["s5_b_re"]), (0, 2, 1, 3))).reshape(DEPTH, 64, 1024),
        "s5_b_im_t": lambda: f32(np.transpose(f32(inp["s5_b_im"]), (0, 2, 1, 3))).reshape(DEPTH, 64, 1024),
        "s5_c_re_t": lambda: f32(np.transpose(f32(inp["s5_c_re"]), (0, 3, 1, 2))).reshape(DEPTH, 64, 1024),
        "s5_c_im_t": lambda: f32(np.transpose(f32(inp["s5_c_im"]), (0, 3, 1, 2))).reshape(DEPTH, 64, 1024),
        "s5_d_c": lambda: colmaj(inp["s5_d"]),
        "s5_glu_w": lambda: f32(inp["s5_glu_w"]),
        "s5_glu_b_c": lambda: colmaj(inp["s5_glu_b"]),
    }
    for k in CONST_ORDER:
        shared["c_" + k] = (lambda k=k: cs[k])
    for i in range(DEPTH):
        shared["w_in%d" % i] = (lambda i=i: f32(inp["w_in"][i]))
    percore = {
        "x_p": lambda ci, b0: f32(inp["x_prompt"][ci]),
        "x_s": lambda ci, b0: f32(inp["x_sample"][b0:b0 + NB]).reshape(TS, D),
        "mem": lambda ci, b0: f32(inp["mem_prompt"][ci]),
        "ck": lambda ci, b0: f32(inp["cache_mem_k"][:, b0:b0 + NB]).reshape(DEPTH, NB, 256, D),
        "cv": lambda ci, b0: f32(inp["cache_mem_v"][:, b0:b0 + NB]).reshape(DEPTH, NB, 256, D),
        "st_conv": lambda ci, b0: f32(inp["state_ssd_conv"][:, b0:b0 + NB]).reshape(DEPTH, NB * 3, 1536),
        "st_ssd": lambda ci, b0: f32(inp["state_ssd"][:, b0:b0 + NB]).reshape(DEPTH, NB, 1024, 128),
        "st_mlc": lambda ci, b0: f32(inp["state_mlstm_c"][:, b0:b0 + NB]).reshape(DEPTH, NB, 1024, 256),
        "st_mln": lambda ci, b0: f32(inp["state_mlstm_n"][:, b0:b0 + NB]).reshape(DEPTH, NB, 1024),
        "st_mlm": lambda ci, b0: f32(np.transpose(f32(inp["state_mlstm_m"][:, b0:b0 + NB]), (0, 2, 1))),
        "st_s5_t": lambda ci, b0: f32(np.transpose(np.stack([f32(inp["state_s5_re"][:, b0:b0 + NB]), f32(inp["state_s5_im"][:, b0:b0 + NB])], axis=1), (0, 4, 1, 3, 2))),
    }
    names = set(shared) | set(percore)
    if used is not None:
        names &= set(used)
    sh = {k: shared[k]() for k in names if k in shared}
    maps = []
    for ci in range(NCORES):
        m = dict(sh)
        for k in names:
            if k in percore:
                m[k] = percore[k](ci, ci * NB)
        maps.append(m)
    return maps


_NC_CACHE = {}


def run(inp, depth=DEPTH, stage=99, dbg=False):
    key = (depth, stage, dbg)
    if key not in _NC_CACHE:
        _NC_CACHE[key] = build(depth, stage, dbg)
    nc = _NC_CACHE[key]
    maps = make_in_maps(inp, used=set(nc._used_inputs.keys()))
    res = run_bass_kernel_spmd(nc, maps, core_ids=list(range(NCORES)))
    return res.results


def kernel(**inputs):
    r = run(inputs)
    L = DEPTH
    pst = lambda k, shp: np.stack([np.asarray(x[k], np.float32).reshape((L,) + shp) for x in r], axis=1)
    sst = lambda k, shp: np.concatenate([np.asarray(x[k], np.float32).reshape((L, NB) + shp) for x in r], axis=1)
    y_prompt = np.stack([x["y_p"] for x in r], 0)
    y_sample = np.concatenate([x["y_s"].reshape(NB, 4, D) for x in r], 0)
    mm_s = np.concatenate([np.transpose(np.asarray(x["mm_so"], np.float32), (0, 2, 1)) for x in r], axis=1)
    return (y_prompt, y_sample,
            pst("mk_o", (256, 4, 256)), pst("mv_o", (256, 4, 256)),
            pst("conv_po", (3, 1536)), pst("ssd_po", (16, 64, 128)),
            pst("s5re_po", (64, 64)), pst("s5im_po", (64, 64)),
            pst("mc_po", (4, 256, 256)), pst("mn_po", (4, 256)), pst("mm_po", (4,)),
            sst("conv_so", (3, 1536)), sst("ssd_so", (16, 64, 128)),
            sst("s5re_so", (64, 64)), sst("s5im_so", (64, 64)),
            sst("mc_so", (4, 256, 256)), sst("mn_so", (4, 256)), mm_s)
```
